# Optimizing a Trainium2 kernel written in Bass

```python
import math
import jax, jax.numpy as jnp
from jax import lax
import numpy as np

D_MODEL = 1024
BATCH = 32
SEQ = 256
DEPTH = 2
DEC_BATCH = 2
DEC_SEQ = 4096
PAST_LEN = 512

GRID_W = 64
POS_BASE = 10000.0
EPS = 1e-6
D_FF = 4 * D_MODEL
MIX_WIDTH = D_MODEL
N_MOD = 6

S5_CH = 3 * MIX_WIDTH // 4
S5_GROUP = 16
S5_GROUPS = S5_CH // S5_GROUP
S5_STATE = 64
DT_MIN = 1e-3
DT_MAX = 1e-1
FNET_CH = MIX_WIDTH - S5_CH
FNET_GROUPS = 4
FNET_GW = FNET_CH // FNET_GROUPS

POOL_WINDOWS = (2, 4, 8, 16)
POOL_GROUPS = len(POOL_WINDOWS)
POOL_CH = MIX_WIDTH // 2
POOL_GW = POOL_CH // POOL_GROUPS
GMLP_CH = MIX_WIDTH - POOL_CH
GMLP_HEADS = 4
GMLP_HD = GMLP_CH // GMLP_HEADS
GMLP_CHUNK = 128

kernel_name = "hybrid_s5_fnet_pool_gmlp_diffusion_step"

F32 = jnp.float32


def rmsnorm(x, g):
    xf = x.astype(F32)
    y = xf * lax.rsqrt(jnp.mean(xf * xf, axis=-1, keepdims=True) + EPS)
    return (y * g.astype(F32)).astype(x.dtype)


def group_layernorm(v, g, b, groups):
    bt, l, cn = v.shape
    vf = v.astype(F32).reshape(bt, l, groups, cn // groups)
    mu = jnp.mean(vf, axis=-1, keepdims=True)
    var = jnp.mean(jnp.square(vf - mu), axis=-1, keepdims=True)
    y = ((vf - mu) * lax.rsqrt(var + EPS)).reshape(bt, l, cn)
    return (y * g.astype(F32) + b.astype(F32)).astype(v.dtype)


def grid_pos_embed(length, dtype):
    rows = length // GRID_W
    rr, cc = jnp.meshgrid(jnp.arange(rows, dtype=F32), jnp.arange(GRID_W, dtype=F32), indexing="ij")
    quarter = D_MODEL // 4
    omega = 1.0 / (POS_BASE ** (jnp.arange(quarter, dtype=F32) / quarter))

    def axis_embed(p):
        ang = p.reshape(-1)[:, None] * omega[None, :]
        return jnp.concatenate([jnp.sin(ang), jnp.cos(ang)], axis=-1)

    return jnp.concatenate([axis_embed(rr), axis_embed(cc)], axis=-1).astype(dtype)


def modulation(cond, w_mod, b_mod):
    m = jax.nn.silu(cond) @ w_mod + b_mod
    return jnp.split(m[:, None, :], N_MOD, axis=-1)


def sq_relu_mlp(h, w1, w2):
    return jnp.square(jax.nn.relu(h @ w1)) @ w2


def s5_discretize(lam_re, lam_im, log_dt, b_re, b_im):
    dt = jnp.exp(log_dt)[:, None]
    mag = jnp.exp(lam_re * dt)
    ab_re = mag * jnp.cos(lam_im * dt)
    ab_im = mag * jnp.sin(lam_im * dt)
    num_re = ab_re - 1.0
    num_im = ab_im
    den = lam_re * lam_re + lam_im * lam_im
    coef_re = (num_re * lam_re + num_im * lam_im) / den
    coef_im = (num_im * lam_re - num_re * lam_im) / den
    bb_re = coef_re[..., None] * b_re - coef_im[..., None] * b_im
    bb_im = coef_re[..., None] * b_im + coef_im[..., None] * b_re
    return ab_re, ab_im, bb_re, bb_im


def _complex_linear_combine(e1, e2):
    a1r, a1i, b1r, b1i = e1
    a2r, a2i, b2r, b2i = e2
    return (a2r * a1r - a2i * a1i,
            a2r * a1i + a2i * a1r,
            a2r * b1r - a2i * b1i + b2r,
            a2r * b1i + a2i * b1r + b2i)


def s5_direction(u, h0_re, h0_im, lam_re, lam_im, log_dt, b_re, b_im, c_re, c_im):
    ab_re, ab_im, bb_re, bb_im = s5_discretize(lam_re, lam_im, log_dt, b_re, b_im)
    bu_re = jnp.einsum("blgh,gph->blgp", u, bb_re)
    bu_im = jnp.einsum("blgh,gph->blgp", u, bb_im)
    bu_re = bu_re.at[:, 0].add(ab_re * h0_re - ab_im * h0_im)
    bu_im = bu_im.at[:, 0].add(ab_re * h0_im + ab_im * h0_re)
    a_re = jnp.broadcast_to(ab_re, bu_re.shape)
    a_im = jnp.broadcast_to(ab_im, bu_im.shape)
    _, _, h_re, h_im = lax.associative_scan(_complex_linear_combine, (a_re, a_im, bu_re, bu_im), axis=1)
    y = jnp.einsum("blgp,ghp->blgh", h_re, c_re) - jnp.einsum("blgp,ghp->blgh", h_im, c_im)
    return y, h_re[:, -1], h_im[:, -1]


def s5_mixer(u, h0_re, h0_im, lam_re, lam_im, log_dt, b_re, b_im, c_re, c_im, d, w_glu, b_glu):
    bt, l, _ = u.shape
    lam_re, lam_im, log_dt, b_re, b_im, c_re, c_im, d = (
        a.astype(F32) for a in (lam_re, lam_im, log_dt, b_re, b_im, c_re, c_im, d))
    h0_re = h0_re.astype(F32)
    h0_im = h0_im.astype(F32)
    uf = u.astype(F32).reshape(bt, l, S5_GROUPS, S5_GROUP)
    y_f, hf_re, hf_im = s5_direction(uf, h0_re[:, 0], h0_im[:, 0], lam_re[0], lam_im[0], log_dt[0],
                                     b_re[0], b_im[0], c_re[0], c_im[0])
    y_b, hb_re, hb_im = s5_direction(jnp.flip(uf, 1), h0_re[:, 1], h0_im[:, 1], lam_re[1], lam_im[1], log_dt[1],
                                     b_re[1], b_im[1], c_re[1], c_im[1])
    y = (y_f + jnp.flip(y_b, 1)).reshape(bt, l, S5_CH) + d * uf.reshape(bt, l, S5_CH)
    g = jax.nn.gelu(y)
    out = g * jax.nn.sigmoid(g @ w_glu.astype(F32) + b_glu.astype(F32))
    s_re = jnp.stack([hf_re, hb_re], axis=1)
    s_im = jnp.stack([hf_im, hb_im], axis=1)
    return out.astype(u.dtype), s_re, s_im


def fnet_mixer(u, w, b):
    bt, l, _ = u.shape
    uf = u.astype(F32).reshape(bt, l, FNET_GROUPS, FNET_GW)
    z = jnp.fft.fft2(uf, axes=(1, 3), norm="ortho").real.astype(u.dtype)
    out = jnp.einsum("blgc,gcd->blgd", z, w) + b
    return out.reshape(bt, l, FNET_CH)


def pool_mixer(u, w, scale):
    bt, l, _ = u.shape
    uf = u.astype(F32).reshape(bt, l, POOL_GROUPS, POOL_GW)
    cs = jnp.concatenate([jnp.zeros((bt, 1, POOL_GROUPS, POOL_GW), F32), jnp.cumsum(uf, axis=1)], axis=1)
    t = jnp.arange(l)
    pooled = []
    for gi, win in enumerate(POOL_WINDOWS):
        lo = jnp.clip(t - win // 2, 0, l)
        hi = jnp.clip(t + win // 2, 0, l)
        s = cs[:, hi, gi] - cs[:, lo, gi]
        pooled.append(s / (hi - lo).astype(F32)[None, :, None])
    p = (jnp.stack(pooled, axis=2) - uf).astype(u.dtype)
    mixed = jnp.einsum("blgc,gcd->blgd", p, w).reshape(bt, l, POOL_CH)
    return mixed * scale


def gmlp_mixer(uv, ln_g, ln_b, ws, bs):
    bt, l, _ = uv.shape
    z = jax.nn.gelu(uv)
    u, v = jnp.split(z, 2, axis=-1)
    v = group_layernorm(v, ln_g, ln_b, GMLP_HEADS)
    n_chunks = l // GMLP_CHUNK
    vh = v.reshape(bt, n_chunks, GMLP_CHUNK, GMLP_HEADS, GMLP_HD)
    sv = jnp.einsum("bnkhd,hqk->bnqhd", vh, ws) + bs.T[:, :, None]
    return u * sv.reshape(bt, l, GMLP_CH)


def even_mixer(h, h0_re, h0_im, w_in, w_out, lam_re, lam_im, log_dt, b_re, b_im, c_re, c_im, d,
               w_glu, b_glu, fnet_w, fnet_b):
    z = h @ w_in
    ya, s_re, s_im = s5_mixer(z[..., :S5_CH], h0_re, h0_im, lam_re, lam_im, log_dt, b_re, b_im,
                              c_re, c_im, d, w_glu, b_glu)
    yb = fnet_mixer(z[..., S5_CH:], fnet_w, fnet_b)
    return jnp.concatenate([ya, yb], axis=-1) @ w_out, (s_re, s_im)


def odd_mixer(h, w_in, w_out, pool_w, pool_scale, ln_g, ln_b, ws, bs):
    z = h @ w_in
    yc = pool_mixer(z[..., :POOL_CH], pool_w, pool_scale)
    yd = gmlp_mixer(z[..., POOL_CH:], ln_g, ln_b, ws, bs)
    return jnp.concatenate([yc, yd], axis=-1) @ w_out, ()


def run_layer(x, cond, mixer, w_mod, b_mod, g_mix_pre, g_mix_post, g_ff_pre, g_ff_post, w_ff1, w_ff2):
    sh1, sc1, gt1, sh2, sc2, gt2 = modulation(cond, w_mod, b_mod)
    h = rmsnorm(x, g_mix_pre) * (1.0 + sc1) + sh1
    y, aux = mixer(h)
    x = x + gt1 * rmsnorm(y, g_mix_post)
    h = rmsnorm(x, g_ff_pre) * (1.0 + sc2) + sh2
    x = x + gt2 * rmsnorm(sq_relu_mlp(h, w_ff1, w_ff2), g_ff_post)
    return x, aux


def setup_inputs(seed: int = 0) -> dict:
    key = jax.random.key(seed)
    ks = iter(jax.random.split(key, 80))

    def nrm(shape, scale=1.0):
        return jax.random.normal(next(ks), shape, F32) * scale

    def gain(shape):
        return 1.0 + nrm(shape, 0.05)

    inp = {}
    inp["x_prompt"] = nrm((BATCH, SEQ, D_MODEL))
    inp["x_sample"] = nrm((DEC_BATCH, DEC_SEQ, D_MODEL))
    inp["state_l0_s5_re"] = nrm((DEC_BATCH, 2, S5_GROUPS, S5_STATE), 0.1)
    inp["state_l0_s5_im"] = nrm((DEC_BATCH, 2, S5_GROUPS, S5_STATE), 0.1)
    inp["c"] = nrm((DEC_BATCH, D_MODEL))
    inp["c_ctx"] = nrm((D_MODEL,))
    for li in range(DEPTH):
        p = "l%d_" % li
        inp[p + "w_mod"] = nrm((D_MODEL, N_MOD * D_MODEL), 0.5 * D_MODEL ** -0.5)
        inp[p + "b_mod"] = nrm((N_MOD * D_MODEL,), 0.02)
        inp[p + "g_mix_pre"] = gain((D_MODEL,))
        inp[p + "g_mix_post"] = gain((D_MODEL,))
        inp[p + "g_ff_pre"] = gain((D_MODEL,))
        inp[p + "g_ff_post"] = gain((D_MODEL,))
        inp[p + "w_ff1"] = nrm((D_MODEL, D_FF), D_MODEL ** -0.5)
        inp[p + "w_ff2"] = nrm((D_FF, D_MODEL), D_FF ** -0.5)
        if li % 2 == 0:
            inp[p + "w_in"] = nrm((D_MODEL, MIX_WIDTH), D_MODEL ** -0.5)
            inp[p + "w_out"] = nrm((MIX_WIDTH, D_MODEL), MIX_WIDTH ** -0.5)
            n_idx = jnp.arange(S5_STATE, dtype=F32)
            inp[p + "s5_lambda_re"] = -0.5 + nrm((2, S5_GROUPS, S5_STATE), 0.01)
            inp[p + "s5_lambda_im"] = math.pi * n_idx + nrm((2, S5_GROUPS, S5_STATE), 0.01)
            inp[p + "s5_log_dt"] = jax.random.uniform(next(ks), (2, S5_GROUPS), F32,
                                                      math.log(DT_MIN), math.log(DT_MAX))
            inp[p + "s5_b_re"] = nrm((2, S5_GROUPS, S5_STATE, S5_GROUP), (2 * S5_GROUP) ** -0.5)
            inp[p + "s5_b_im"] = nrm((2, S5_GROUPS, S5_STATE, S5_GROUP), (2 * S5_GROUP) ** -0.5)
            inp[p + "s5_c_re"] = nrm((2, S5_GROUPS, S5_GROUP, S5_STATE), (2 * S5_STATE) ** -0.5)
            inp[p + "s5_c_im"] = nrm((2, S5_GROUPS, S5_GROUP, S5_STATE), (2 * S5_STATE) ** -0.5)
            inp[p + "s5_d"] = nrm((S5_CH,))
            inp[p + "s5_w_glu"] = nrm((S5_CH, S5_CH), S5_CH ** -0.5)
            inp[p + "s5_b_glu"] = nrm((S5_CH,), 0.02)
            inp[p + "fnet_w"] = nrm((FNET_GROUPS, FNET_GW, FNET_GW), FNET_GW ** -0.5)
            inp[p + "fnet_b"] = nrm((FNET_GROUPS, FNET_GW), 0.02)
        else:
            inp[p + "w_in"] = nrm((D_MODEL, POOL_CH + 2 * GMLP_CH), D_MODEL ** -0.5)
            inp[p + "w_out"] = nrm((MIX_WIDTH, D_MODEL), MIX_WIDTH ** -0.5)
            inp[p + "pool_w"] = nrm((POOL_GROUPS, POOL_GW, POOL_GW), POOL_GW ** -0.5)
            inp[p + "pool_scale"] = 1.0 + nrm((POOL_CH,), 0.1)
            inp[p + "gmlp_ln_g"] = gain((GMLP_CH,))
            inp[p + "gmlp_ln_b"] = nrm((GMLP_CH,), 0.02)
            inp[p + "gmlp_ws"] = nrm((GMLP_HEADS, GMLP_CHUNK, GMLP_CHUNK), GMLP_CHUNK ** -0.5)
            inp[p + "gmlp_bs"] = 1.0 + nrm((GMLP_HEADS, GMLP_CHUNK), 0.1)
    return inp


def reference(x_prompt, x_sample, state_l0_s5_re, state_l0_s5_im, c, c_ctx,
              l0_w_mod, l0_b_mod, l0_g_mix_pre, l0_g_mix_post, l0_g_ff_pre, l0_g_ff_post, l0_w_ff1, l0_w_ff2,
              l0_w_in, l0_w_out, l0_s5_lambda_re, l0_s5_lambda_im, l0_s5_log_dt, l0_s5_b_re, l0_s5_b_im,
              l0_s5_c_re, l0_s5_c_im, l0_s5_d, l0_s5_w_glu, l0_s5_b_glu, l0_fnet_w, l0_fnet_b,
              l1_w_mod, l1_b_mod, l1_g_mix_pre, l1_g_mix_post, l1_g_ff_pre, l1_g_ff_post, l1_w_ff1, l1_w_ff2,
              l1_w_in, l1_w_out, l1_pool_w, l1_pool_scale, l1_gmlp_ln_g, l1_gmlp_ln_b, l1_gmlp_ws, l1_gmlp_bs):
    common = (
        (l0_w_mod, l0_b_mod, l0_g_mix_pre, l0_g_mix_post, l0_g_ff_pre, l0_g_ff_post, l0_w_ff1, l0_w_ff2),
        (l1_w_mod, l1_b_mod, l1_g_mix_pre, l1_g_mix_post, l1_g_ff_pre, l1_g_ff_post, l1_w_ff1, l1_w_ff2),
    )
    even_params = (l0_w_in, l0_w_out, l0_s5_lambda_re, l0_s5_lambda_im, l0_s5_log_dt, l0_s5_b_re, l0_s5_b_im,
                   l0_s5_c_re, l0_s5_c_im, l0_s5_d, l0_s5_w_glu, l0_s5_b_glu, l0_fnet_w, l0_fnet_b)
    odd_params = (l1_w_in, l1_w_out, l1_pool_w, l1_pool_scale, l1_gmlp_ln_g, l1_gmlp_ln_b, l1_gmlp_ws, l1_gmlp_bs)

    def trunk(x, cond, h0_re, h0_im):
        states = []
        for i in range(DEPTH):
            if i % 2 == 0:
                mixer = lambda h: even_mixer(h, h0_re, h0_im, *even_params)
            else:
                mixer = lambda h: odd_mixer(h, *odd_params)
            x, aux = run_layer(x, cond, mixer, *common[i])
            states.extend(aux)
        return x, states

    zero_state = jnp.zeros((x_prompt.shape[0], 2, S5_GROUPS, S5_STATE), F32)
    y_prompt, ctx_states = trunk(x_prompt, c_ctx[None, :], zero_state, zero_state)
    new_l0_s5_re = ctx_states[0].astype(x_prompt.dtype)
    new_l0_s5_im = ctx_states[1].astype(x_prompt.dtype)

    xs = x_sample + grid_pos_embed(x_sample.shape[1], x_sample.dtype)[None]
    y_sample, _ = trunk(xs, c, state_l0_s5_re, state_l0_s5_im)

    return (y_prompt, y_sample, new_l0_s5_re, new_l0_s5_im)
```

```python
import math
from contextlib import ExitStack, contextmanager
import numpy as np
import ml_dtypes
import concourse.bass as bass
import concourse.mybir as mybir
from concourse.bass_utils import run_bass_kernel_spmd

F32 = mybir.dt.float32
BF16 = mybir.dt.bfloat16
ALU = mybir.AluOpType
AF = mybir.ActivationFunctionType

SAME_ENGINE_SYNC = True
EPS = 1e-6
NCORES = 8


class Tok:
    __slots__ = ("w", "r")

    def __init__(self):
        self.w = None
        self.r = []


class Prog:
    ENGS = ("pe", "act", "dve", "pool", "sp")
    NDMA = {"sp": 24, "pool": 12, "act": 4}

    def __init__(self, nc, es):
        self.nc = nc
        self.scopes = [es]
        self.streams = {e: [] for e in self.ENGS}
        self.cnt = {e: 0 for e in self.ENGS}
        self.sems = {}
        for e in self.ENGS:
            self.sems[e] = es.enter_context(nc.semaphore("s_" + e))
        self.sems["bar"] = es.enter_context(nc.semaphore("s_bar"))
        self.barcnt = 0
        self.dval = {}
        self.dnext = {}
        for q, n in self.NDMA.items():
            self.dnext[q] = 0
            for k in range(n):
                key = "d_%s_%d" % (q, k)
                self.sems[key] = es.enter_context(nc.semaphore(key))
                self.dval[key] = 0
        self.waited = {}
        self.ninstr = 0
        self.uid = 0
        self.psum = []
        self.psum_next = 0
        self.last_unsig = None

    def sb(self, name, shape, dt):
        self.uid += 1
        return self.scopes[-1].enter_context(self.nc.sbuf_tensor("%s_%d" % (name, self.uid), list(shape), dt))

    def init_psum(self):
        for k in range(8):
            t = self.scopes[0].enter_context(self.nc.psum_tensor("psb%d" % k, [128, 512], F32))
            self.psum.append((t, Tok()))

    def ps(self):
        k = self.psum_next
        self.psum_next = (k + 1) % 8
        return self.psum[k]

    @contextmanager
    def scope(self):
        es = ExitStack()
        self.scopes.append(es)
        try:
            yield
        finally:
            self.barrier()
            self.scopes.pop()
            es.close()

    def _need(self, eng, ev, waits):
        if ev is None:
            return
        key, val = ev
        if key == "pe" and eng != "pe" and val > self.cnt["pe"]:
            idx = self.last_unsig
            assert idx is not None and val == self.cnt["pe"] + 1
            ent = self.streams["pe"][idx]
            assert ent[0] == "op" and ent[2] is None
            self.cnt["pe"] += 1
            self.streams["pe"][idx] = ("op", ent[1], self.sems["pe"], 1)
            self.last_unsig = None
        if key == eng:
            if eng in ("pe", "sp"):
                return
            if not SAME_ENGINE_SYNC:
                return
            if val > self.cnt[eng]:
                return
        if val > waits.get(key, 0):
            waits[key] = val

    def _emit_waits(self, eng, waits):
        for key, val in waits.items():
            if self.waited.get((eng, key), 0) >= val:
                continue
            self.waited[(eng, key)] = val
            sem = self.sems[key]
            self.streams[eng].append(("wait", sem, val))

    def _deps(self, eng, reads, writes):
        waits = {}
        for t in reads:
            self._need(eng, t.w, waits)
        for t in writes:
            self._need(eng, t.w, waits)
            for ev in t.r:
                self._need(eng, ev, waits)
        return waits

    def _commit(self, ev, reads, writes):
        for t in writes:
            t.w = ev
            t.r = []
        for t in reads:
            if t not in writes:
                t.r.append(ev)
                if len(t.r) > 48:
                    best = {}
                    for k, v in t.r:
                        if v > best.get(k, 0):
                            best[k] = v
                    t.r = list(best.items())

    def op(self, eng, fn, reads=(), writes=(), signal=True):
        waits = self._deps(eng, reads, writes)
        self._emit_waits(eng, waits)
        self.ninstr += 1
        if signal:
            self.cnt[eng] += 1
            ev = (eng, self.cnt[eng])
            self.streams[eng].append(("op", fn, self.sems[eng], 1))
            if eng == "pe":
                self.last_unsig = None
        else:
            assert eng == "pe"
            ev = (eng, self.cnt[eng] + 1)
            self.streams[eng].append(("op", fn, None, 0))
            self.last_unsig = len(self.streams[eng]) - 1
        self._commit(ev, reads, writes)
        return ev

    def dma(self, q, fn, reads=(), writes=()):
        waits = self._deps(q, reads, writes)
        n = self.NDMA[q]
        k = self.dnext[q]
        self.dnext[q] = (k + 1) % n
        key = "d_%s_%d" % (q, k)
        if self.dval[key] > waits.get(key, 0):
            waits[key] = self.dval[key]
        self._emit_waits(q, waits)
        self.dval[key] += 16
        ev = (key, self.dval[key])
        sem = self.sems[key]
        self.ninstr += 1
        self.streams[q].append(("op", fn, sem, 16))
        self._commit(ev, reads, writes)
        return ev

    def barrier(self):
        assert self.last_unsig is None, "barrier with unsignalled PE work pending"
        for e in ("pe", "act", "dve", "pool"):
            if self.cnt[e] > self.waited.get(("sp", e), 0):
                self.waited[("sp", e)] = self.cnt[e]
                self.streams["sp"].append(("wait", self.sems[e], self.cnt[e]))
        for key, val in self.dval.items():
            if val > self.waited.get(("sp", key), 0):
                self.waited[("sp", key)] = val
                self.streams["sp"].append(("wait", self.sems[key], val))
        self.barcnt += 1
        bs = self.sems["bar"]
        self.streams["sp"].append(("seminc", bs, 1))
        for e in ("pe", "act", "dve", "pool"):
            self.streams[e].append(("wait", bs, self.barcnt))
            for e2 in ("pe", "act", "dve", "pool"):
                self.waited[(e, e2)] = max(self.waited.get((e, e2), 0), self.cnt[e2])
            for key, val in self.dval.items():
                self.waited[(e, key)] = max(self.waited.get((e, key), 0), val)

    def finish(self):
        self.barrier()
        nc = self.nc
        streams = self.streams

        def run(e, lst):
            for ent in lst:
                if ent[0] == "op":
                    ins = ent[1](e)
                    if ent[2] is not None:
                        ins.then_inc(ent[2], ent[3])
                elif ent[0] == "wait":
                    e.wait_ge(ent[1], ent[2])
                else:
                    e.sem_inc(ent[1], ent[2])

        with nc.Block() as block:
            @block.sync
            def _(e):
                run(e, streams["sp"])

            @block.tensor
            def _(e):
                run(e, streams["pe"])

            @block.scalar
            def _(e):
                run(e, streams["act"])

            @block.vector
            def _(e):
                run(e, streams["dve"])

            @block.gpsimd
            def _(e):
                run(e, streams["pool"])


class _Stop(Exception):
    pass


class T:
    def __init__(self, P, name, shape, dt):
        self.t = P.sb(name, shape, dt)
        self.k = Tok()
        self.shape = list(shape)

    def __getitem__(self, idx):
        return self.t[idx]

    @property
    def a(self):
        return self.t[:]


def build(stage=9, dbg=()):
    nc = bass.Bass("TRN2", target_bir_lowering=False)
    D = {}

    def din(name, shape, dt=F32):
        D[name] = nc.dram_tensor(name, list(shape), dt, kind="ExternalInput").ap()
        return D[name]

    def dout(name, shape, dt=F32):
        D[name] = nc.dram_tensor(name, list(shape), dt, kind="ExternalOutput").ap()
        return D[name]

    def dscr(name, shape, dt=F32):
        D[name] = nc.dram_tensor(name, list(shape), dt, kind="Internal").ap()
        return D[name]

    din("xpT", [1024, 1024])
    din("xsT", [1024, 4096])
    din("posT", [1024, 4096])
    din("cond", [128, 16])
    din("bmod", [128, 96])
    din("gains", [128, 64])
    for l in range(2):
        din("w_mod%d" % l, [1024, 6144])
        din("w_ff1_%d" % l, [1024, 4096])
        din("w_ff2_%d" % l, [4096, 1024])
        din("w_out%d" % l, [1024, 1024])
    din("w_in0", [1024, 1024])
    din("w_in1", [1024, 1536])
    din("w_glu", [768, 768])
    din("s5C", [128, 7, 768])
    din("s5B", [128, 5, 768])
    din("cmisc", [128, 64])
    din("ident", [128, 128])
    din("h0s", [128, 2, 48])
    din("fconst", [128, 4, 128])
    din("ctp", [128, 2, 2, 256])
    din("meta", [128, 16])
    din("dmask", [128, 2, 2, 128])
    din("cts", [128, 32, 2, 2048], BF16)
    din("pool_w", [4, 128, 128])
    din("pmat", [128, 10, 4, 128])
    din("wsT", [128, 4, 128])
    din("lnv", [3, 512])
    dout("ypT", [1024, 1024])
    dout("ysT", [1024, 1024])
    dout("st_re", [128, 6, 32])
    dout("st_im", [128, 6, 32])
    for ent in dbg:
        dout(ent[0], ent[1], BF16 if (len(ent) > 2 and ent[2] == 'bf16') else F32)
    dscr("scr_bv", [6, 128, 4096], BF16)
    dscr("scr_yc", [6, 128, 4096], BF16)
    dscr("scr_kt", [6, 128, 1920], BF16)
    dscr("scr_e", [6, 128, 2, 1024], F32)
    dscr("scr_x", [1024, 4096], F32)
    dscr("scr_d", [6, 2, 128, 1024], F32)
    dscr("scr_u", [128, 32, 512], BF16)

    with ExitStack() as es:
        P = Prog(nc, es)
        P.init_psum()
        try:
            _build_body(nc, P, D, stage, set(e[0] for e in dbg))
        except _Stop:
            pass
        P.finish()
    return nc


def _build_body(nc, P, D, stage, dbg):
    op, dma = P.op, P.dma

    def load(dst_ap, dst_tok, src_ap, q="sp"):
        return dma(q, lambda e: e.dma_start(out=dst_ap, in_=src_ap), writes=[dst_tok])

    scr_tok = {}

    def store(dst_ap, src_ap, src_tok, q="sp", key=None):
        tk = Tok()
        if key is not None:
            scr_tok[key] = tk
        return dma(q, lambda e: e.dma_start(out=dst_ap, in_=src_ap), reads=[src_tok], writes=[tk])

    def loadr(dst_ap, dst_tok, src_ap, src_tok, q="sp"):
        return dma(q, lambda e: e.dma_start(out=dst_ap, in_=src_ap), reads=[src_tok], writes=[dst_tok])

    def tt(eng, out, a, b, alu, reads, writes):
        return op(eng, lambda e: e.tensor_tensor(out=out, in0=a, in1=b, op=alu), reads=reads, writes=writes)

    def ts(eng, out, a, s1, op0, reads, writes, s2=None, op1=None):
        if op1 is None:
            return op(eng, lambda e: e.tensor_scalar(out=out, in0=a, scalar1=s1, scalar2=None, op0=op0), reads=reads, writes=writes)
        return op(eng, lambda e: e.tensor_scalar(out=out, in0=a, scalar1=s1, scalar2=s2, op0=op0, op1=op1), reads=reads, writes=writes)

    def stt(out, a, s, b, op0, op1, reads, writes):
        return op("dve", lambda e: e.scalar_tensor_tensor(out=out, in0=a, scalar=s, in1=b, op0=op0, op1=op1), reads=reads, writes=writes)

    def act(out, a, func, reads, writes, scale=None, bias=None):
        kw = {}
        if scale is not None:
            kw["scale"] = scale
        if bias is not None:
            kw["bias"] = bias
        return op("act", lambda e: e.activation(out=out, in_=a, func=func, **kw), reads=reads, writes=writes)

    def mm(out, lhsT, rhs, start, stop, reads, ptok, signal, tp=None):
        kw = {}
        if tp is not None:
            kw["tile_position"] = tp
        return op("pe", lambda e: e.matmul(out, lhsT, rhs, start=start, stop=stop, **kw), reads=reads, writes=[ptok], signal=signal)

    def cp(eng, out, in_, reads, writes):
        return op(eng, lambda e: e.tensor_copy(out=out, in_=in_), reads=reads, writes=writes)

    def recip(out, in_, reads, writes):
        return op("dve", lambda e: e.reciprocal(out=out, in_=in_), reads=reads, writes=writes)

    def ckpt(x):
        if stage < x:
            raise _Stop()

    def dbg_out(name, src_ap, tok):
        if name in dbg:
            store(D[name], src_ap, tok)

    ones_bf = T(P, "ones", [128, 128], BF16)
    op("pool", lambda e: e.memset(ones_bf.a, 1.0), writes=[ones_bf.k])
    epsc = T(P, "epsc", [128, 1], F32)
    op("pool", lambda e: e.memset(epsc.a, EPS), writes=[epsc.k])
    halfpi = T(P, "halfpi", [128, 1], F32)
    op("pool", lambda e: e.memset(halfpi.a, math.pi / 2), writes=[halfpi.k])
    cmisc = T(P, "cmisc", [128, 64], F32)
    load(cmisc.a, cmisc.k, D["cmisc"])
    ident = T(P, "ident", [128, 128], F32)
    load(ident.a, ident.k, D["ident"])
    meta = T(P, "meta", [128, 16], F32)
    load(meta.a, meta.k, D["meta"])
    maskJB = lambda j: cmisc[:, j:j + 1]
    maskJC = lambda j: cmisc[:, 24 + j:25 + j]
    nmaskJC = lambda j: cmisc[:, 26 + j:27 + j]
    maskQ = lambda q: cmisc[:, 2 + q:3 + q]
    dcol = lambda i: cmisc[:, 6 + i:7 + i]
    bglu = lambda i: cmisc[:, 12 + i:13 + i]
    fnetb = lambda jt: cmisc[:, 18 + jt:19 + jt]
    poolsc = lambda g: cmisc[:, 20 + g:21 + g]

    mods = [dict(), dict()]
    mod_all = T(P, "mod_all", [128, 2, 6, 16], F32)
    with P.scope():
        cond = T(P, "cond", [128, 16], F32)
        load(cond.a, cond.k, D["cond"])
        scT = T(P, "scT", [128, 16], BF16)
        act(scT.a, cond.a, AF.Silu, [cond.k], [scT.k])
        bmod = T(P, "bmod", [128, 96], F32)
        load(bmod.a, bmod.k, D["bmod"])
        gains = T(P, "gains", [128, 64], F32)
        load(gains.a, gains.k, D["gains"])
        modv = T(P, "modv", [128, 2, 48, 2], F32)
        wslab = [T(P, "wslab%d" % b, [128, 8, 768], BF16) for b in range(3)]
        nslab = 0
        for l in range(2):
            wsrc = D["w_mod%d" % l].rearrange("(kt p) f -> p kt f", p=128)
            pt, pk = P.ps()
            for sl in range(8):
                wb = wslab[nslab % 3]
                nslab += 1
                dma("pool", lambda e, wb=wb, sl=sl, wsrc=wsrc: e.dma_start(out=wb.a, in_=wsrc[:, :, sl * 768:(sl + 1) * 768]), writes=[wb.k])
                for f6 in range(6):
                    ft = sl * 6 + f6
                    for kt in range(8):
                        mm(pt[:, 2 * ft:2 * ft + 2], wb[:, kt, f6 * 128:(f6 + 1) * 128], scT[:, 2 * kt:2 * kt + 2],
                           kt == 0, kt == 7, [wb.k, scT.k], pk, signal=(kt == 7))
            pv = pt[:, 0:96].rearrange("p (f c) -> p f c", c=2)
            for c in range(2):
                tt("dve", modv[:, l, :, c], pv[:, :, c], bmod[:, l * 48:(l + 1) * 48], ALU.add, [pk, bmod.k], [modv.k])
            gv = lambda kind: gains[:, (l * 4 + kind) * 8:(l * 4 + kind) * 8 + 8]
            mo = lambda m, c: modv[:, l, 8 * m:8 * m + 8, c]
            ma = lambda kind, c: mod_all[:, l, kind, :].rearrange("p (d c) -> p d c", c=2)[:, :, c]
            for c in range(2):
                stt(ma(0, c), mo(1, c), 1.0, gv(0), ALU.add, ALU.mult, [modv.k, gains.k], [mod_all.k])
                cp("dve", ma(1, c), mo(0, c), [modv.k], [mod_all.k])
                tt("dve", ma(2, c), mo(2, c), gv(1), ALU.mult, [modv.k, gains.k], [mod_all.k])
                stt(ma(3, c), mo(4, c), 1.0, gv(2), ALU.add, ALU.mult, [modv.k, gains.k], [mod_all.k])
                cp("dve", ma(4, c), mo(3, c), [modv.k], [mod_all.k])
                tt("dve", ma(5, c), mo(5, c), gv(3), ALU.mult, [modv.k, gains.k], [mod_all.k])
        dbg_out("d_mod", mod_all.a.rearrange("p a b c -> p (a b c)"), mod_all.k)

    def modcol(l, kind, dt, c):
        return mod_all[:, l, kind, 2 * dt + c:2 * dt + c + 1]

    if stage <= 0:
        return

    lv = T(P, "lv", [128, 16, 48], F32)
    NI = 2
    W = NI * 128
    cnt = [0]

    def tmp(name="t", w=None):
        cnt[0] += 1
        return T(P, "%s%d" % (name, cnt[0]), [128, w or W], F32)

    def a_chain(eng, lre, lim, ldt, ktok, npow, bre, bim):
        r = {}
        dt = tmp("dt")
        act(dt.a, ldt, AF.Exp, [ktok], [dt.k])
        lr = tmp("lr"); li = tmp("li")
        tt(eng, lr.a, lre, dt.a, ALU.mult, [ktok, dt.k], [lr.k])
        tt(eng, li.a, lim, dt.a, ALU.mult, [ktok, dt.k], [li.k])
        mag = tmp("mag")
        act(mag.a, lr.a, AF.Exp, [lr.k], [mag.k])
        c = tmp("c"); s = tmp("s")
        act(s.a, li.a, AF.Sin, [li.k], [s.k], scale=0.125)
        act(c.a, li.a, AF.Sin, [li.k], [c.k], scale=-0.125, bias=halfpi.a)
        t1 = tmp("t1"); t2 = tmp("t2")
        for _ in range(3):
            c2 = tmp("c"); s2 = tmp("s")
            tt(eng, t1.a, c.a, c.a, ALU.mult, [c.k], [t1.k])
            tt(eng, t2.a, s.a, s.a, ALU.mult, [s.k], [t2.k])
            tt(eng, s2.a, c.a, s.a, ALU.mult, [c.k, s.k], [s2.k])
            ts(eng, s2.a, s2.a, 2.0, ALU.mult, [s2.k], [s2.k])
            tt(eng, c2.a, t1.a, t2.a, ALU.subtract, [t1.k, t2.k], [c2.k])
            c, s = c2, s2
        r["unit"] = (c, s)
        r["lr"] = lr
        r["li"] = li
        abr = tmp("abr"); abi = tmp("abi")
        tt(eng, abr.a, mag.a, c.a, ALU.mult, [mag.k, c.k], [abr.k])
        tt(eng, abi.a, mag.a, s.a, ALU.mult, [mag.k, s.k], [abi.k])
        r["ab"] = (abr, abi)
        nr = tmp("nr")
        ts(eng, nr.a, abr.a, -1.0, ALU.add, [abr.k], [nr.k])
        den = tmp("den")
        tt(eng, t1.a, lre, lre, ALU.mult, [ktok], [t1.k])
        tt(eng, t2.a, lim, lim, ALU.mult, [ktok], [t2.k])
        tt(eng, den.a, t1.a, t2.a, ALU.add, [t1.k, t2.k], [den.k])
        recip(den.a, den.a, [den.k], [den.k])
        cr = tmp("cr"); ci = tmp("ci")
        tt(eng, t1.a, nr.a, lre, ALU.mult, [nr.k, ktok], [t1.k])
        tt(eng, t2.a, abi.a, lim, ALU.mult, [abi.k, ktok], [t2.k])
        tt(eng, cr.a, t1.a, t2.a, ALU.add, [t1.k, t2.k], [cr.k])
        tt(eng, cr.a, cr.a, den.a, ALU.mult, [cr.k, den.k], [cr.k])
        tt(eng, t1.a, abi.a, lre, ALU.mult, [abi.k, ktok], [t1.k])
        tt(eng, t2.a, nr.a, lim, ALU.mult, [nr.k, ktok], [t2.k])
        tt(eng, ci.a, t1.a, t2.a, ALU.subtract, [t1.k, t2.k], [ci.k])
        tt(eng, ci.a, ci.a, den.a, ALU.mult, [ci.k, den.k], [ci.k])
        Br = tmp("Br"); Bi = tmp("Bi")
        tt(eng, t1.a, cr.a, bre, ALU.mult, [cr.k, ktok], [t1.k])
        tt(eng, t2.a, ci.a, bim, ALU.mult, [ci.k, ktok], [t2.k])
        tt(eng, Br.a, t1.a, t2.a, ALU.subtract, [t1.k, t2.k], [Br.k])
        tt(eng, t1.a, cr.a, bim, ALU.mult, [cr.k, ktok], [t1.k])
        tt(eng, t2.a, ci.a, bre, ALU.mult, [ci.k, ktok], [t2.k])
        tt(eng, Bi.a, t1.a, t2.a, ALU.add, [t1.k, t2.k], [Bi.k])
        r["Bb"] = (Br, Bi)
        pw = {1: (abr, abi)}
        for k in range(2, npow + 1):
            pr = tmp("pr"); pi = tmp("pi")
            p0r, p0i = pw[k - 1]
            tt(eng, t1.a, p0r.a, abr.a, ALU.mult, [p0r.k, abr.k], [t1.k])
            tt(eng, t2.a, p0i.a, abi.a, ALU.mult, [p0i.k, abi.k], [t2.k])
            tt(eng, pr.a, t1.a, t2.a, ALU.subtract, [t1.k, t2.k], [pr.k])
            tt(eng, t1.a, p0r.a, abi.a, ALU.mult, [p0r.k, abi.k], [t1.k])
            tt(eng, t2.a, p0i.a, abr.a, ALU.mult, [p0i.k, abr.k], [t2.k])
            tt(eng, pi.a, t1.a, t2.a, ALU.add, [t1.k, t2.k], [pi.k])
            pw[k] = (pr, pi)
        r["pw"] = pw
        r["t1"], r["t2"] = t1, t2
        return r

    def s5_prep(half):
        i0 = NI * half
        NC_ = NI * 8
        c8k = T(P, "c8k", [128, NC_], F32); s8k = T(P, "s8k", [128, NC_], F32); lrk = T(P, "lrk", [128, NC_], F32); lik = T(P, "lik", [128, NC_], F32)
        ldk = T(P, "ldk", [128, NC_], F32); limk = T(P, "limk", [128, NC_], F32); lrek = T(P, "lrek", [128, NC_], F32)
        with P.scope():
            s5C = T(P, "s5C", [128, 7, W], F32)
            load(s5C.a, s5C.k, D["s5C"][:, :, W * half:W * (half + 1)])
            s5B = T(P, "s5B", [128, 5, W], F32)
            load(s5B.a, s5B.k, D["s5B"][:, :, W * half:W * (half + 1)])
            rc = a_chain("dve", s5C[:, 0, :], s5C[:, 1, :], s5C[:, 2, :], s5C.k, 8, s5C[:, 5, :], s5C[:, 6, :])
            rb = a_chain("pool", s5B[:, 0, :], s5B[:, 1, :], s5B[:, 2, :], s5B.k, 7, s5B[:, 3, :], s5B[:, 4, :])
            Br, Bi = rb["Bb"]
            bvt = T(P, "bvt", [128, NI * 4096], BF16)
            bv6 = bvt.a.rearrange("p (i d s r j q) -> p i d s r j q", i=NI, d=2, s=8, r=2, j=2)
            wre = tmp("wre"); wim = tmp("wim")
            t1, t2 = rb["t1"], rb["t2"]
            v3 = lambda t: t.a.rearrange("p (i d q) -> p i d q", i=NI, d=2)
            for s in range(8):
                for d in range(2):
                    e = (7 - s) if d == 0 else s
                    sl = lambda t, d=d: v3(t)[:, :, d, :]
                    if e == 0:
                        cp("pool", sl(wre), sl(Br), [Br.k], [wre.k])
                        cp("pool", sl(wim), sl(Bi), [Bi.k], [wim.k])
                    else:
                        pr, pi = rb["pw"][e]
                        tt("pool", sl(t1), sl(pr), sl(Br), ALU.mult, [pr.k, Br.k], [t1.k])
                        tt("pool", sl(t2), sl(pi), sl(Bi), ALU.mult, [pi.k, Bi.k], [t2.k])
                        tt("pool", sl(wre), sl(t1), sl(t2), ALU.subtract, [t1.k, t2.k], [wre.k])
                        tt("pool", sl(t1), sl(pr), sl(Bi), ALU.mult, [pr.k, Bi.k], [t1.k])
                        tt("pool", sl(t2), sl(pi), sl(Br), ALU.mult, [pi.k, Br.k], [t2.k])
                        tt("pool", sl(wim), sl(t1), sl(t2), ALU.add, [t1.k, t2.k], [wim.k])
                for r_, src_ in ((0, wre), (1, wim)):
                    for j in range(2):
                        act(bv6[:, :, :, s, r_, j, :], v3(src_), AF.Copy, [src_.k, cmisc.k], [bvt.k], scale=maskJB(j))
            for i in range(NI):
                store(D["scr_bv"][i0 + i], bvt[:, i * 4096:(i + 1) * 4096], bvt.k, key=("bv", i0 + i))
            if half == 0:
                dbg_out("d_bv0", bvt[:, 0:4096], bvt.k)


            cre, cim = s5C[:, 3, :], s5C[:, 4, :]
            t1, t2 = rc["t1"], rc["t2"]
            v4 = lambda ap: ap.rearrange("p (i q d h) -> p i q d h", i=NI, q=4, d=2)
            h0v = lambda t: v4(t.a)[:, :, :, :, 0].rearrange("p i q d -> p (i q d)")
            uc, us = rc["unit"]
            cp("dve", c8k.a, h0v(uc), [uc.k], [c8k.k])
            cp("dve", s8k.a, h0v(us), [us.k], [s8k.k])
            cp("dve", ldk.a, v4(s5C[:, 2, :])[:, :, :, :, 0].rearrange("p i q d -> p (i q d)"), [s5C.k], [ldk.k])
            cp("dve", limk.a, v4(s5C[:, 1, :])[:, :, :, :, 0].rearrange("p i q d -> p (i q d)"), [s5C.k], [limk.k])
            cp("dve", lrek.a, v4(s5C[:, 0, :])[:, :, :, :, 0].rearrange("p i q d -> p (i q d)"), [s5C.k], [lrek.k])
            yct = T(P, "yct", [128, NI * 4096], BF16)
            yc7 = yct.a.rearrange("p (i q d s r j h) -> p i q d s r j h", i=NI, q=4, d=2, s=8, r=2, j=2)
            cpr = tmp("cpr"); cpi = tmp("cpi")
            for s in range(8):
                for d in range(2):
                    e = (s + 1) if d == 0 else (8 - s)
                    pr, pi = rc["pw"][e]
                    sl = lambda ap, d=d: v4(ap)[:, :, :, d, :]
                    tt("dve", sl(t1.a), sl(cre), sl(pr.a), ALU.mult, [s5C.k, pr.k], [t1.k])
                    tt("dve", sl(t2.a), sl(cim), sl(pi.a), ALU.mult, [s5C.k, pi.k], [t2.k])
                    tt("dve", sl(cpr.a), sl(t1.a), sl(t2.a), ALU.subtract, [t1.k, t2.k], [cpr.k])
                    tt("dve", sl(t1.a), sl(cre), sl(pi.a), ALU.mult, [s5C.k, pi.k], [t1.k])
                    tt("dve", sl(t2.a), sl(cim), sl(pr.a), ALU.mult, [s5C.k, pr.k], [t2.k])
                    stt(sl(cpi.a), sl(t1.a), -1.0, sl(t2.a), ALU.mult, ALU.subtract, [t1.k, t2.k], [cpi.k])
                for r_, src_ in ((0, cpr), (1, cpi)):
                    for j in range(2):
                        for i in range(NI):
                            act(yc7[:, i, :, :, s, r_, j, :], v4(src_.a)[:, i], AF.Copy, [src_.k, cmisc.k], [yct.k], scale=maskJC(j))
            for i in range(NI):
                store(D["scr_yc"][i0 + i], yct[:, i * 4096:(i + 1) * 4096], yct.k, key=("yc", i0 + i))
            if half == 0:
                dbg_out("d_yc0", yct[:, 0:4096], yct.k)
            Brc, Bic = rc["Bb"]
            Lre = T(P, "Lre", [128, NI * 2048], F32)
            Lim = T(P, "Lim", [128, NI * 2048], F32)
            L6 = lambda t: t.a.rearrange("p (i q d l j h) -> p i q d l j h", i=NI, q=4, d=2, l=8, j=2)
            Rre = T(P, "Rre", [128, NI * 256], F32)
            Rim = T(P, "Rim", [128, NI * 256], F32)
            R5 = lambda t: t.a.rearrange("p (i q d j h) -> p i q d j h", i=NI, q=4, d=2, j=2)
            for j in range(2):
                for i in range(NI):
                    act(R5(Rre)[:, i, :, :, j, :], v4(cre)[:, i], AF.Copy, [s5C.k, cmisc.k], [Rre.k], scale=maskJC(j))
                    act(R5(Rim)[:, i, :, :, j, :], v4(cim)[:, i], AF.Copy, [s5C.k, cmisc.k], [Rim.k], scale=nmaskJC(j))
            lre_t = tmp("lre_t"); lim_t = tmp("lim_t")
            for dlag in range(8):
                if dlag == 0:
                    srcs = (Brc, Bic)
                else:
                    pr, pi = rc["pw"][dlag]
                    tt("dve", t1.a, pr.a, Brc.a, ALU.mult, [pr.k, Brc.k], [t1.k])
                    tt("dve", t2.a, pi.a, Bic.a, ALU.mult, [pi.k, Bic.k], [t2.k])
                    tt("dve", lre_t.a, t1.a, t2.a, ALU.subtract, [t1.k, t2.k], [lre_t.k])
                    tt("dve", t1.a, pr.a, Bic.a, ALU.mult, [pr.k, Bic.k], [t1.k])
                    tt("dve", t2.a, pi.a, Brc.a, ALU.mult, [pi.k, Brc.k], [t2.k])
                    tt("dve", lim_t.a, t1.a, t2.a, ALU.add, [t1.k, t2.k], [lim_t.k])
                    srcs = (lre_t, lim_t)
                for dst, src_ in ((Lre, srcs[0]), (Lim, srcs[1])):
                    for j in range(2):
                        for i in range(NI):
                            act(L6(dst)[:, i, :, :, dlag, j, :], v4(src_.a)[:, i], AF.Copy, [src_.k, cmisc.k], [dst.k], scale=maskJC(j))
            kt_sb = T(P, "kt_sb", [128, NI * 1920], BF16)
            kt4 = kt_sb.a.rearrange("p (i s q c) -> p i s q c", i=NI, s=15, q=4)
            dtmp = T(P, "dtmp", [128, 32], F32)
            for i in range(NI):
                pt, pk = P.ps()
                pv = pt[:, 0:480].rearrange("p (s c) -> p s c", c=32)
                for q in range(4):
                    rows = slice(32 * q, 32 * q + 32)
                    for slot in range(15):
                        if slot == 0:
                            terms = [(0, 0), (1, 0)]
                        elif slot < 8:
                            terms = [(0, slot)]
                        else:
                            terms = [(1, slot - 7)]
                        n = 2 * len(terms)
                        kk = 0
                        for (d, dlag) in terms:
                            for (Lt, Rt) in ((Lre, Rre), (Lim, Rim)):
                                mm(pv[rows, slot, :], L6(Lt)[:, i, q, d, dlag].rearrange("p j h -> p (j h)"),
                                   R5(Rt)[:, i, q, d].rearrange("p j h -> p (j h)"), kk == 0, kk == n - 1, [Lt.k, Rt.k], pk,
                                   signal=(kk == n - 1), tp=(0, 32 * q))
                                kk += 1
                for q2 in range(4):
                    ts("dve", kt4[:, i, 1:15, q2, :], pv[:, 1:15, :], maskQ(q2), ALU.mult, [pk, cmisc.k], [kt_sb.k])
                    ts("dve", dtmp.a, ident[:, 32 * q2:32 * q2 + 32], dcol(i0 + i), ALU.mult, [ident.k, cmisc.k], [dtmp.k])
                    stt(kt4[:, i, 0, q2, :], pv[:, 0, :], maskQ(q2), dtmp.a, ALU.mult, ALU.add, [pk, cmisc.k, dtmp.k], [kt_sb.k])
            for i in range(NI):
                store(D["scr_kt"][i0 + i], kt_sb[:, i * 1920:(i + 1) * 1920], kt_sb.k, key=("kt", i0 + i))
            if half == 0:
                dbg_out("d_kt0", kt_sb[:, 0:1920], kt_sb.k)

        with P.scope():
            a48 = T(P, "a48", [128, NC_], F32); b48 = T(P, "b48", [128, NC_], F32)
            lvs = lambda slot: lv[:, slot, NC_ * half:NC_ * (half + 1)]

            def csq(c, s):
                c2 = T(P, "cq", [128, NC_], F32); s2 = T(P, "sq", [128, NC_], F32)
                tt("dve", a48.a, c.a, c.a, ALU.mult, [c.k], [a48.k])
                tt("dve", b48.a, s.a, s.a, ALU.mult, [s.k], [b48.k])
                tt("dve", s2.a, c.a, s.a, ALU.mult, [c.k, s.k], [s2.k])
                ts("dve", s2.a, s2.a, 2.0, ALU.mult, [s2.k], [s2.k])
                tt("dve", c2.a, a48.a, b48.a, ALU.subtract, [a48.k, b48.k], [c2.k])
                return c2, s2
            MAGIC = 12582912.0

            def sm(name="sm"):
                return T(P, name, [128, NC_], F32)
            nn = sm("nn")
            ts("dve", nn.a, ldk.a, 1.4426950408889634, ALU.mult, [ldk.k], [nn.k])
            ts("dve", nn.a, nn.a, MAGIC, ALU.add, [nn.k], [nn.k])
            ts("dve", nn.a, nn.a, MAGIC, ALU.subtract, [nn.k], [nn.k])
            rx = sm("rx")
            stt(rx.a, nn.a, -0.693359375, ldk.a, ALU.mult, ALU.add, [nn.k, ldk.k], [rx.k])
            stt(rx.a, nn.a, 2.12194440e-4, rx.a, ALU.mult, ALU.add, [nn.k, rx.k], [rx.k])
            er = sm("er")
            ts("dve", er.a, rx.a, 1.0 / 10.0, ALU.mult, [rx.k], [er.k], s2=1.0, op1=ALU.add)
            for n in range(9, 0, -1):
                stt(er.a, er.a, 1.0 / n, rx.a, ALU.mult, ALU.mult, [er.k, rx.k], [er.k])
                ts("dve", er.a, er.a, 1.0, ALU.add, [er.k], [er.k])
            p2 = sm("p2"); p2t = sm("p2t")
            first = True
            for v in range(-12, 0):
                dst = p2 if first else p2t
                ts("dve", dst.a, nn.a, float(v), ALU.is_equal, [nn.k], [dst.k], s2=float(2.0 ** v), op1=ALU.mult)
                if not first:
                    tt("dve", p2.a, p2.a, p2t.a, ALU.add, [p2.k, p2t.k], [p2.k])
                first = False
            dta = sm("dta")
            tt("dve", dta.a, er.a, p2.a, ALU.mult, [er.k, p2.k], [dta.k])
            tt("dve", lik.a, limk.a, dta.a, ALU.mult, [limk.k, dta.k], [lik.k])
            tt("dve", lrk.a, lrek.a, dta.a, ALU.mult, [lrek.k, dta.k], [lrk.k])
            phi = sm("phi")
            ts("dve", phi.a, lik.a, 8.0, ALU.mult, [lik.k], [phi.k])
            kf = sm("kf")
            ts("dve", kf.a, phi.a, 2.0 / math.pi, ALU.mult, [phi.k], [kf.k])
            ts("dve", kf.a, kf.a, MAGIC, ALU.add, [kf.k], [kf.k])
            ts("dve", kf.a, kf.a, MAGIC, ALU.subtract, [kf.k], [kf.k])
            rr = sm("rr")
            stt(rr.a, kf.a, -1.5703125, phi.a, ALU.mult, ALU.add, [kf.k, phi.k], [rr.k])
            stt(rr.a, kf.a, -4.837512969970703e-4, rr.a, ALU.mult, ALU.add, [kf.k, rr.k], [rr.k])
            stt(rr.a, kf.a, -7.549789954891882e-8, rr.a, ALU.mult, ALU.add, [kf.k, rr.k], [rr.k])
            z = sm("z")
            tt("dve", z.a, rr.a, rr.a, ALU.mult, [rr.k], [z.k])
            ps_ = sm("ps")
            ts("dve", ps_.a, z.a, 1.0 / 362880.0, ALU.mult, [z.k], [ps_.k])
            for coef in (-1.0 / 5040.0, 1.0 / 120.0, -1.0 / 6.0):
                stt(ps_.a, ps_.a, coef, z.a, ALU.add, ALU.mult, [ps_.k, z.k], [ps_.k])
            sr = sm("sr")
            stt(sr.a, ps_.a, 1.0, rr.a, ALU.add, ALU.mult, [ps_.k, rr.k], [sr.k])
            pc_ = sm("pc")
            ts("dve", pc_.a, z.a, -1.0 / 3628800.0, ALU.mult, [z.k], [pc_.k])
            for coef in (1.0 / 40320.0, -1.0 / 720.0, 1.0 / 24.0, -0.5):
                stt(pc_.a, pc_.a, coef, z.a, ALU.add, ALU.mult, [pc_.k, z.k], [pc_.k])
            cr_ = sm("cr")
            ts("dve", cr_.a, pc_.a, 1.0, ALU.add, [pc_.k], [cr_.k])
            mq = sm("mq")
            ts("dve", mq.a, kf.a, 0.25, ALU.mult, [kf.k], [mq.k], s2=-0.375, op1=ALU.add)
            ts("dve", mq.a, mq.a, MAGIC, ALU.add, [mq.k], [mq.k])
            ts("dve", mq.a, mq.a, MAGIC, ALU.subtract, [mq.k], [mq.k])
            qd = sm("qd")
            stt(qd.a, mq.a, -4.0, kf.a, ALU.mult, ALU.add, [mq.k, kf.k], [qd.k])
            mA = sm("mA"); mB = sm("mB"); m2 = sm("m2")
            ts("dve", mA.a, qd.a, 0.0, ALU.is_equal, [qd.k], [mA.k])
            ts("dve", m2.a, qd.a, 2.0, ALU.is_equal, [qd.k], [m2.k])
            tt("dve", mA.a, mA.a, m2.a, ALU.subtract, [mA.k, m2.k], [mA.k])
            ts("dve", mB.a, qd.a, 1.0, ALU.is_equal, [qd.k], [mB.k])
            ts("dve", m2.a, qd.a, 3.0, ALU.is_equal, [qd.k], [m2.k])
            tt("dve", mB.a, mB.a, m2.a, ALU.subtract, [mB.k, m2.k], [mB.k])
            c8 = sm("c8"); s8 = sm("s8")
            tt("dve", a48.a, sr.a, mA.a, ALU.mult, [sr.k, mA.k], [a48.k])
            tt("dve", b48.a, cr_.a, mB.a, ALU.mult, [cr_.k, mB.k], [b48.k])
            tt("dve", s8.a, a48.a, b48.a, ALU.add, [a48.k, b48.k], [s8.k])
            tt("dve", a48.a, cr_.a, mA.a, ALU.mult, [cr_.k, mA.k], [a48.k])
            tt("dve", b48.a, sr.a, mB.a, ALU.mult, [sr.k, mB.k], [b48.k])
            tt("dve", c8.a, a48.a, b48.a, ALU.subtract, [a48.k, b48.k], [c8.k])
            x8 = sm("x8")
            ts("dve", x8.a, lrk.a, 8.0, ALU.mult, [lrk.k], [x8.k])
            pe_ = sm("pe")
            ts("dve", pe_.a, x8.a, 1.0 / 10.0, ALU.mult, [x8.k], [pe_.k], s2=1.0, op1=ALU.add)
            for n in range(9, 0, -1):
                stt(pe_.a, pe_.a, 1.0 / n, x8.a, ALU.mult, ALU.mult, [pe_.k, x8.k], [pe_.k])
                ts("dve", pe_.a, pe_.a, 1.0, ALU.add, [pe_.k], [pe_.k])
            cp("dve", lvs(0), pe_.a, [pe_.k], [lv.k])
            Ec = T(P, "Ec", [128, NC_, 128], F32); Es = T(P, "Es", [128, NC_, 128], F32)
            op("pool", lambda e: e.memset(Ec[:, :, 0:1], 1.0), writes=[Ec.k])
            op("pool", lambda e: e.memset(Es[:, :, 0:1], 0.0), writes=[Es.k])
            wc, ws_ = c8, s8
            e1 = T(P, "e1", [128, NC_, 64], F32); e2 = T(P, "e2", [128, NC_, 64], F32)
            for m in range(7):
                n = 1 << m
                bc = lambda t, n=n: t.a.unsqueeze(2).to_broadcast([128, NC_, n])
                lo = slice(0, n); hi = slice(n, 2 * n)
                tt("dve", e1[:, :, lo], Ec[:, :, lo], bc(wc), ALU.mult, [Ec.k, wc.k], [e1.k])
                tt("dve", e2[:, :, lo], Es[:, :, lo], bc(ws_), ALU.mult, [Es.k, ws_.k], [e2.k])
                tt("dve", Ec[:, :, hi], e1[:, :, lo], e2[:, :, lo], ALU.subtract, [e1.k, e2.k], [Ec.k])
                tt("dve", e1[:, :, lo], Ec[:, :, lo], bc(ws_), ALU.mult, [Ec.k, ws_.k], [e1.k])
                tt("dve", e2[:, :, lo], Es[:, :, lo], bc(wc), ALU.mult, [Es.k, wc.k], [e2.k])
                tt("dve", Es[:, :, hi], e1[:, :, lo], e2[:, :, lo], ALU.add, [e1.k, e2.k], [Es.k])
                if m < 6:
                    wc, ws_ = csq(wc, ws_)
            cp("dve", lvs(1), c8.a, [c8.k], [lv.k])
            cp("dve", lvs(2), s8.a, [s8.k], [lv.k])
            tt("dve", a48.a, Ec[:, :, 127], c8.a, ALU.mult, [Ec.k, c8.k], [a48.k])
            tt("dve", b48.a, Es[:, :, 127], s8.a, ALU.mult, [Es.k, s8.k], [b48.k])
            tt("dve", lvs(3), a48.a, b48.a, ALU.subtract, [a48.k, b48.k], [lv.k])
            tt("dve", a48.a, Ec[:, :, 127], s8.a, ALU.mult, [Ec.k, s8.k], [a48.k])
            tt("dve", b48.a, Es[:, :, 127], c8.a, ALU.mult, [Es.k, c8.k], [b48.k])
            tt("dve", lvs(4), a48.a, b48.a, ALU.add, [a48.k, b48.k], [lv.k])
            act(lvs(5), lrk.a, AF.Exp, [lrk.k], [lv.k], scale=1024.0)
            Es2 = Es.a.rearrange("p (a d) c -> p a d c", d=2)
            ts("pool", Es2[:, :, 1, :], Es2[:, :, 1, :], -1.0, ALU.mult, [Es.k], [Es.k])
            dmk = T(P, "dmk", [128, 2, 2, 128], F32)
            load(dmk.a, dmk.k, D["dmask"])
            r8x = T(P, "r8x", [128, NC_, 128], F32)
            cp("dve", r8x.a, lvs(0).unsqueeze(2).to_broadcast([128, NC_, 128]), [lv.k], [r8x.k])
            for ty in range(2):
                dtt = T(P, "dtt%d" % ty, [128, NC_, 128], F32)
                tt("dve", dtt.a.rearrange("p (a d) c -> p a d c", d=2), r8x.a.rearrange("p (a d) c -> p a d c", d=2),
                   dmk[:, ty, :, :].unsqueeze(1).to_broadcast([128, NC_ // 2, 2, 128]), ALU.mult, [r8x.k, dmk.k], [dtt.k])
                for i in range(NI):
                    for d in range(2):
                        store(D["scr_d"][i0 + i, ty][:, d * 512:(d + 1) * 512].rearrange("p (q c) -> p q c", q=4),
                              dtt[:, 8 * i:8 * i + 8, :].rearrange("p (q d) c -> p q d c", d=2)[:, :, d, :], dtt.k, key=("dt%d" % d, i0 + i, ty))
            for i in range(NI):
                store(D["scr_e"][i0 + i, :, 0, :], Ec[:, 8 * i:8 * i + 8, :].rearrange("p a c -> p (a c)"), Ec.k, key=("ec", i0 + i))
                store(D["scr_e"][i0 + i, :, 1, :], Es[:, 8 * i:8 * i + 8, :].rearrange("p a c -> p (a c)"), Es.k, key=("es", i0 + i))
            if half == 0:
                dbg_out("d_ec0", Ec[:, 0:8, :].rearrange("p a c -> p (a c)"), Ec.k)
                dbg_out("d_es0", Es[:, 0:8, :].rearrange("p a c -> p (a c)"), Es.k)

    for half in range(6 // NI):
        with P.scope():
            s5_prep(half)
    dbg_out("d_lv", lv.a.rearrange("p a b -> p (a b)"), lv.k)

    if stage <= 1:
        return

    xres = T(P, "xres", [128, 8, 1024], F32)
    _xk0 = [Tok(), Tok()]
    xk = [_xk0, _xk0]
    stout = [T(P, "stout%d" % r_, [128, 192], F32) for r_ in range(2)]
    REG = [dict(name="p", off=0, cond=0, nseq=4, n=32, xsrc=D["xpT"], pos=None),
           dict(name="s", off=1024, cond=1, nseq=1, n=128, xsrc=D["xsT"][:, 0:1024], pos=D["posT"][:, 0:1024])]

    def wload_bf16(dst_ap, dst_tok, src_ap):
        return dma("pool", lambda e: e.dma_start(out=dst_ap, in_=src_ap), writes=[dst_tok])

    class NormScratch:
        def __init__(self):
            self.sq = [T(P, "nsq%d" % b, [128, 512], BF16) for b in range(2)]
            self.rstd = T(P, "nrstd", [128, 512], F32)
            self.tmp = [T(P, "ntmp%d" % b, [128, 512], F32) for b in range(2)]

    def sumsq_rstd(xa, xks, n, ns):
        pt, pk = P.ps()
        for dt in range(8):
            sq = ns.sq[dt % 2]
            act(sq[:, :n], xa(dt), AF.Square, xks, [sq.k])
            mm(pt[:, :n], ones_bf.a, sq[:, :n], dt == 0, dt == 7, [ones_bf.k, sq.k], pk, signal=(dt == 7))
        act(ns.rstd[:, :n], pt[:, :n], AF.Ln, [pk, epsc.k], [ns.rstd.k], scale=1.0 / 1024.0, bias=epsc.a)
        act(ns.rstd[:, :n], ns.rstd[:, :n], AF.Exp, [ns.rstd.k], [ns.rstd.k], scale=-0.5)

    def modnorm(xa, xks, l, kS, kSH, c, outa, outk, n, ns):
        sumsq_rstd(xa, xks, n, ns)
        for dt in range(8):
            tb = ns.tmp[dt % 2]
            stt(tb[:, :n], xa(dt), modcol(l, kS, dt, c), ns.rstd[:, :n], ALU.mult, ALU.mult, xks + [mod_all.k, ns.rstd.k], [tb.k])
            act(outa(dt), tb[:, :n], AF.Identity, [tb.k, mod_all.k], [outk], bias=modcol(l, kSH, dt, c))

    def post_residual(ya, yks, l, kGG, c, xa, xks, n, ns):
        sumsq_rstd(ya, yks, n, ns)
        for dt in range(8):
            tb = ns.tmp[dt % 2]
            stt(tb[:, :n], ya(dt), modcol(l, kGG, dt, c), ns.rstd[:, :n], ALU.mult, ALU.mult, yks + [mod_all.k, ns.rstd.k], [tb.k])
            tt("pool", xa(dt), xa(dt), tb[:, :n], ALU.add, xks + [tb.k], xks)

    def load_region(rg):
        ri = 0 if rg["name"] == "p" else 1
        src3 = rg["xsrc"].rearrange("(c p) n -> p c n", p=128)
        for blk in range(2):
            dst = xres[:, :, rg["off"] + blk * 512: rg["off"] + (blk + 1) * 512]
            load(dst, xk[ri][blk], src3[:, :, blk * 512:(blk + 1) * 512])
        if rg["pos"] is not None:
            with P.scope():
                pos3 = rg["pos"].rearrange("(c p) n -> p c n", p=128)
                for blk in range(2):
                    pb = T(P, "posb", [128, 8, 512], F32)
                    load(pb.a, pb.k, pos3[:, :, blk * 512:(blk + 1) * 512])
                    dst = xres[:, :, rg["off"] + blk * 512: rg["off"] + (blk + 1) * 512]
                    tt("pool", dst, dst, pb.a, ALU.add, [xk[ri][blk], pb.k], [xk[ri][blk]])

    def s5_section(rname, nseq, n, hT, hk, get_win, gT, mode, init, fin, spans=((0, 512), (512, 512))):
        full = (mode == "full")
        with P.scope():
            NB = 2
            if full:
                XFs = [[T(P, "XF%d" % r_, [128, 4, nseq, n + 1], BF16) for r_ in range(2)] for _ in range(NB)]
                XBs = [[T(P, "XB%d" % r_, [128, 4, nseq, n + 1], BF16) for r_ in range(2)] for _ in range(NB)]
                ycws = [T(P, "ycw", [128, 4096], BF16) for _ in range(NB)]
                ktws = [T(P, "ktw", [128, 1920], BF16) for _ in range(NB)]
                for bb in range(NB):
                    for r_ in range(2):
                        op("pool", lambda e, t=XFs[bb][r_]: e.memset(t.a, 0.0), writes=[XFs[bb][r_].k])
                        op("pool", lambda e, t=XBs[bb][r_]: e.memset(t.a, 0.0), writes=[XBs[bb][r_].k])
            bvws = [T(P, "bvw", [128, 4096], BF16) for _ in range(NB)]
            ews = [T(P, "ew", [128, 2, 1024], F32) for _ in range(NB)]
            dws = [T(P, "dw", [128, 1024], F32) for _ in range(NB)]
            ty = 0 if nseq > 1 else 1
            gsm = [T(P, "gsm%d" % b, [128, 8], F32) for b in range(2)]
            zis = [T(P, "zi", [128, 1024], BF16) for _ in range(NB)]
            tbs = [[T(P, "s5t%d" % b, [128, 1024], F32) for b in range(4)]] * NB
            Gbs = [[T(P, "s5g%d" % b, [128, 1024], F32) for b in range(2)]] * NB
            fsm = [T(P, "fsm%d" % b, [128, 4], F32) for b in range(2)]
            wb_next = get_win(0)
            for i in range(6):
                bb = i % NB
                bvw, ew, zi, tb_, Gb = bvws[bb], ews[bb], zis[bb], tbs[bb], Gbs[bb]
                dw = dws[bb]
                dma("sp", lambda e, dw=dw, i=i: e.dma_start(out=dw.a, in_=D["scr_d"][i, ty]),
                    reads=[scr_tok[("dt0", i, ty)], scr_tok[("dt1", i, ty)]], writes=[dw.k])
                if full:
                    XF, XB, ycw, ktw = XFs[bb], XBs[bb], ycws[bb], ktws[bb]
                loadr(bvw.a, bvw.k, D["scr_bv"][i], scr_tok[("bv", i)])
                if full:
                    loadr(ycw.a, ycw.k, D["scr_yc"][i], scr_tok[("yc", i)])
                    loadr(ktw.a, ktw.k, D["scr_kt"][i], scr_tok[("kt", i)])
                loadr(ew[:, 0, :], ew.k, D["scr_e"][i, :, 0, :], scr_tok[("ec", i)])
                loadr(ew[:, 1, :], ew.k, D["scr_e"][i, :, 1, :], scr_tok[("es", i)])
                wb = wb_next
                for blk in range(2):
                    pt, pk = P.ps()
                    for kt in range(8):
                        mm(pt[:, :], wb[:, kt, :], hT[:, kt, blk * 512:(blk + 1) * 512], kt == 0, kt == 7, [wb.k, hk[blk]], pk, signal=(kt == 7))
                    act(zi[:, blk * 512:(blk + 1) * 512], pt[:, :], AF.Identity, [pk], [zi.k])
                if i < 5:
                    wb_next = get_win(i + 1)
                if i == 0:
                    dbg_out("d_z0_" + rname, zi.a, zi.k)
                zv = zi.a.rearrange("p (c s) -> p c s", s=8)
                bv5 = bvw.a.rearrange("p (d s r m) -> p d s r m", d=2, s=8, r=2)
                Vps = [P.ps() for _ in range(4)]
                for r_ in range(2):
                    for d in range(2):
                        col = (r_ * 2 + d) * 128
                        for s in range(8):
                            for q in range(4):
                                pt, pk = Vps[q]
                                rows = slice(32 * q, 32 * q + 32)
                                mm(pt[:, col:col + 128], bv5[rows, d, s, r_, :], zv[rows, :, s], s == 0, s == 7, [bvw.k, zi.k], pk,
                                   signal=(s == 7 and r_ == 1 and d == 1), tp=(32 * q, 0))
                t1, t2, t3, t4 = tb_
                dq = lambda ap: ap.rearrange("p (d q c) -> p d q c", d=2, q=4)
                for q in range(4):
                    hs = slice(q * 256, (q + 1) * 256)
                    vp, vk = Vps[q]
                    v2 = lambda ap: ap.rearrange("p (d c) -> p d c", d=2)
                    vr, vi = v2(vp[:, 0:256]), v2(vp[:, 256:512])
                    ec_, es_ = v2(ew[:, 0, hs]), v2(ew[:, 1, hs])
                    o1, o2, o3, o4 = dq(t1.a)[:, :, q, :], dq(t2.a)[:, :, q, :], dq(t3.a)[:, :, q, :], dq(t4.a)[:, :, q, :]
                    tt("dve", o1, vr, ec_, ALU.mult, [vk, ew.k], [t1.k])
                    tt("dve", o4, vi, es_, ALU.mult, [vk, ew.k], [t4.k])
                    tt("dve", o3, vr, es_, ALU.mult, [vk, ew.k], [t3.k])
                    tt("dve", o2, vi, ec_, ALU.mult, [vk, ew.k], [t2.k])
                    tt("pool", o1, o1, o4, ALU.add, [t1.k, t4.k], [t1.k])
                    tt("pool", o2, o2, o3, ALU.subtract, [t2.k, t3.k], [t2.k])
                for r_, (src_, dst) in enumerate(((t1, Gb[0]), (t2, Gb[1]))):
                    if init is not None:
                        gi_ = init["G"][r_]
                        tt("dve", gsm[r_].a, gi_[:, 8 * i:8 * i + 8], lv[:, 0, 8 * i:8 * i + 8], ALU.mult, [gi_.k, lv.k], [gsm[r_].k])
                        gq = gsm[r_].a.rearrange("p (q d) -> p q d", d=2)
                        tt("dve", dq(src_.a)[:, 0, :, 0], dq(src_.a)[:, 0, :, 0], gq[:, :, 0], ALU.add, [src_.k, gsm[r_].k], [src_.k])
                        tt("dve", dq(src_.a)[:, 1, :, 127], dq(src_.a)[:, 1, :, 127], gq[:, :, 1], ALU.add, [src_.k, gsm[r_].k], [src_.k])
                    for d in range(2):
                        hs = slice(d * 512, (d + 1) * 512)
                        sa, da, de = src_[:, hs], dst[:, hs], dw[:, hs]
                        if d == 1:
                            sa, da, de = sa[:, ::-1], da[:, ::-1], de[:, ::-1]
                        op("dve", lambda e, da=da, sa=sa, de=de: e.tensor_tensor_scan(out=da, data0=de, data1=sa, initial=0.0, op0=ALU.mult, op1=ALU.add),
                           reads=[src_.k, dw.k], writes=[dst.k])
                g0, g1 = Gb
                if not full:
                    Gv = lambda ap: ap.rearrange("p (cb c) -> p cb c", c=128)
                    Fre, Fim = fin
                    fv = lambda t_: t_[:, 8 * i:8 * i + 8].rearrange("p (q d) -> p q d", d=2)
                    gre_f, gim_f = dq(g0.a)[:, 0, :, n - 1], dq(g1.a)[:, 0, :, n - 1]
                    ec_f, es_f = Gv(ew[:, 0, :])[:, 0::2, n - 1], Gv(ew[:, 1, :])[:, 0::2, n - 1]
                    fa, fb = fsm
                    tt("dve", fa.a, gre_f, ec_f, ALU.mult, [g0.k, ew.k], [fa.k])
                    tt("dve", fb.a, gim_f, es_f, ALU.mult, [g1.k, ew.k], [fb.k])
                    tt("dve", fv(Fre)[:, :, 0], fa.a, fb.a, ALU.subtract, [fa.k, fb.k], [Fre.k])
                    tt("dve", fa.a, gre_f, es_f, ALU.mult, [g0.k, ew.k], [fa.k])
                    tt("dve", fb.a, gim_f, ec_f, ALU.mult, [g1.k, ew.k], [fb.k])
                    tt("dve", fv(Fim)[:, :, 0], fa.a, fb.a, ALU.add, [fa.k, fb.k], [Fim.k])
                    cp("dve", fv(Fre)[:, :, 1], dq(g0.a)[:, 1, :, 0], [g0.k], [Fre.k])
                    cp("dve", fv(Fim)[:, :, 1], dq(g1.a)[:, 1, :, 0], [g1.k], [Fim.k])
                    continue
                qd = lambda ap: ap.rearrange("p (q d c) -> p q d c", q=4, d=2)
                gp = lambda ap: ap.rearrange("p (d q c) -> p q d c", d=2, q=4)
                g2 = tbs[0][0] if False else None
                tt("dve", qd(t1.a), gp(g0.a), qd(ew[:, 0, :]), ALU.mult, [g0.k, ew.k], [t1.k])
                tt("dve", qd(t2.a), gp(g1.a), qd(ew[:, 1, :]), ALU.mult, [g1.k, ew.k], [t2.k])
                tt("dve", qd(t3.a), gp(g0.a), qd(ew[:, 1, :]), ALU.mult, [g0.k, ew.k], [t3.k])
                tt("dve", gp(g0.a), gp(g1.a), qd(ew[:, 0, :]), ALU.mult, [g1.k, ew.k], [g0.k])
                v5 = lambda t: (t.a.rearrange("p (d q b n) -> p q d b n", q=4, d=2, b=nseq) if t is g0
                                else t.a.rearrange("p (q d b n) -> p q d b n", q=4, d=2, b=nseq))
                tt("pool", XF[0][:, :, :, 1:n + 1], v5(t1)[:, :, 0], v5(t2)[:, :, 0], ALU.subtract, [t1.k, t2.k], [XF[0].k])
                tt("pool", XB[0][:, :, :, 0:n], v5(t1)[:, :, 1], v5(t2)[:, :, 1], ALU.subtract, [t1.k, t2.k], [XB[0].k])
                tt("pool", XF[1][:, :, :, 1:n + 1], v5(t3)[:, :, 0], v5(g0)[:, :, 0], ALU.add, [t3.k, g0.k], [XF[1].k])
                tt("pool", XB[1][:, :, :, 0:n], v5(t3)[:, :, 1], v5(g0)[:, :, 1], ALU.add, [t3.k, g0.k], [XB[1].k])
                if init is not None:
                    for r_ in range(2):
                        sv_ = init["S"][r_][:, 8 * i:8 * i + 8].rearrange("p (q d) -> p q d", d=2)
                        cp("dve", XF[r_][:, :, 0, 0], sv_[:, :, 0], [init["S"][r_].k], [XF[r_].k])
                        cp("dve", XB[r_][:, :, 0, n], sv_[:, :, 1], [init["S"][r_].k], [XB[r_].k])
                if rname == "p":
                    so = lambda t_: t_.a.rearrange("p (i q d b) -> p i q d b", i=6, q=4, d=2)
                    tt("dve", so(stout[0])[:, i, :, 0, :], v5(t1)[:, :, 0, :, n - 1], v5(t2)[:, :, 0, :, n - 1], ALU.subtract, [t1.k, t2.k], [stout[0].k])
                    tt("dve", so(stout[0])[:, i, :, 1, :], v5(t1)[:, :, 1, :, 0], v5(t2)[:, :, 1, :, 0], ALU.subtract, [t1.k, t2.k], [stout[0].k])
                    tt("dve", so(stout[1])[:, i, :, 0, :], v5(t3)[:, :, 0, :, n - 1], v5(g0)[:, :, 0, :, n - 1], ALU.add, [t3.k, g0.k], [stout[1].k])
                    tt("dve", so(stout[1])[:, i, :, 1, :], v5(t3)[:, :, 1, :, 0], v5(g0)[:, :, 1, :, 0], ALU.add, [t3.k, g0.k], [stout[1].k])
                kt3 = ktw.a.rearrange("p (s m) -> p s m", s=15)
                yc6 = ycw.a.rearrange("p (q d s r m) -> p q d s r m", q=4, d=2, s=8, r=2)
                for (st, nt) in spans:
                    pt, pk = P.ps()
                    ncs = nt // 8
                    nb = max(1, ncs // n)
                    cpb = ncs // nb
                    zb = zi[:, st:st + nt]
                    zbv = zb.rearrange("p (c s) -> p c s", s=8)
                    pv = pt[:, :nt].rearrange("p (c s) -> p c s", s=8)
                    mm(pt[:, :nt], kt3[:, 0, :], zb, True, False, [ktw.k, zi.k], pk, signal=False)
                    for dl in range(1, 8):
                        mm(pv[:, :, dl:8], kt3[:, dl, :], zbv[:, :, 0:8 - dl], False, False, [ktw.k, zi.k], pk, signal=False)
                        mm(pv[:, :, 0:8 - dl], kt3[:, 7 + dl, :], zbv[:, :, dl:8], False, False, [ktw.k, zi.k], pk, signal=False)
                    pv4 = pt[:, :nt].rearrange("p (b c s) -> p b c s", b=nb, s=8)
                    cnt_ = 0
                    for d in range(2):
                        for s in range(8):
                            for r_ in range(2):
                                for q in range(4):
                                    rows = slice(32 * q, 32 * q + 32)
                                    Xt = (XF if d == 0 else XB)[r_]
                                    if nseq > 1:
                                        b0 = (st // 8) // n
                                        c0 = 0 if d == 0 else 1
                                        rhs = Xt[:, q, b0:b0 + nb, c0:c0 + cpb]
                                    else:
                                        c0 = st // 8 + (0 if d == 0 else 1)
                                        rhs = Xt[:, q, 0:1, c0:c0 + ncs]
                                    cnt_ += 1
                                    mm(pv4[rows, :, :, s], yc6[:, q, d, s, r_, :], rhs, False, cnt_ == 128, [ycw.k, Xt.k], pk,
                                       signal=(cnt_ == 128), tp=(0, 32 * q))
                    act(gT[:, i, st:st + nt], pt[:, :nt], AF.Gelu_apprx_tanh, [pk], [gT.k])

    def glu_wout_post(rg, gT, catT, spans=((0, 512), (512, 512))):
        off, c = rg["off"], rg["cond"]
        with P.scope():
            wg = T(P, "wglu", [128, 6, 768], BF16)
            wload_bf16(wg.a, wg.k, D["w_glu"].rearrange("(kt p) m -> p kt m", p=128))
            wo = T(P, "wout", [128, 8, 1024], BF16)
            wload_bf16(wo.a, wo.k, D["w_out0"].rearrange("(kt p) m -> p kt m", p=128))
            sg = [T(P, "sg%d" % b, [128, 512], F32) for b in range(2)]
            yf = T(P, "yf", [128, 8, 512], F32)
            ns = NormScratch()
            for (st, nt) in spans:
                bs_ = slice(st, st + nt)
                for mt in range(6):
                    pt, pk = P.ps()
                    for kt in range(6):
                        mm(pt[:, :nt], wg[:, kt, mt * 128:(mt + 1) * 128], gT[:, kt, bs_], kt == 0, kt == 5, [wg.k, gT.k], pk, signal=(kt == 5))
                    sb_ = sg[mt % 2]
                    act(sb_[:, :nt], pt[:, :nt], AF.Sigmoid, [pk, cmisc.k], [sb_.k], bias=bglu(mt))
                    tt("dve", catT[:, mt, bs_], gT[:, mt, bs_], sb_[:, :nt], ALU.mult, [gT.k, sb_.k], [catT.k])
            dbg_out("d_cat_" + rg["name"], catT.a.rearrange("p a b -> p (a b)"), catT.k)
            for (st, nt) in spans:
                bs_ = slice(st, st + nt)
                for mt in range(8):
                    pt, pk = P.ps()
                    for kt in range(8):
                        mm(pt[:, :nt], wo[:, kt, mt * 128:(mt + 1) * 128], catT[:, kt, bs_], kt == 0, kt == 7, [wo.k, catT.k], pk, signal=(kt == 7))
                    act(yf[:, mt, :nt], pt[:, :nt], AF.Identity, [pk], [yf.k])
                xs_ = slice(off + st, off + st + nt)
                post_residual(lambda dt: yf[:, dt, :nt], [yf.k], 0, 2, c, lambda dt: xres[:, dt, xs_], [xk[0][st // 512]], nt, ns)


    def l0_mixer(rg):
        ri = 0 if rg["name"] == "p" else 1
        off, c, nseq, n = rg["off"], rg["cond"], rg["nseq"], rg["n"]
        with P.scope():
            hT = T(P, "hT", [128, 8, 1024], BF16)
            hk = [Tok(), Tok()]
            catT = T(P, "catT", [128, 8, 1024], BF16)
            gT = T(P, "gT", [128, 6, 1024], BF16)
            winb = [T(P, "winb%d" % b, [128, 8, 128], BF16) for b in range(2)]
            nwin = [0]
            win_src = D["w_in0"].rearrange("(kt p) m -> p kt m", p=128)

            def get_win(col_tile):
                wb = winb[nwin[0] % 2]
                nwin[0] += 1
                wload_bf16(wb.a, wb.k, win_src[:, :, col_tile * 128:(col_tile + 1) * 128])
                return wb

            with P.scope():
                ns = NormScratch()
                for blk in range(2):
                    sl = slice(off + blk * 512, off + (blk + 1) * 512)
                    modnorm(lambda dt: xres[:, dt, sl], [xk[ri][blk]], 0, 0, 1, c,
                            lambda dt: hT[:, dt, blk * 512:(blk + 1) * 512], hk[blk], 512, ns)
            dbg_out("d_hT_" + rg["name"], hT.a.rearrange("p a b -> p (a b)"), hk[1])

            ckpt(2.2)
            with P.scope():
                fcf = T(P, "fcf", [128, 4, 128], F32)
                load(fcf.a, fcf.k, D["fconst"])
                fcb = T(P, "fcb", [128, 4, 128], BF16)
                cp("dve", fcb.a, fcf.a, [fcf.k], [fcb.k])
                ctf = T(P, "ctf", [128, 2, 2, 256], F32)
                load(ctf.a, ctf.k, D["ctp"])
                ctb = T(P, "ctb", [128, 2, 2, 256], BF16)
                cp("dve", ctb.a, ctf.a, [ctf.k], [ctb.k])
                zfT = T(P, "zfT", [128, 2, 1024], BF16)
                PQ = T(P, "PQ", [128, 8, 512], BF16)
                ZT = T(P, "ZT", [128, 2, 1024], BF16)
                for jt in range(2):
                    wb = get_win(6 + jt)
                    for blk in range(2):
                        pt, pk = P.ps()
                        for kt in range(8):
                            mm(pt[:, :], wb[:, kt, :], hT[:, kt, blk * 512:(blk + 1) * 512], kt == 0, kt == 7, [wb.k, hk[blk]], pk, signal=(kt == 7))
                        act(zfT[:, jt, blk * 512:(blk + 1) * 512], pt[:, :], AF.Identity, [pk], [zfT.k])
                for t8 in range(8):
                    pt, pk = P.ps()
                    tsl = slice(t8 * 128, (t8 + 1) * 128)
                    for cs in range(2):
                        for jt in range(2):
                            col = cs * 256 + jt * 128
                            mm(pt[:, col:col + 128], zfT[:, jt, tsl], fcb[:, cs, :], True, True, [zfT.k, fcb.k], pk, signal=(cs == 1 and jt == 1))
                    cp("dve", PQ[:, t8, :], pt[:, :], [pk], [PQ.k])
                if rg["name"] == "p":
                    for jt in range(2):
                        for blk in range(2):
                            pt, pk = P.ps()
                            for b2 in range(2):
                                kk = 0
                                for ttl in range(2):
                                    t8 = (blk * 2 + b2) * 2 + ttl
                                    for cs in range(2):
                                        mm(pt[:, b2 * 256:(b2 + 1) * 256], PQ[:, t8, cs * 256 + jt * 128: cs * 256 + (jt + 1) * 128], ctb[:, cs, ttl, :],
                                           kk == 0, kk == 3, [PQ.k, ctb.k], pk, signal=(kk == 3))
                                        kk += 1
                            act(ZT[:, jt, blk * 512:(blk + 1) * 512], pt[:, :], AF.Identity, [pk], [ZT.k])
                for jt in range(2):
                    for blk in range(2):
                        pt, pk = P.ps()
                        mm(pt[:, :], fcb[:, 2 + jt, :], ZT[:, jt, blk * 512:(blk + 1) * 512], True, True, [fcb.k, ZT.k], pk, signal=True)
                        act(catT[:, 6 + jt, blk * 512:(blk + 1) * 512], pt[:, :], AF.Identity, [pk, cmisc.k], [catT.k], bias=fnetb(jt))
            dbg_out("d_yb_" + rg["name"], catT[:, 6:8, :].rearrange("p a b -> p (a b)"), catT.k)

            ckpt(2.3)
            s5_section(rg["name"], nseq, n, hT, hk, get_win, gT, "full", None, None)
            dbg_out("d_g_" + rg["name"], gT.a.rearrange("p a b -> p (a b)"), gT.k)

            ckpt(2.6)
            glu_wout_post(rg, gT, catT)

    ckpt(2.01)
    load_region(REG[0])
    ckpt(2.05)
    l0_mixer(REG[0])
    if "d_x1_p" in dbg:
        store(D["d_x1_p"].rearrange("p (a b) -> p a b", a=8), xres[:, :, 0:1024], xk[0][1])
    for nm, t_ in (("st_re", stout[0]), ("st_im", stout[1])):
        store(D[nm].rearrange("p a b -> p (a b)"), t_.a, t_.k)
    ckpt(3.0)

    def ffn(l, rg, spans=((0, 512), (512, 512)), xbuf=None, xtoks=None):
        off, c = rg["off"], rg["cond"]
        xb_ = xres if xbuf is None else xbuf
        w1src = D["w_ff1_%d" % l].rearrange("(kt p) j -> p kt j", p=128)
        w2src = D["w_ff2_%d" % l].rearrange("(jt p) m -> p jt m", p=128)
        with P.scope():
            ns = NormScratch()
            h2 = T(P, "h2", [128, 8, 512], BF16)
            hid = T(P, "hid", [128, 32, 512], BF16)
            w1s = [T(P, "w1s%d" % b, [128, 8, 512], BF16) for b in range(4)]
            w2s = [T(P, "w2s%d" % b, [128, 32, 128], BF16) for b in range(4)]
            rl = [T(P, "rl%d" % b, [128, 512], F32) for b in range(2)]
            yf = T(P, "yff", [128, 8, 512], F32)
            for (st, nt) in spans:
                xs_ = slice(off + st, off + st + nt)
                xkk = [xk[0][st // 512]] if xtoks is None else xtoks
                modnorm(lambda dt: xb_[:, dt, xs_], xkk, l, 3, 4, c, lambda dt: h2[:, dt, :nt], h2.k, nt, ns)
                for jg in range(8):
                    wb = w1s[jg % 4]
                    wload_bf16(wb.a, wb.k, w1src[:, :, jg * 512:(jg + 1) * 512])
                    for j4 in range(4):
                        jt = jg * 4 + j4
                        pt, pk = P.ps()
                        for kt in range(8):
                            mm(pt[:, :nt], wb[:, kt, j4 * 128:(j4 + 1) * 128], h2[:, kt, :nt], kt == 0, kt == 7, [wb.k, h2.k], pk, signal=(kt == 7))
                        rb = rl[jt % 2]
                        act(rb[:, :nt], pt[:, :nt], AF.Relu, [pk], [rb.k])
                        tt("dve", hid[:, jt, :nt], rb[:, :nt], rb[:, :nt], ALU.mult, [rb.k], [hid.k])
                for mt in range(8):
                    wb = w2s[mt % 4]
                    wload_bf16(wb.a, wb.k, w2src[:, :, mt * 128:(mt + 1) * 128])
                    pt, pk = P.ps()
                    for jt in range(32):
                        mm(pt[:, :nt], wb[:, jt, :], hid[:, jt, :nt], jt == 0, jt == 31, [wb.k, hid.k], pk, signal=(jt == 31))
                    act(yf[:, mt, :nt], pt[:, :nt], AF.Identity, [pk], [yf.k])
                post_residual(lambda dt: yf[:, dt, :nt], [yf.k], l, 5, c, lambda dt: xb_[:, dt, xs_], xkk, nt, ns)

    ffn(0, REG[0])
    if "d_x2_p" in dbg:
        store(D["d_x2_p"].rearrange("p (a b) -> p a b", a=8), xres[:, :, 0:1024], xk[0][1])
    ckpt(4.0)

    def l1_mixer(rg, Uall=None, tile_base=0, tps_all=None, own=False):
        ri = 0 if rg["name"] == "p" else 1
        off, c, nseq = rg["off"], rg["cond"], rg["nseq"]
        w1 = D["w_in1"].rearrange("(kt p) m -> p kt m", p=128)
        with P.scope():
            cat1 = T(P, "cat1", [128, 8, 1024], BF16)
            with P.scope():
                hT = T(P, "h1T", [128, 8, 1024], BF16)
                hk = [Tok(), Tok()]
                with P.scope():
                    ns = NormScratch()
                    for blk in range(2):
                        sl = slice(off + blk * 512, off + (blk + 1) * 512)
                        modnorm(lambda dt: xres[:, dt, sl], [xk[ri][blk]], 1, 0, 1, c,
                                lambda dt: hT[:, dt, blk * 512:(blk + 1) * 512], hk[blk], 512, ns)
                wg = T(P, "w1g", [128, 8, 512], BF16); wv = T(P, "w1v", [128, 8, 512], BF16)
                if Uall is None:
                    wu = T(P, "w1u", [128, 8, 512], BF16)
                    wload_bf16(wu.a, wu.k, w1[:, :, 0:512])
                wload_bf16(wg.a, wg.k, w1[:, :, 512:1024])
                wload_bf16(wv.a, wv.k, w1[:, :, 1024:1536])
                pmf = T(P, "pmf", [128, 10, 4, 128], F32)
                load(pmf.a, pmf.k, D["pmat"])
                pmb = T(P, "pmb", [128, 10, 4, 128], BF16)
                cp("dve", pmb.a, pmf.a, [pmf.k], [pmb.k])
                wst = T(P, "wst", [128, 4, 128], BF16)
                wload_bf16(wst.a, wst.k, D["wsT"])
                pwb = T(P, "pwb", [128, 4, 128], BF16)
                wload_bf16(pwb.a, pwb.k, D["pool_w"].rearrange("g c d -> c g d"))
                lnb = T(P, "lnb", [128, 3, 512], F32)
                for k3 in range(3):
                    load(lnb[:, k3, :], lnb.k, D["lnv"][k3].partition_broadcast(128))
                Utm = T(P, "Utm", [128, 8, 512], BF16) if Uall is None else None
                vn = T(P, "vn", [128, 8, 512], BF16)
                uT = T(P, "uT", [128, 4, 1024], BF16)
                pT = T(P, "pT", [128, 4, 1024], BF16)
                gv = [T(P, "gv%d" % b, [128, 512], F32) for b in range(2)]
                st6 = T(P, "st6", [128, 4, 6], F32)
                mv = T(P, "mv", [128, 4, 2], F32)
                rs4 = T(P, "rs4", [128, 4], F32)
                for t8 in range(8):
                    tsl = slice(t8 * 128, (t8 + 1) * 128)
                    hkk = hk[t8 // 4]
                    if Uall is None:
                        pt, pk = P.ps()
                        for kt in range(8):
                            mm(pt[:, :], hT[:, kt, tsl], wu[:, kt, :], kt == 0, kt == 7, [hkk, wu.k], pk, signal=(kt == 7))
                        cp("dve", Utm[:, t8, :], pt[:, :], [pk], [Utm.k])
                    pt, pk = P.ps()
                    for kt in range(8):
                        mm(pt[:, :], hT[:, kt, tsl], wv[:, kt, :], kt == 0, kt == 7, [hkk, wv.k], pk, signal=(kt == 7))
                    g_ = gv[t8 % 2]
                    act(g_.a, pt[:, :], AF.Gelu_apprx_tanh, [pk], [g_.k])
                    for h in range(4):
                        op("dve", lambda e, g_=g_, h=h: e.bn_stats(out=st6[:, h, :], in_=g_[:, h * 128:(h + 1) * 128]), reads=[g_.k], writes=[st6.k])
                    for h in range(4):
                        op("dve", lambda e, h=h: e.bn_aggr(out=mv[:, h, :], in_=st6[:, h, :]), reads=[st6.k], writes=[mv.k])
                    act(rs4.a, mv[:, :, 1], AF.Sqrt, [mv.k, epsc.k], [rs4.k], bias=epsc.a)
                    recip(rs4.a, rs4.a, [rs4.k], [rs4.k])
                    for h in range(4):
                        hs = slice(h * 128, (h + 1) * 128)
                        ts("dve", g_[:, hs], g_[:, hs], mv[:, h, 0:1], ALU.subtract, [g_.k, mv.k, rs4.k], [g_.k], s2=rs4[:, h:h + 1], op1=ALU.mult)
                    tt("pool", g_.a, g_.a, lnb[:, 0, :], ALU.mult, [g_.k, lnb.k], [g_.k])
                    tt("pool", vn[:, t8, :], g_.a, lnb[:, 1, :], ALU.add, [g_.k, lnb.k], [vn.k])
                for h in range(4):
                    for blk in range(2):
                        pt, pk = P.ps()
                        for kt in range(8):
                            mm(pt[:, :], wg[:, kt, h * 128:(h + 1) * 128], hT[:, kt, blk * 512:(blk + 1) * 512], kt == 0, kt == 7, [wg.k, hk[blk]], pk, signal=(kt == 7))
                        act(uT[:, h, blk * 512:(blk + 1) * 512], pt[:, :], AF.Gelu_apprx_tanh, [pk], [uT.k])
                tps = (8 // nseq) if tps_all is None else tps_all
                Usrc = Utm if Uall is None else Uall
                for g in range(4):
                    for half in range(2):
                        pt, pk = P.ps()
                        for t4 in range(4):
                            t8 = half * 4 + t4
                            tg = tile_base + t8
                            tl = tg % tps
                            terms = []
                            if own:
                                if tg == 0:
                                    terms = [(31, 7), (0, 5), (1, 1)]
                                elif tg == 7:
                                    terms = [(6, 0), (7, 6), (8, 8)]
                                else:
                                    terms = [(tg - 1, 0), (tg, 2), (tg + 1, 1)]
                            else:
                                if tl > 0:
                                    terms.append((tg - 1, 0))
                                cur = 3 if tl == 0 else (4 if tl == tps - 1 else 2)
                                terms.append((tg, cur))
                                if tl < tps - 1:
                                    terms.append((tg + 1, 1))
                            for kk, (tsrc, slot) in enumerate(terms):
                                mm(pt[:, t4 * 128:(t4 + 1) * 128], Usrc[:, tsrc, g * 128:(g + 1) * 128], pmb[:, slot, g, :], kk == 0, kk == len(terms) - 1,
                                   [Usrc.k, pmb.k], pk, signal=(kk == len(terms) - 1))
                        cp("dve", pT[:, g, half * 512:(half + 1) * 512], pt[:, :], [pk], [pT.k])
                    for half in range(2):
                        pt, pk = P.ps()
                        mm(pt[:, :], pwb[:, g, :], pT[:, g, half * 512:(half + 1) * 512], True, True, [pwb.k, pT.k], pk, signal=True)
                        act(cat1[:, g, half * 512:(half + 1) * 512], pt[:, :], AF.Identity, [pk, cmisc.k], [cat1.k], scale=poolsc(g))
                for t8 in range(8):
                    tsl = slice(t8 * 128, (t8 + 1) * 128)
                    pt, pk = P.ps()
                    for h in range(4):
                        mm(pt[:, h * 128:(h + 1) * 128], vn[:, t8, h * 128:(h + 1) * 128], wst[:, h, :], True, True, [vn.k, wst.k], pk, signal=(h == 3))
                    g_ = gv[t8 % 2]
                    tt("dve", g_.a, pt[:, :], lnb[:, 2, :], ALU.add, [pk, lnb.k], [g_.k])
                    tt("dve", cat1[:, 4:8, tsl], g_.a.rearrange("p (h q) -> p h q", h=4), uT[:, :, tsl], ALU.mult, [g_.k, uT.k], [cat1.k])
            dbg_out("d_cat1_" + rg["name"], cat1.a.rearrange("p a b -> p (a b)"), cat1.k)
            with P.scope():
                wo = T(P, "wout1", [128, 8, 1024], BF16)
                wload_bf16(wo.a, wo.k, D["w_out1"].rearrange("(kt p) m -> p kt m", p=128))
                yf = T(P, "yf1", [128, 8, 512], F32)
                ns = NormScratch()
                for blk in range(2):
                    bs_ = slice(blk * 512, (blk + 1) * 512)
                    for mt in range(8):
                        pt, pk = P.ps()
                        for kt in range(8):
                            mm(pt[:, :], wo[:, kt, mt * 128:(mt + 1) * 128], cat1[:, kt, bs_], kt == 0, kt == 7, [wo.k, cat1.k], pk, signal=(kt == 7))
                        act(yf[:, mt, :], pt[:, :], AF.Identity, [pk], [yf.k])
                    xs_ = slice(off + blk * 512, off + (blk + 1) * 512)
                    post_residual(lambda dt: yf[:, dt, :], [yf.k], 1, 2, c, lambda dt: xres[:, dt, xs_], [xk[ri][blk]], 512, ns)

    l1_mixer(REG[0])
    if "d_x3_p" in dbg:
        store(D["d_x3_p"].rearrange("p (a b) -> p a b", a=8), xres[:, :, 0:1024], xk[0][1])
    ckpt(5.0)
    ffn(1, REG[0])
    yo = D["ypT"].rearrange("(c p) n -> p c n", p=128)
    for blk in range(2):
        dma("sp", lambda e, blk=blk: e.dma_start(out=yo[:, :, blk * 512:(blk + 1) * 512], in_=xres[:, :, blk * 512:(blk + 1) * 512]),
            reads=[xk[0][blk]], writes=[Tok()])
    ckpt(6.0)

    xsd = D["scr_x"].rearrange("(c p) n -> p c n", p=128)
    xdk = [[Tok(), Tok()] for _ in range(4)]
    SREG = [dict(name="s%d" % q, off=0, cond=1, nseq=1, n=128, q=q) for q in range(4)]

    def xs_load(q):
        for blk in range(2):
            loadr(xres[:, :, blk * 512:(blk + 1) * 512], xk[0][blk], xsd[:, :, q * 1024 + blk * 512: q * 1024 + (blk + 1) * 512], xdk[q][blk])

    def xs_store(q, dst3=None):
        d3 = xsd if dst3 is None else dst3
        for blk in range(2):
            dma("sp", lambda e, blk=blk: e.dma_start(out=d3[:, :, q * 1024 + blk * 512: q * 1024 + (blk + 1) * 512], in_=xres[:, :, blk * 512:(blk + 1) * 512]),
                reads=[xk[0][blk]], writes=[xdk[q][blk]])

    def make_get_win(winb):
        nwin = [0]
        win_src = D["w_in0"].rearrange("(kt p) m -> p kt m", p=128)

        def get_win(col_tile):
            wb = winb[nwin[0] % 2]
            nwin[0] += 1
            wload_bf16(wb.a, wb.k, win_src[:, :, col_tile * 128:(col_tile + 1) * 128])
            return wb
        return get_win

    with P.scope():
        ybT = T(P, "ybTall", [128, 2, 2048], BF16)
        Ffin = [[T(P, "Ff%d_%d" % (q, r_), [128, 48], F32) for r_ in range(2)] for q in range(4)]
        Sin = [[T(P, "Si%d_%d" % (q, r_), [128, 48], F32) for r_ in range(2)] for q in range(4)]
        Gin = [[T(P, "Gi%d_%d" % (q, r_), [128, 48], F32) for r_ in range(2)] for q in range(4)]
        fcb = T(P, "fcb_s", [128, 4, 128], BF16)
        with P.scope():
            fcf = T(P, "fcf_s", [128, 4, 128], F32)
            load(fcf.a, fcf.k, D["fconst"])
            cp("dve", fcb.a, fcf.a, [fcf.k], [fcb.k])
        xs3 = D["xsT"].rearrange("(c p) n -> p c n", p=128)
        pos3 = D["posT"].rearrange("(c p) n -> p c n", p=128)
        pqscope = P.scope()
        pqscope.__enter__()
        PQall = T(P, "PQall", [128, 32, 512], BF16)
        for q in range(4):
            with P.scope():
                with P.scope():
                    for blk in range(2):
                        cs_ = slice(q * 1024 + blk * 512, q * 1024 + (blk + 1) * 512)
                        load(xres[:, :, blk * 512:(blk + 1) * 512], xk[0][blk], xs3[:, :, cs_])
                        pb = T(P, "posb", [128, 8, 512], F32)
                        load(pb.a, pb.k, pos3[:, :, cs_])
                        dst = xres[:, :, blk * 512:(blk + 1) * 512]
                        tt("dve" if blk == 0 else "pool", dst, dst, pb.a, ALU.add, [xk[0][blk], pb.k], [xk[0][blk]])
                xs_store(q)
                hT = T(P, "hTa", [128, 8, 1024], BF16)
                hk = [Tok(), Tok()]
                winb = [T(P, "winba%d" % b, [128, 8, 128], BF16) for b in range(2)]
                get_win = make_get_win(winb)
                with P.scope():
                    ns = NormScratch()
                    for blk in range(2):
                        sl = slice(blk * 512, (blk + 1) * 512)
                        modnorm(lambda dt: xres[:, dt, sl], [xk[0][blk]], 0, 0, 1, 1, lambda dt: hT[:, dt, sl], hk[blk], 512, ns)
                with P.scope():
                    zfT = T(P, "zfTa", [128, 2, 1024], BF16)
                    for jt in range(2):
                        wb = get_win(6 + jt)
                        for blk in range(2):
                            pt, pk = P.ps()
                            for kt in range(8):
                                mm(pt[:, :], wb[:, kt, :], hT[:, kt, blk * 512:(blk + 1) * 512], kt == 0, kt == 7, [wb.k, hk[blk]], pk, signal=(kt == 7))
                            act(zfT[:, jt, blk * 512:(blk + 1) * 512], pt[:, :], AF.Identity, [pk], [zfT.k])
                    for t8 in range(8):
                        pt, pk = P.ps()
                        tsl = slice(t8 * 128, (t8 + 1) * 128)
                        for cs in range(2):
                            for jt in range(2):
                                col = cs * 256 + jt * 128
                                mm(pt[:, col:col + 128], zfT[:, jt, tsl], fcb[:, cs, :], True, True, [zfT.k, fcb.k], pk, signal=(cs == 1 and jt == 1))
                        cp("dve", PQall[:, 8 * q + t8, :], pt[:, :], [pk], [PQall.k])
                s5_section("s", 1, 128, hT, hk, get_win, None, "finals", None, (Ffin[q][0], Ffin[q][1]))
        with P.scope():
            ZT = T(P, "ZTs", [128, 2, 2048], BF16)
            slab = [T(P, "cts%d" % b, [128, 8, 2, 512], BF16) for b in range(2)]
            nsl = 0
            YB = {0: 0, 1: 1, 2: 2, 3: 7}
            for kb in range(4):
                (p0, k0), (p1, k1) = P.ps(), P.ps()
                for qq in range(4):
                    sb_ = slab[nsl % 2]
                    nsl += 1
                    load(sb_.a, sb_.k, D["cts"][:, 8 * qq:8 * qq + 8, :, kb * 512:(kb + 1) * 512])
                    for t8 in range(8):
                        for cs in range(2):
                            first = (qq == 0 and t8 == 0 and cs == 0)
                            last = (qq == 3 and t8 == 7 and cs == 1)
                            for jt, (pp, kk) in enumerate(((p0, k0), (p1, k1))):
                                mm(pp[:, :], PQall[:, 8 * qq + t8, cs * 256 + jt * 128: cs * 256 + (jt + 1) * 128], sb_[:, t8, cs, :], first, last,
                                   [PQall.k, sb_.k], kk, signal=last)
                act(ZT[:, 0, kb * 512:(kb + 1) * 512], p0[:, :], AF.Identity, [k0], [ZT.k])
                act(ZT[:, 1, kb * 512:(kb + 1) * 512], p1[:, :], AF.Identity, [k1], [ZT.k])
            for jt in range(2):
                for kb in range(4):
                    pt, pk = P.ps()
                    mm(pt[:, :], fcb[:, 2 + jt, :], ZT[:, jt, kb * 512:(kb + 1) * 512], True, True, [fcb.k, ZT.k], pk, signal=True)
                    act(ybT[:, jt, kb * 512:(kb + 1) * 512], pt[:, :], AF.Identity, [pk, cmisc.k], [ybT.k], bias=fnetb(jt))
        pqscope.__exit__(None, None, None)
        with P.scope():
            h0 = T(P, "h0s", [128, 2, 48], F32)
            load(h0.a, h0.k, D["h0s"])
            Are = T(P, "Are", [128, 48], F32); Aim = T(P, "Aim", [128, 48], F32)
            tt("dve", Are.a, lv[:, 5, :], lv[:, 3, :], ALU.mult, [lv.k], [Are.k])
            tt("dve", Aim.a, lv[:, 5, :], lv[:, 4, :], ALU.mult, [lv.k], [Aim.k])
            Rc = T(P, "Rc", [128, 48], F32); Rs = T(P, "Rs", [128, 48], F32)
            ev = lambda ap, d: ap.rearrange("p (a d) -> p a d", d=2)[:, :, d]
            cp("dve", ev(Rc.a, 0), ev(lv[:, 1, :], 0), [lv.k], [Rc.k])
            cp("dve", ev(Rs.a, 0), ev(lv[:, 2, :], 0), [lv.k], [Rs.k])
            cp("dve", ev(Rc.a, 1), ev(lv[:, 3, :], 1), [lv.k], [Rc.k])
            cp("dve", ev(Rs.a, 1), ev(lv[:, 4, :], 1), [lv.k], [Rs.k])
            ca = T(P, "ca", [128, 48], F32); cb_ = T(P, "cb", [128, 48], F32)
            Tr = T(P, "Tr", [128, 48], F32); Ti = T(P, "Ti", [128, 48], F32)
            op("pool", lambda e: e.memset(Tr.a, 0.0), writes=[Tr.k])
            op("pool", lambda e: e.memset(Ti.a, 0.0), writes=[Ti.k])
            for d, visits, mbase in ((0, [0, 1, 2, 3, 0, 1, 2], 0), (1, [3, 2, 1, 0, 3, 2, 1], 4)):
                for k in visits:
                    mcol = meta[:, mbase + k:mbase + k + 1]
                    for r_, Tt in ((0, Tr), (1, Ti)):
                        tt("dve", ev(ca.a, d), ev(h0[:, r_, :], d), ev(Tt.a, d), ALU.subtract, [h0.k, Tt.k], [ca.k])
                        stt(ev(Sin[k][r_].a, d), ev(ca.a, d), mcol, ev(Tt.a, d), ALU.mult, ALU.add, [ca.k, meta.k, Tt.k], [Sin[k][r_].k])
                    sr, si = Sin[k]
                    tt("dve", ev(ca.a, d), ev(Are.a, d), ev(sr.a, d), ALU.mult, [Are.k, sr.k], [ca.k])
                    tt("dve", ev(cb_.a, d), ev(Aim.a, d), ev(si.a, d), ALU.mult, [Aim.k, si.k], [cb_.k])
                    tt("dve", ev(ca.a, d), ev(ca.a, d), ev(cb_.a, d), ALU.subtract, [ca.k, cb_.k], [ca.k])
                    tt("dve", ev(Tr.a, d), ev(ca.a, d), ev(Ffin[k][0].a, d), ALU.add, [ca.k, Ffin[k][0].k], [Tr.k])
                    tt("dve", ev(ca.a, d), ev(Are.a, d), ev(si.a, d), ALU.mult, [Are.k, si.k], [ca.k])
                    tt("dve", ev(cb_.a, d), ev(Aim.a, d), ev(sr.a, d), ALU.mult, [Aim.k, sr.k], [cb_.k])
                    tt("dve", ev(ca.a, d), ev(ca.a, d), ev(cb_.a, d), ALU.add, [ca.k, cb_.k], [ca.k])
                    tt("dve", ev(Ti.a, d), ev(ca.a, d), ev(Ffin[k][1].a, d), ALU.add, [ca.k, Ffin[k][1].k], [Ti.k])
            for q in range(4):
                sr, si = Sin[q]
                tt("dve", ca.a, sr.a, Rc.a, ALU.mult, [sr.k, Rc.k], [ca.k])
                tt("dve", cb_.a, si.a, Rs.a, ALU.mult, [si.k, Rs.k], [cb_.k])
                tt("dve", Gin[q][0].a, ca.a, cb_.a, ALU.subtract, [ca.k, cb_.k], [Gin[q][0].k])
                tt("dve", ca.a, sr.a, Rs.a, ALU.mult, [sr.k, Rs.k], [ca.k])
                tt("dve", cb_.a, si.a, Rc.a, ALU.mult, [si.k, Rc.k], [cb_.k])
                tt("dve", Gin[q][1].a, ca.a, cb_.a, ALU.add, [ca.k, cb_.k], [Gin[q][1].k])
        uk = [Tok() for _ in range(32)]
        SPANS = {0: ((0, 512), (512, 512)), 1: ((0, 128),), 3: ((896, 128),)}
        UT8 = {0: list(range(8)), 1: [0], 3: [7]}
        xh = T(P, "xh", [128, 8, 256], F32)
        HCOL = {1: 0, 3: 128}

        def u_tiles(xb_, xtoks_of, spans, tiles):
            with P.scope():
                hT1 = T(P, "hT1u", [128, 8, 1024], BF16)
                hk1 = [Tok(), Tok()]
                wu = T(P, "w1uu", [128, 8, 512], BF16)
                ustg = [T(P, "ustg%d" % b, [128, 512], BF16) for b in range(2)]
                wload_bf16(wu.a, wu.k, D["w_in1"].rearrange("(kt p) m -> p kt m", p=128)[:, :, 0:512])
                with P.scope():
                    ns = NormScratch()
                    for (st, nt) in spans:
                        sl = slice(st, st + nt)
                        modnorm(lambda dt: xb_[:, dt, sl], xtoks_of(st), 1, 0, 1, 1, lambda dt: hT1[:, dt, sl], hk1[st // 512], nt, ns)
                for n_, (col, tg) in enumerate(tiles):
                    pt, pk = P.ps()
                    for kt in range(8):
                        mm(pt[:, :], hT1[:, kt, col:col + 128], wu[:, kt, :], kt == 0, kt == 7, [hk1[col // 512], wu.k], pk, signal=(kt == 7))
                    ust = ustg[n_ % 2]
                    cp("dve", ust.a, pt[:, :], [pk], [ust.k])
                    dma("sp", lambda e, ust=ust, tg=tg: e.dma_start(out=D["scr_u"][:, tg, :], in_=ust.a), reads=[ust.k], writes=[uk[tg]])

        for q in (1, 3, 0):
            rg = SREG[q]
            spans = SPANS[q]
            xs_load(q)
            with P.scope():
                hT = T(P, "hTb", [128, 8, 1024], BF16)
                hk = [Tok(), Tok()]
                catT = T(P, "catTb", [128, 8, 1024], BF16)
                gT = T(P, "gTb", [128, 6, 1024], BF16)
                winb = [T(P, "winbb%d" % b, [128, 8, 128], BF16) for b in range(2)]
                get_win = make_get_win(winb)
                with P.scope():
                    ns = NormScratch()
                    for blk in range(2):
                        sl = slice(blk * 512, (blk + 1) * 512)
                        modnorm(lambda dt: xres[:, dt, sl], [xk[0][blk]], 0, 0, 1, 1, lambda dt: hT[:, dt, sl], hk[blk], 512, ns)
                s5_section("s", 1, 128, hT, hk, get_win, gT, "full", dict(S=Sin[q], G=Gin[q]), None, spans=spans)
                YOFF = {0: 0, 1: 1024, 3: 1536 - 512}
                for (st, nt) in spans:
                    cp("pool", catT[:, 6:8, st:st + nt], ybT[:, :, YOFF[q] + st:YOFF[q] + st + nt], [ybT.k], [catT.k])
                glu_wout_post(rg, gT, catT, spans=spans)
            if q != 0:
                (st, nt), = spans
                cp("pool", xh[:, :, HCOL[q]:HCOL[q] + 128], xres[:, :, st:st + nt], [xk[0][st // 512]], [xh.k])
                continue
            ffn(0, SREG[1], spans=((0, 256),), xbuf=xh, xtoks=[xh.k])
            u_tiles(xh, lambda st: [xh.k], ((0, 256),), [(0, 8), (128, 31)])
            ffn(0, rg, spans=spans)
            u_tiles(xres, lambda st: [xk[0][st // 512]], spans, [(128 * t8, t8) for t8 in range(8)])
            xs_store(q)
    with P.scope():
        yso = D["ysT"].rearrange("(c p) n -> p c n", p=128)
        Uall = T(P, "Uall", [128, 32, 512], BF16)
        for tg in list(range(9)) + [31]:
            loadr(Uall[:, tg, :], Uall.k, D["scr_u"][:, tg, :], uk[tg])
        rg = SREG[0]
        xs_load(0)
        l1_mixer(rg, Uall=Uall, tile_base=0, tps_all=32, own=True)
        ffn(1, rg)
        xs_store(0, dst3=yso)


_POOL_WINDOWS = (2, 4, 8, 16)


def _fm(v):
    v = np.asarray(v, np.float32)
    return np.ascontiguousarray(v.reshape(-1, 128).T)


def _pos_embed_T():
    rows = 4096 // 64
    rr, cc = np.meshgrid(np.arange(rows, dtype=np.float32), np.arange(64, dtype=np.float32), indexing="ij")
    quarter = 256
    omega = (1.0 / (np.float32(10000.0) ** (np.arange(quarter, dtype=np.float32) / np.float32(quarter)))).astype(np.float32)

    def ax(p):
        ang = p.reshape(-1)[:, None].astype(np.float32) * omega[None, :]
        return np.concatenate([np.sin(ang), np.cos(ang)], axis=-1)
    pe = np.concatenate([ax(rr), ax(cc)], axis=-1).astype(np.float32)
    return np.ascontiguousarray(pe.T)


def _s5_layouts(inp):
    lre = np.asarray(inp["l0_s5_lambda_re"], np.float32)
    lim = np.asarray(inp["l0_s5_lambda_im"], np.float32)
    ldt = np.asarray(inp["l0_s5_log_dt"], np.float32)
    bre = np.asarray(inp["l0_s5_b_re"], np.float32)
    bim = np.asarray(inp["l0_s5_b_im"], np.float32)
    cre = np.asarray(inp["l0_s5_c_re"], np.float32)
    cim = np.asarray(inp["l0_s5_c_im"], np.float32)

    def g6(a):
        return a.reshape((2, 6, 4, 2) + a.shape[2:])

    def lc_state(a):
        x = np.transpose(g6(a), (3, 4, 1, 2, 0))
        return np.repeat(x[..., None], 16, axis=-1)
    ldt3 = np.repeat(ldt[:, :, None], 64, axis=2)
    lc = [lc_state(lre), lc_state(lim), lc_state(ldt3)]
    c6 = lambda a: np.transpose(g6(a), (3, 5, 1, 2, 0, 4))
    b6 = lambda a: np.transpose(g6(a), (3, 4, 1, 2, 0, 5))
    lc += [c6(cre), c6(cim), b6(bre), b6(bim)]
    s5C = np.stack([x.reshape(128, 768) for x in lc], axis=1).astype(np.float32)

    def lb_state(a):
        x = np.transpose(g6(a), (2, 3, 1, 0, 4))
        return np.repeat(x[:, :, None], 16, axis=2)
    lb = [lb_state(lre), lb_state(lim), lb_state(ldt3)]
    bb = lambda a: np.transpose(g6(a), (2, 3, 5, 1, 0, 4))
    lb += [bb(bre), bb(bim)]
    s5B = np.stack([x.reshape(128, 768) for x in lb], axis=1).astype(np.float32)
    return np.ascontiguousarray(s5C), np.ascontiguousarray(s5B)


def _lv_layout(a):
    x = np.asarray(a, np.float32).reshape(2, 6, 4, 2, 64)
    return np.ascontiguousarray(np.transpose(x, (3, 4, 1, 2, 0)).reshape(128, 48))


def _band(w, kind):
    h = w // 2
    M = np.zeros((128, 128), np.float64)
    for t in range(128):
        lo, hi = t - h, t + h
        cnt = float(w)
        if kind == "first":
            cnt = float(hi - max(lo, 0))
        if kind == "last":
            cnt = float(min(hi, 128) - lo)
        if kind in ("mid", "first", "last"):
            for tp in range(max(lo, 0), min(hi, 128)):
                M[tp, t] += 1.0 / cnt
            M[t, t] -= 1.0
        elif kind == "prev":
            for tp in range(128):
                if lo <= tp - 128 < hi:
                    M[tp, t] += 1.0 / cnt
        elif kind == "next":
            for tp in range(128):
                if lo <= tp + 128 < hi:
                    M[tp, t] += 1.0 / cnt
    return M.astype(np.float32)


_ALL_INPUTS = (
    "x_prompt", "x_sample", "state_l0_s5_re", "state_l0_s5_im", "c", "c_ctx",
    "l0_w_mod", "l0_b_mod", "l0_g_mix_pre", "l0_g_mix_post", "l0_g_ff_pre", "l0_g_ff_post", "l0_w_ff1", "l0_w_ff2",
    "l0_w_in", "l0_w_out", "l0_s5_lambda_re", "l0_s5_lambda_im", "l0_s5_log_dt", "l0_s5_b_re", "l0_s5_b_im",
    "l0_s5_c_re", "l0_s5_c_im", "l0_s5_d", "l0_s5_w_glu", "l0_s5_b_glu", "l0_fnet_w", "l0_fnet_b",
    "l1_w_mod", "l1_b_mod", "l1_g_mix_pre", "l1_g_mix_post", "l1_g_ff_pre", "l1_g_ff_post", "l1_w_ff1", "l1_w_ff2",
    "l1_w_in", "l1_w_out", "l1_pool_w", "l1_pool_scale", "l1_gmlp_ln_g", "l1_gmlp_ln_b", "l1_gmlp_ws", "l1_gmlp_bs",
)


def _host_inputs(inp):
    for _n in _ALL_INPUTS:
        assert _n in inp, _n
    f32 = lambda a: np.ascontiguousarray(np.asarray(a, np.float32))
    shared = {}
    for l in range(2):
        shared["w_mod%d" % l] = f32(inp["l%d_w_mod" % l])
        shared["w_ff1_%d" % l] = f32(inp["l%d_w_ff1" % l])
        shared["w_ff2_%d" % l] = f32(inp["l%d_w_ff2" % l])
        shared["w_out%d" % l] = f32(inp["l%d_w_out" % l])
    shared["w_in0"] = f32(inp["l0_w_in"])
    shared["w_in1"] = f32(inp["l1_w_in"])
    shared["w_glu"] = f32(inp["l0_s5_w_glu"])
    shared["pool_w"] = f32(inp["l1_pool_w"])
    shared["wsT"] = np.ascontiguousarray(np.transpose(np.asarray(inp["l1_gmlp_ws"], np.float32), (2, 0, 1)))
    shared["lnv"] = np.ascontiguousarray(np.stack([np.asarray(inp["l1_gmlp_ln_g"], np.float32), np.asarray(inp["l1_gmlp_ln_b"], np.float32),
                                                   np.asarray(inp["l1_gmlp_bs"], np.float32).reshape(512)], axis=0))
    shared["bmod"] = np.concatenate([_fm(inp["l%d_b_mod" % l]) for l in range(2)], axis=1)
    gl = []
    for l in range(2):
        for k in ("g_mix_pre", "g_mix_post", "g_ff_pre", "g_ff_post"):
            gl.append(_fm(inp["l%d_%s" % (l, k)]))
    shared["gains"] = np.concatenate(gl, axis=1)
    s5C, s5B = _s5_layouts(inp)
    shared["s5C"], shared["s5B"] = s5C, s5B
    cm = np.zeros((128, 64), np.float32)
    pidx = np.arange(128)
    jj = (pidx // 16) % 2
    jj_c = pidx // 64
    cm[:, 0] = (jj == 0); cm[:, 1] = (jj == 1)
    for q in range(4):
        cm[:, 2 + q] = (pidx // 32 == q)
    cm[:, 6:12] = _fm(inp["l0_s5_d"])
    cm[:, 12:18] = _fm(inp["l0_s5_b_glu"])
    cm[:, 18:20] = _fm(np.asarray(inp["l0_fnet_b"]).reshape(-1))
    cm[:, 20:24] = _fm(inp["l1_pool_scale"])
    cm[:, 24] = (jj_c == 0); cm[:, 25] = (jj_c == 1)
    cm[:, 26] = -1.0 * (jj_c == 0); cm[:, 27] = -1.0 * (jj_c == 1)
    shared["cmisc"] = cm
    shared["ident"] = np.eye(128, dtype=np.float32)
    dm = np.ones((128, 2, 2, 128), np.float32)
    cidx = np.arange(128)
    dm[:, 0, 0, cidx % 32 == 0] = 0.0
    dm[:, 0, 1, cidx % 32 == 31] = 0.0
    dm[:, 1, 0, 0] = 0.0
    dm[:, 1, 1, 127] = 0.0
    shared["dmask"] = dm
    fc = np.zeros((128, 4, 128), np.float32)
    cc = np.arange(64)
    ang = 2 * np.pi * np.outer(cc, cc) / 64.0
    C64 = (np.cos(ang) / 8.0).astype(np.float32); S64 = (np.sin(ang) / 8.0).astype(np.float32)
    fw = np.asarray(inp["l0_fnet_w"], np.float32)
    for g2 in range(2):
        sl = slice(64 * g2, 64 * g2 + 64)
        fc[sl, 0, sl] = C64
        fc[sl, 1, sl] = S64
        fc[sl, 2, sl] = fw[g2]
        fc[sl, 3, sl] = fw[2 + g2]
    shared["fconst"] = fc
    tt_ = np.arange(256)
    angp = 2 * np.pi * np.outer(tt_, tt_) / 256.0
    ctp = np.stack([np.cos(angp) / 16.0, -np.sin(angp) / 16.0], axis=0).astype(np.float32)
    shared["ctp"] = np.ascontiguousarray(np.transpose(ctp.reshape(2, 2, 128, 256), (2, 0, 1, 3)))
    bands = {k: [_band(w, k) for w in _POOL_WINDOWS] for k in ("prev", "next", "mid", "first", "last")}
    posT = _pos_embed_T()
    base = np.arange(4096, dtype=np.float64) * (2 * np.pi / 4096.0)
    ctab = (np.cos(base) / 64.0).astype(np.float32)
    stab = (-np.sin(base) / 64.0).astype(np.float32)
    ctab_bf = ctab.astype(ml_dtypes.bfloat16)
    stab_bf = stab.astype(ml_dtypes.bfloat16)
    xp = np.asarray(inp["x_prompt"], np.float32)
    xs = np.asarray(inp["x_sample"], np.float32)
    cctx = np.asarray(inp["c_ctx"], np.float32)
    cs_ = np.asarray(inp["c"], np.float32)
    sre = np.asarray(inp["state_l0_s5_re"], np.float32)
    sim = np.asarray(inp["state_l0_s5_im"], np.float32)
    maps = []
    for core in range(NCORES):
        s, j = core // 4, core % 4
        m = dict(shared)
        m["xpT"] = np.ascontiguousarray(xp[4 * core:4 * core + 4].reshape(1024, 1024).T)
        order = np.concatenate([np.arange(1024 * ((j + k) % 4), 1024 * ((j + k) % 4) + 1024) for k in range(4)])
        m["xsT"] = np.ascontiguousarray(xs[s][order].T)
        m["posT"] = np.ascontiguousarray(posT[:, order])
        kord = np.concatenate([order[0:1536], order[3584:4096]])
        idx = (order[:, None].astype(np.int64) * kord[None, :].astype(np.int64)) % 4096
        cts = np.empty((128, 32, 2, 2048), ml_dtypes.bfloat16)
        cts[:, :, 0, :] = np.transpose(ctab_bf[idx].reshape(32, 128, 2048), (1, 0, 2))
        cts[:, :, 1, :] = np.transpose(stab_bf[idx].reshape(32, 128, 2048), (1, 0, 2))
        m["cts"] = cts
        cond = np.stack([cctx, cs_[s]], axis=-1)
        m["cond"] = np.ascontiguousarray(np.transpose(cond.reshape(8, 128, 2), (1, 0, 2)).reshape(128, 16))
        m["h0s"] = np.ascontiguousarray(np.stack([_lv_layout(sre[s]), _lv_layout(sim[s])], axis=1))
        meta = np.zeros((128, 16), np.float32)
        for k in range(4):
            meta[:, k] = 1.0 if (j + k) % 4 == 0 else 0.0
            meta[:, 4 + k] = 1.0 if (j + k) % 4 == 3 else 0.0
        m["meta"] = meta
        pm = np.zeros((128, 10, 4, 128), np.float32)
        for gi in range(4):
            pm[:, 0, gi] = bands["prev"][gi]
            pm[:, 1, gi] = bands["next"][gi]
            pm[:, 2, gi] = bands["mid"][gi]
            pm[:, 3, gi] = bands["first"][gi]
            pm[:, 4, gi] = bands["last"][gi]
            pm[:, 5, gi] = bands["first" if j == 0 else "mid"][gi]
            pm[:, 6, gi] = bands["last" if j == 3 else "mid"][gi]
            if j > 0:
                pm[:, 7, gi] = bands["prev"][gi]
            if j < 3:
                pm[:, 8, gi] = bands["next"][gi]
        m["pmat"] = pm
        maps.append(m)
    return maps


_NC_CACHE = {}


def _get_nc(stage=9, dbg=()):
    key = (stage, repr(dbg))
    if key not in _NC_CACHE:
        _NC_CACHE[key] = build(stage, dbg)
    return _NC_CACHE[key]


def _run(inp, stage=9, dbg=()):
    nc = _get_nc(stage, dbg)
    maps = _host_inputs(inp)
    res = run_bass_kernel_spmd(nc, maps, core_ids=list(range(NCORES)))
    return res.results


def kernel(**inp):
    res = _run(inp)
    yp = np.zeros((32, 256, 1024), np.float32)
    ys = np.zeros((2, 4096, 1024), np.float32)
    nre = np.zeros((32, 2, 48, 64), np.float32)
    nim = np.zeros((32, 2, 48, 64), np.float32)
    for core in range(NCORES):
        r = res[core]
        s, j = core // 4, core % 4
        yp[4 * core:4 * core + 4] = np.asarray(r["ypT"]).T.reshape(4, 256, 1024)
        ys[s, 1024 * j:1024 * j + 1024] = np.asarray(r["ysT"]).T
        for name, dst in (("st_re", nre), ("st_im", nim)):
            x = np.asarray(r[name]).reshape(2, 64, 6, 4, 2, 4)
            x = np.transpose(x, (5, 4, 2, 3, 0, 1))
            dst[4 * core:4 * core + 4] = x.reshape(4, 2, 48, 64)
    return (yp, ys, nre, nim)
```

```python
import math
from contextlib import ExitStack, contextmanager
import numpy as np
import ml_dtypes
import concourse.bass as bass
import concourse.mybir as mybir
from concourse.bass_utils import run_bass_kernel_spmd

F32 = mybir.dt.float32
BF16 = mybir.dt.bfloat16
ALU = mybir.AluOpType
AF = mybir.ActivationFunctionType

SAME_ENGINE_SYNC = True
EPS = 1e-6
NCORES = 8


class Tok:
    __slots__ = ("w", "r")

    def __init__(self):
        self.w = None
        self.r = []


class Prog:
    ENGS = ("pe", "act", "dve", "pool", "sp")
    NDMA = {"sp": 24, "pool": 12, "act": 4}

    def __init__(self, nc, es):
        self.nc = nc
        self.scopes = [es]
        self.streams = {e: [] for e in self.ENGS}
        self.cnt = {e: 0 for e in self.ENGS}
        self.sems = {}
        for e in self.ENGS:
            self.sems[e] = es.enter_context(nc.semaphore("s_" + e))
        self.sems["bar"] = es.enter_context(nc.semaphore("s_bar"))
        self.barcnt = 0
        self.dval = {}
        self.dnext = {}
        for q, n in self.NDMA.items():
            self.dnext[q] = 0
            for k in range(n):
                key = "d_%s_%d" % (q, k)
                self.sems[key] = es.enter_context(nc.semaphore(key))
                self.dval[key] = 0
        self.waited = {}
        self.ninstr = 0
        self.uid = 0
        self.psum = []
        self.psum_next = 0
        self.last_unsig = None

    def sb(self, name, shape, dt):
        self.uid += 1
        return self.scopes[-1].enter_context(self.nc.sbuf_tensor("%s_%d" % (name, self.uid), list(shape), dt))

    def init_psum(self):
        for k in range(8):
            t = self.scopes[0].enter_context(self.nc.psum_tensor("psb%d" % k, [128, 512], F32))
            self.psum.append((t, Tok()))

    def ps(self):
        k = self.psum_next
        self.psum_next = (k + 1) % 8
        return self.psum[k]

    @contextmanager
    def scope(self):
        es = ExitStack()
        self.scopes.append(es)
        try:
            yield
        finally:
            self.barrier()
            self.scopes.pop()
            es.close()

    def _need(self, eng, ev, waits):
        if ev is None:
            return
        key, val = ev
        if key == "pe" and eng != "pe" and val > self.cnt["pe"]:
            idx = self.last_unsig
            assert idx is not None and val == self.cnt["pe"] + 1
            ent = self.streams["pe"][idx]
            assert ent[0] == "op" and ent[2] is None
            self.cnt["pe"] += 1
            self.streams["pe"][idx] = ("op", ent[1], self.sems["pe"], 1)
            self.last_unsig = None
        if key == eng:
            if eng in ("pe", "sp"):
                return
            if not SAME_ENGINE_SYNC:
                return
            if val > self.cnt[eng]:
                return
        if val > waits.get(key, 0):
            waits[key] = val

    def _emit_waits(self, eng, waits):
        for key, val in waits.items():
            if self.waited.get((eng, key), 0) >= val:
                continue
            self.waited[(eng, key)] = val
            sem = self.sems[key]
            self.streams[eng].append(("wait", sem, val))

    def _deps(self, eng, reads, writes):
        waits = {}
        for t in reads:
            self._need(eng, t.w, waits)
        for t in writes:
            self._need(eng, t.w, waits)
            for ev in t.r:
                self._need(eng, ev, waits)
        return waits

    def _commit(self, ev, reads, writes):
        for t in writes:
            t.w = ev
            t.r = []
        for t in reads:
            if t not in writes:
                t.r.append(ev)
                if len(t.r) > 48:
                    best = {}
                    for k, v in t.r:
                        if v > best.get(k, 0):
                            best[k] = v
                    t.r = list(best.items())

    def op(self, eng, fn, reads=(), writes=(), signal=True):
        waits = self._deps(eng, reads, writes)
        self._emit_waits(eng, waits)
        self.ninstr += 1
        if signal:
            self.cnt[eng] += 1
            ev = (eng, self.cnt[eng])
            self.streams[eng].append(("op", fn, self.sems[eng], 1))
            if eng == "pe":
                self.last_unsig = None
        else:
            assert eng == "pe"
            ev = (eng, self.cnt[eng] + 1)
            self.streams[eng].append(("op", fn, None, 0))
            self.last_unsig = len(self.streams[eng]) - 1
        self._commit(ev, reads, writes)
        return ev

    def dma(self, q, fn, reads=(), writes=()):
        waits = self._deps(q, reads, writes)
        n = self.NDMA[q]
        k = self.dnext[q]
        self.dnext[q] = (k + 1) % n
        key = "d_%s_%d" % (q, k)
        if self.dval[key] > waits.get(key, 0):
            waits[key] = self.dval[key]
        self._emit_waits(q, waits)
        self.dval[key] += 16
        ev = (key, self.dval[key])
        sem = self.sems[key]
        self.ninstr += 1
        self.streams[q].append(("op", fn, sem, 16))
        self._commit(ev, reads, writes)
        return ev

    def barrier(self):
        assert self.last_unsig is None, "barrier with unsignalled PE work pending"
        for e in ("pe", "act", "dve", "pool"):
            if self.cnt[e] > self.waited.get(("sp", e), 0):
                self.waited[("sp", e)] = self.cnt[e]
                self.streams["sp"].append(("wait", self.sems[e], self.cnt[e]))
        for key, val in self.dval.items():
            if val > self.waited.get(("sp", key), 0):
                self.waited[("sp", key)] = val
                self.streams["sp"].append(("wait", self.sems[key], val))
        self.barcnt += 1
        bs = self.sems["bar"]
        self.streams["sp"].append(("seminc", bs, 1))
        for e in ("pe", "act", "dve", "pool"):
            self.streams[e].append(("wait", bs, self.barcnt))
            for e2 in ("pe", "act", "dve", "pool"):
                self.waited[(e, e2)] = max(self.waited.get((e, e2), 0), self.cnt[e2])
            for key, val in self.dval.items():
                self.waited[(e, key)] = max(self.waited.get((e, key), 0), val)

    def finish(self):
        self.barrier()
        nc = self.nc
        streams = self.streams

        def run(e, lst):
            for ent in lst:
                if ent[0] == "op":
                    ins = ent[1](e)
                    if ent[2] is not None:
                        ins.then_inc(ent[2], ent[3])
                elif ent[0] == "wait":
                    e.wait_ge(ent[1], ent[2])
                else:
                    e.sem_inc(ent[1], ent[2])

        with nc.Block() as block:
            @block.sync
            def _(e):
                run(e, streams["sp"])

            @block.tensor
            def _(e):
                run(e, streams["pe"])

            @block.scalar
            def _(e):
                run(e, streams["act"])

            @block.vector
            def _(e):
                run(e, streams["dve"])

            @block.gpsimd
            def _(e):
                run(e, streams["pool"])


class _Stop(Exception):
    pass


class T:
    def __init__(self, P, name, shape, dt):
        self.t = P.sb(name, shape, dt)
        self.k = Tok()
        self.shape = list(shape)

    def __getitem__(self, idx):
        return self.t[idx]

    @property
    def a(self):
        return self.t[:]


def build(stage=9, dbg=()):
    nc = bass.Bass("TRN2", target_bir_lowering=False)
    D = {}

    def din(name, shape, dt=F32):
        D[name] = nc.dram_tensor(name, list(shape), dt, kind="ExternalInput").ap()
        return D[name]

    def dout(name, shape, dt=F32):
        D[name] = nc.dram_tensor(name, list(shape), dt, kind="ExternalOutput").ap()
        return D[name]

    def dscr(name, shape, dt=F32):
        D[name] = nc.dram_tensor(name, list(shape), dt, kind="Internal").ap()
        return D[name]

    din("xpT", [1024, 1024])
    din("xsT", [1024, 4096])
    din("posT", [1024, 4096])
    din("cond", [128, 16])
    din("bmod", [128, 96])
    din("gains", [128, 64])
    for l in range(2):
        din("w_mod%d" % l, [1024, 6144])
        din("w_ff1_%d" % l, [1024, 4096])
        din("w_ff2_%d" % l, [4096, 1024])
        din("w_out%d" % l, [1024, 1024])
    din("w_in0", [1024, 1024])
    din("w_in1", [1024, 1536])
    din("w_glu", [768, 768])
    din("s5C", [128, 7, 768])
    din("s5B", [128, 5, 768])
    din("cmisc", [128, 64])
    din("ident", [128, 128])
    din("h0s", [128, 2, 48])
    din("fconst", [128, 4, 128])
    din("ctp", [128, 2, 2, 256])
    din("meta", [128, 16])
    din("dmask", [128, 2, 2, 128])
    din("cts", [128, 32, 2, 2048], BF16)
    din("pool_w", [4, 128, 128])
    din("pmat", [128, 10, 4, 128])
    din("wsT", [128, 4, 128])
    din("lnv", [3, 512])
    dout("ypT", [1024, 1024])
    dout("ysT", [1024, 1024])
    dout("st_re", [128, 6, 32])
    dout("st_im", [128, 6, 32])
    for ent in dbg:
        dout(ent[0], ent[1], BF16 if (len(ent) > 2 and ent[2] == 'bf16') else F32)
    dscr("scr_bv", [6, 128, 4096], BF16)
    dscr("scr_yc", [6, 128, 4096], BF16)
    dscr("scr_kt", [6, 128, 1920], BF16)
    dscr("scr_e", [6, 128, 2, 1024], F32)
    dscr("scr_x", [1024, 4096], F32)
    dscr("scr_d", [6, 2, 128, 1024], F32)
    dscr("scr_u", [128, 32, 512], BF16)

    with ExitStack() as es:
        P = Prog(nc, es)
        P.init_psum()
        try:
            _build_body(nc, P, D, stage, set(e[0] for e in dbg))
        except _Stop:
            pass
        P.finish()
    return nc


def _build_body(nc, P, D, stage, dbg):
    op, dma = P.op, P.dma

    def load(dst_ap, dst_tok, src_ap, q="sp"):
        return dma(q, lambda e: e.dma_start(out=dst_ap, in_=src_ap), writes=[dst_tok])

    scr_tok = {}

    def store(dst_ap, src_ap, src_tok, q="sp", key=None):
        tk = Tok()
        if key is not None:
            scr_tok[key] = tk
        return dma(q, lambda e: e.dma_start(out=dst_ap, in_=src_ap), reads=[src_tok], writes=[tk])

    def loadr(dst_ap, dst_tok, src_ap, src_tok, q="sp"):
        return dma(q, lambda e: e.dma_start(out=dst_ap, in_=src_ap), reads=[src_tok], writes=[dst_tok])

    def tt(eng, out, a, b, alu, reads, writes):
        return op(eng, lambda e: e.tensor_tensor(out=out, in0=a, in1=b, op=alu), reads=reads, writes=writes)

    def ts(eng, out, a, s1, op0, reads, writes, s2=None, op1=None):
        if op1 is None:
            return op(eng, lambda e: e.tensor_scalar(out=out, in0=a, scalar1=s1, scalar2=None, op0=op0), reads=reads, writes=writes)
        return op(eng, lambda e: e.tensor_scalar(out=out, in0=a, scalar1=s1, scalar2=s2, op0=op0, op1=op1), reads=reads, writes=writes)

    def stt(out, a, s, b, op0, op1, reads, writes):
        return op("dve", lambda e: e.scalar_tensor_tensor(out=out, in0=a, scalar=s, in1=b, op0=op0, op1=op1), reads=reads, writes=writes)

    def act(out, a, func, reads, writes, scale=None, bias=None):
        kw = {}
        if scale is not None:
            kw["scale"] = scale
        if bias is not None:
            kw["bias"] = bias
        return op("act", lambda e: e.activation(out=out, in_=a, func=func, **kw), reads=reads, writes=writes)

    def mm(out, lhsT, rhs, start, stop, reads, ptok, signal, tp=None):
        kw = {}
        if tp is not None:
            kw["tile_position"] = tp
        return op("pe", lambda e: e.matmul(out, lhsT, rhs, start=start, stop=stop, **kw), reads=reads, writes=[ptok], signal=signal)

    def cp(eng, out, in_, reads, writes):
        return op(eng, lambda e: e.tensor_copy(out=out, in_=in_), reads=reads, writes=writes)

    def recip(out, in_, reads, writes):
        return op("dve", lambda e: e.reciprocal(out=out, in_=in_), reads=reads, writes=writes)

    def ckpt(x):
        if stage < x:
            raise _Stop()

    def dbg_out(name, src_ap, tok):
        if name in dbg:
            store(D[name], src_ap, tok)

    ones_bf = T(P, "ones", [128, 128], BF16)
    op("pool", lambda e: e.memset(ones_bf.a, 1.0), writes=[ones_bf.k])
    epsc = T(P, "epsc", [128, 1], F32)
    op("pool", lambda e: e.memset(epsc.a, EPS), writes=[epsc.k])
    halfpi = T(P, "halfpi", [128, 1], F32)
    op("pool", lambda e: e.memset(halfpi.a, math.pi / 2), writes=[halfpi.k])
    cmisc = T(P, "cmisc", [128, 64], F32)
    load(cmisc.a, cmisc.k, D["cmisc"])
    ident = T(P, "ident", [128, 128], F32)
    load(ident.a, ident.k, D["ident"])
    meta = T(P, "meta", [128, 16], F32)
    load(meta.a, meta.k, D["meta"])
    maskJB = lambda j: cmisc[:, j:j + 1]
    maskJC = lambda j: cmisc[:, 24 + j:25 + j]
    nmaskJC = lambda j: cmisc[:, 26 + j:27 + j]
    maskQ = lambda q: cmisc[:, 2 + q:3 + q]
    dcol = lambda i: cmisc[:, 6 + i:7 + i]
    bglu = lambda i: cmisc[:, 12 + i:13 + i]
    fnetb = lambda jt: cmisc[:, 18 + jt:19 + jt]
    poolsc = lambda g: cmisc[:, 20 + g:21 + g]

    mods = [dict(), dict()]
    mod_all = T(P, "mod_all", [128, 2, 6, 16], F32)
    with P.scope():
        cond = T(P, "cond", [128, 16], F32)
        load(cond.a, cond.k, D["cond"])
        scT = T(P, "scT", [128, 16], BF16)
        act(scT.a, cond.a, AF.Silu, [cond.k], [scT.k])
        bmod = T(P, "bmod", [128, 96], F32)
        load(bmod.a, bmod.k, D["bmod"])
        gains = T(P, "gains", [128, 64], F32)
        load(gains.a, gains.k, D["gains"])
        modv = T(P, "modv", [128, 2, 48, 2], F32)
        wslab = [T(P, "wslab%d" % b, [128, 8, 768], BF16) for b in range(3)]
        nslab = 0
        for l in range(2):
            wsrc = D["w_mod%d" % l].rearrange("(kt p) f -> p kt f", p=128)
            pt, pk = P.ps()
            for sl in range(8):
                wb = wslab[nslab % 3]
                nslab += 1
                dma("pool", lambda e, wb=wb, sl=sl, wsrc=wsrc: e.dma_start(out=wb.a, in_=wsrc[:, :, sl * 768:(sl + 1) * 768]), writes=[wb.k])
                for f6 in range(6):
                    ft = sl * 6 + f6
                    for kt in range(8):
                        mm(pt[:, 2 * ft:2 * ft + 2], wb[:, kt, f6 * 128:(f6 + 1) * 128], scT[:, 2 * kt:2 * kt + 2],
                           kt == 0, kt == 7, [wb.k, scT.k], pk, signal=(kt == 7))
            pv = pt[:, 0:96].rearrange("p (f c) -> p f c", c=2)
            for c in range(2):
                tt("dve", modv[:, l, :, c], pv[:, :, c], bmod[:, l * 48:(l + 1) * 48], ALU.add, [pk, bmod.k], [modv.k])
            gv = lambda kind: gains[:, (l * 4 + kind) * 8:(l * 4 + kind) * 8 + 8]
            mo = lambda m, c: modv[:, l, 8 * m:8 * m + 8, c]
            ma = lambda kind, c: mod_all[:, l, kind, :].rearrange("p (d c) -> p d c", c=2)[:, :, c]
            for c in range(2):
                stt(ma(0, c), mo(1, c), 1.0, gv(0), ALU.add, ALU.mult, [modv.k, gains.k], [mod_all.k])
                cp("dve", ma(1, c), mo(0, c), [modv.k], [mod_all.k])
                tt("dve", ma(2, c), mo(2, c), gv(1), ALU.mult, [modv.k, gains.k], [mod_all.k])
                stt(ma(3, c), mo(4, c), 1.0, gv(2), ALU.add, ALU.mult, [modv.k, gains.k], [mod_all.k])
                cp("dve", ma(4, c), mo(3, c), [modv.k], [mod_all.k])
                tt("dve", ma(5, c), mo(5, c), gv(3), ALU.mult, [modv.k, gains.k], [mod_all.k])
        dbg_out("d_mod", mod_all.a.rearrange("p a b c -> p (a b c)"), mod_all.k)

    def modcol(l, kind, dt, c):
        return mod_all[:, l, kind, 2 * dt + c:2 * dt + c + 1]

    if stage <= 0:
        return

    lv = T(P, "lv", [128, 16, 48], F32)
    NI = 3
    W = NI * 128
    cnt = [0]

    def tmp(name="t", w=None):
        cnt[0] += 1
        return T(P, "%s%d" % (name, cnt[0]), [128, w or W], F32)

    def a_chain(eng, lre, lim, ldt, ktok, npow, bre, bim):
        r = {}
        dt = tmp("dt")
        act(dt.a, ldt, AF.Exp, [ktok], [dt.k])
        lr = tmp("lr"); li = tmp("li")
        tt(eng, lr.a, lre, dt.a, ALU.mult, [ktok, dt.k], [lr.k])
        tt(eng, li.a, lim, dt.a, ALU.mult, [ktok, dt.k], [li.k])
        mag = tmp("mag")
        act(mag.a, lr.a, AF.Exp, [lr.k], [mag.k])
        c = tmp("c"); s = tmp("s")
        act(s.a, li.a, AF.Sin, [li.k], [s.k], scale=0.125)
        act(c.a, li.a, AF.Sin, [li.k], [c.k], scale=-0.125, bias=halfpi.a)
        t1 = tmp("t1"); t2 = tmp("t2")
        for _ in range(3):
            c2 = tmp("c"); s2 = tmp("s")
            tt(eng, t1.a, c.a, c.a, ALU.mult, [c.k], [t1.k])
            tt(eng, t2.a, s.a, s.a, ALU.mult, [s.k], [t2.k])
            tt(eng, s2.a, c.a, s.a, ALU.mult, [c.k, s.k], [s2.k])
            ts(eng, s2.a, s2.a, 2.0, ALU.mult, [s2.k], [s2.k])
            tt(eng, c2.a, t1.a, t2.a, ALU.subtract, [t1.k, t2.k], [c2.k])
            c, s = c2, s2
        r["unit"] = (c, s)
        r["lr"] = lr
        r["li"] = li
        abr = tmp("abr"); abi = tmp("abi")
        tt(eng, abr.a, mag.a, c.a, ALU.mult, [mag.k, c.k], [abr.k])
        tt(eng, abi.a, mag.a, s.a, ALU.mult, [mag.k, s.k], [abi.k])
        r["ab"] = (abr, abi)
        nr = tmp("nr")
        ts(eng, nr.a, abr.a, -1.0, ALU.add, [abr.k], [nr.k])
        den = tmp("den")
        tt(eng, t1.a, lre, lre, ALU.mult, [ktok], [t1.k])
        tt(eng, t2.a, lim, lim, ALU.mult, [ktok], [t2.k])
        tt(eng, den.a, t1.a, t2.a, ALU.add, [t1.k, t2.k], [den.k])
        recip(den.a, den.a, [den.k], [den.k])
        cr = tmp("cr"); ci = tmp("ci")
        tt(eng, t1.a, nr.a, lre, ALU.mult, [nr.k, ktok], [t1.k])
        tt(eng, t2.a, abi.a, lim, ALU.mult, [abi.k, ktok], [t2.k])
        tt(eng, cr.a, t1.a, t2.a, ALU.add, [t1.k, t2.k], [cr.k])
        tt(eng, cr.a, cr.a, den.a, ALU.mult, [cr.k, den.k], [cr.k])
        tt(eng, t1.a, abi.a, lre, ALU.mult, [abi.k, ktok], [t1.k])
        tt(eng, t2.a, nr.a, lim, ALU.mult, [nr.k, ktok], [t2.k])
        tt(eng, ci.a, t1.a, t2.a, ALU.subtract, [t1.k, t2.k], [ci.k])
        tt(eng, ci.a, ci.a, den.a, ALU.mult, [ci.k, den.k], [ci.k])
        Br = tmp("Br"); Bi = tmp("Bi")
        tt(eng, t1.a, cr.a, bre, ALU.mult, [cr.k, ktok], [t1.k])
        tt(eng, t2.a, ci.a, bim, ALU.mult, [ci.k, ktok], [t2.k])
        tt(eng, Br.a, t1.a, t2.a, ALU.subtract, [t1.k, t2.k], [Br.k])
        tt(eng, t1.a, cr.a, bim, ALU.mult, [cr.k, ktok], [t1.k])
        tt(eng, t2.a, ci.a, bre, ALU.mult, [ci.k, ktok], [t2.k])
        tt(eng, Bi.a, t1.a, t2.a, ALU.add, [t1.k, t2.k], [Bi.k])
        r["Bb"] = (Br, Bi)
        pw = {1: (abr, abi)}
        for k in range(2, npow + 1):
            pr = tmp("pr"); pi = tmp("pi")
            p0r, p0i = pw[k - 1]
            tt(eng, t1.a, p0r.a, abr.a, ALU.mult, [p0r.k, abr.k], [t1.k])
            tt(eng, t2.a, p0i.a, abi.a, ALU.mult, [p0i.k, abi.k], [t2.k])
            tt(eng, pr.a, t1.a, t2.a, ALU.subtract, [t1.k, t2.k], [pr.k])
            tt(eng, t1.a, p0r.a, abi.a, ALU.mult, [p0r.k, abi.k], [t1.k])
            tt(eng, t2.a, p0i.a, abr.a, ALU.mult, [p0i.k, abr.k], [t2.k])
            tt(eng, pi.a, t1.a, t2.a, ALU.add, [t1.k, t2.k], [pi.k])
            pw[k] = (pr, pi)
        r["pw"] = pw
        r["t1"], r["t2"] = t1, t2
        return r

    def s5_prep(half):
        i0 = NI * half
        NC_ = NI * 8
        c8k = T(P, "c8k", [128, NC_], F32); s8k = T(P, "s8k", [128, NC_], F32); lrk = T(P, "lrk", [128, NC_], F32); lik = T(P, "lik", [128, NC_], F32)
        ldk = T(P, "ldk", [128, NC_], F32); limk = T(P, "limk", [128, NC_], F32); lrek = T(P, "lrek", [128, NC_], F32)
        with P.scope():
            s5B = T(P, "s5B", [128, 5, W], F32)
            load(s5B.a, s5B.k, D["s5B"][:, :, W * half:W * (half + 1)])
            rb = a_chain("dve", s5B[:, 0, :], s5B[:, 1, :], s5B[:, 2, :], s5B.k, 7, s5B[:, 3, :], s5B[:, 4, :])
            Br, Bi = rb["Bb"]
            bvt = T(P, "bvt", [128, NI * 4096], BF16)
            bv6 = bvt.a.rearrange("p (i d s r j q) -> p i d s r j q", i=NI, d=2, s=8, r=2, j=2)
            wre = tmp("wre"); wim = tmp("wim")
            t1, t2 = rb["t1"], rb["t2"]
            v3 = lambda t: t.a.rearrange("p (i d q) -> p i d q", i=NI, d=2)
            for s in range(8):
                for d in range(2):
                    e = (7 - s) if d == 0 else s
                    sl = lambda t, d=d: v3(t)[:, :, d, :]
                    if e == 0:
                        cp("dve", sl(wre), sl(Br), [Br.k], [wre.k])
                        cp("dve", sl(wim), sl(Bi), [Bi.k], [wim.k])
                    else:
                        pr, pi = rb["pw"][e]
                        tt("dve", sl(t1), sl(pr), sl(Br), ALU.mult, [pr.k, Br.k], [t1.k])
                        tt("dve", sl(t2), sl(pi), sl(Bi), ALU.mult, [pi.k, Bi.k], [t2.k])
                        tt("dve", sl(wre), sl(t1), sl(t2), ALU.subtract, [t1.k, t2.k], [wre.k])
                        tt("dve", sl(t1), sl(pr), sl(Bi), ALU.mult, [pr.k, Bi.k], [t1.k])
                        tt("dve", sl(t2), sl(pi), sl(Br), ALU.mult, [pi.k, Br.k], [t2.k])
                        tt("dve", sl(wim), sl(t1), sl(t2), ALU.add, [t1.k, t2.k], [wim.k])
                for r_, src_ in ((0, wre), (1, wim)):
                    for j in range(2):
                        act(bv6[:, :, :, s, r_, j, :], v3(src_), AF.Copy, [src_.k, cmisc.k], [bvt.k], scale=maskJB(j))
            for i in range(NI):
                store(D["scr_bv"][i0 + i], bvt[:, i * 4096:(i + 1) * 4096], bvt.k, key=("bv", i0 + i))
            if half == 0:
                dbg_out("d_bv0", bvt[:, 0:4096], bvt.k)

        with P.scope():
            s5C = T(P, "s5C", [128, 7, W], F32)
            load(s5C.a, s5C.k, D["s5C"][:, :, W * half:W * (half + 1)])
            rc = a_chain("dve", s5C[:, 0, :], s5C[:, 1, :], s5C[:, 2, :], s5C.k, 8, s5C[:, 5, :], s5C[:, 6, :])
            cre, cim = s5C[:, 3, :], s5C[:, 4, :]
            t1, t2 = rc["t1"], rc["t2"]
            v4 = lambda ap: ap.rearrange("p (i q d h) -> p i q d h", i=NI, q=4, d=2)
            h0v = lambda t: v4(t.a)[:, :, :, :, 0].rearrange("p i q d -> p (i q d)")
            uc, us = rc["unit"]
            cp("dve", c8k.a, h0v(uc), [uc.k], [c8k.k])
            cp("dve", s8k.a, h0v(us), [us.k], [s8k.k])
            cp("dve", ldk.a, v4(s5C[:, 2, :])[:, :, :, :, 0].rearrange("p i q d -> p (i q d)"), [s5C.k], [ldk.k])
            cp("dve", limk.a, v4(s5C[:, 1, :])[:, :, :, :, 0].rearrange("p i q d -> p (i q d)"), [s5C.k], [limk.k])
            cp("dve", lrek.a, v4(s5C[:, 0, :])[:, :, :, :, 0].rearrange("p i q d -> p (i q d)"), [s5C.k], [lrek.k])
            yct = T(P, "yct", [128, NI * 4096], BF16)
            yc7 = yct.a.rearrange("p (i q d s r j h) -> p i q d s r j h", i=NI, q=4, d=2, s=8, r=2, j=2)
            cpr = tmp("cpr"); cpi = tmp("cpi")
            for s in range(8):
                for d in range(2):
                    e = (s + 1) if d == 0 else (8 - s)
                    pr, pi = rc["pw"][e]
                    sl = lambda ap, d=d: v4(ap)[:, :, :, d, :]
                    tt("dve", sl(t1.a), sl(cre), sl(pr.a), ALU.mult, [s5C.k, pr.k], [t1.k])
                    tt("dve", sl(t2.a), sl(cim), sl(pi.a), ALU.mult, [s5C.k, pi.k], [t2.k])
                    tt("dve", sl(cpr.a), sl(t1.a), sl(t2.a), ALU.subtract, [t1.k, t2.k], [cpr.k])
                    tt("dve", sl(t1.a), sl(cre), sl(pi.a), ALU.mult, [s5C.k, pi.k], [t1.k])
                    tt("dve", sl(t2.a), sl(cim), sl(pr.a), ALU.mult, [s5C.k, pr.k], [t2.k])
                    stt(sl(cpi.a), sl(t1.a), -1.0, sl(t2.a), ALU.mult, ALU.subtract, [t1.k, t2.k], [cpi.k])
                for r_, src_ in ((0, cpr), (1, cpi)):
                    for j in range(2):
                        for i in range(NI):
                            act(yc7[:, i, :, :, s, r_, j, :], v4(src_.a)[:, i], AF.Copy, [src_.k, cmisc.k], [yct.k], scale=maskJC(j))
            for i in range(NI):
                store(D["scr_yc"][i0 + i], yct[:, i * 4096:(i + 1) * 4096], yct.k, key=("yc", i0 + i))
            if half == 0:
                dbg_out("d_yc0", yct[:, 0:4096], yct.k)
            Brc, Bic = rc["Bb"]
            Lre = T(P, "Lre", [128, NI * 2048], F32)
            Lim = T(P, "Lim", [128, NI * 2048], F32)
            L6 = lambda t: t.a.rearrange("p (i q d l j h) -> p i q d l j h", i=NI, q=4, d=2, l=8, j=2)
            Rre = T(P, "Rre", [128, NI * 256], F32)
            Rim = T(P, "Rim", [128, NI * 256], F32)
            R5 = lambda t: t.a.rearrange("p (i q d j h) -> p i q d j h", i=NI, q=4, d=2, j=2)
            for j in range(2):
                for i in range(NI):
                    act(R5(Rre)[:, i, :, :, j, :], v4(cre)[:, i], AF.Copy, [s5C.k, cmisc.k], [Rre.k], scale=maskJC(j))
                    act(R5(Rim)[:, i, :, :, j, :], v4(cim)[:, i], AF.Copy, [s5C.k, cmisc.k], [Rim.k], scale=nmaskJC(j))
            lre_t = tmp("lre_t"); lim_t = tmp("lim_t")
            for dlag in range(8):
                if dlag == 0:
                    srcs = (Brc, Bic)
                else:
                    pr, pi = rc["pw"][dlag]
                    tt("dve", t1.a, pr.a, Brc.a, ALU.mult, [pr.k, Brc.k], [t1.k])
                    tt("dve", t2.a, pi.a, Bic.a, ALU.mult, [pi.k, Bic.k], [t2.k])
                    tt("dve", lre_t.a, t1.a, t2.a, ALU.subtract, [t1.k, t2.k], [lre_t.k])
                    tt("dve", t1.a, pr.a, Bic.a, ALU.mult, [pr.k, Bic.k], [t1.k])
                    tt("dve", t2.a, pi.a, Brc.a, ALU.mult, [pi.k, Brc.k], [t2.k])
                    tt("dve", lim_t.a, t1.a, t2.a, ALU.add, [t1.k, t2.k], [lim_t.k])
                    srcs = (lre_t, lim_t)
                for dst, src_ in ((Lre, srcs[0]), (Lim, srcs[1])):
                    for j in range(2):
                        for i in range(NI):
                            act(L6(dst)[:, i, :, :, dlag, j, :], v4(src_.a)[:, i], AF.Copy, [src_.k, cmisc.k], [dst.k], scale=maskJC(j))
            kt_sb = T(P, "kt_sb", [128, NI * 1920], BF16)
            kt4 = kt_sb.a.rearrange("p (i s q c) -> p i s q c", i=NI, s=15, q=4)
            dtmp = T(P, "dtmp", [128, 32], F32)
            for i in range(NI):
                pt, pk = P.ps()
                pv = pt[:, 0:480].rearrange("p (s c) -> p s c", c=32)
                for q in range(4):
                    rows = slice(32 * q, 32 * q + 32)
                    for slot in range(15):
                        if slot == 0:
                            terms = [(0, 0), (1, 0)]
                        elif slot < 8:
                            terms = [(0, slot)]
                        else:
                            terms = [(1, slot - 7)]
                        n = 2 * len(terms)
                        kk = 0
                        for (d, dlag) in terms:
                            for (Lt, Rt) in ((Lre, Rre), (Lim, Rim)):
                                mm(pv[rows, slot, :], L6(Lt)[:, i, q, d, dlag].rearrange("p j h -> p (j h)"),
                                   R5(Rt)[:, i, q, d].rearrange("p j h -> p (j h)"), kk == 0, kk == n - 1, [Lt.k, Rt.k], pk,
                                   signal=(kk == n - 1), tp=(0, 32 * q))
                                kk += 1
                for q2 in range(4):
                    ts("dve", kt4[:, i, 1:15, q2, :], pv[:, 1:15, :], maskQ(q2), ALU.mult, [pk, cmisc.k], [kt_sb.k])
                    ts("dve", dtmp.a, ident[:, 32 * q2:32 * q2 + 32], dcol(i0 + i), ALU.mult, [ident.k, cmisc.k], [dtmp.k])
                    stt(kt4[:, i, 0, q2, :], pv[:, 0, :], maskQ(q2), dtmp.a, ALU.mult, ALU.add, [pk, cmisc.k, dtmp.k], [kt_sb.k])
            for i in range(NI):
                store(D["scr_kt"][i0 + i], kt_sb[:, i * 1920:(i + 1) * 1920], kt_sb.k, key=("kt", i0 + i))
            if half == 0:
                dbg_out("d_kt0", kt_sb[:, 0:1920], kt_sb.k)

        with P.scope():
            a48 = T(P, "a48", [128, NC_], F32); b48 = T(P, "b48", [128, NC_], F32)
            lvs = lambda slot: lv[:, slot, NC_ * half:NC_ * (half + 1)]

            def csq(c, s):
                c2 = T(P, "cq", [128, NC_], F32); s2 = T(P, "sq", [128, NC_], F32)
                tt("dve", a48.a, c.a, c.a, ALU.mult, [c.k], [a48.k])
                tt("dve", b48.a, s.a, s.a, ALU.mult, [s.k], [b48.k])
                tt("dve", s2.a, c.a, s.a, ALU.mult, [c.k, s.k], [s2.k])
                ts("dve", s2.a, s2.a, 2.0, ALU.mult, [s2.k], [s2.k])
                tt("dve", c2.a, a48.a, b48.a, ALU.subtract, [a48.k, b48.k], [c2.k])
                return c2, s2
            MAGIC = 12582912.0

            def sm(name="sm"):
                return T(P, name, [128, NC_], F32)
            nn = sm("nn")
            ts("dve", nn.a, ldk.a, 1.4426950408889634, ALU.mult, [ldk.k], [nn.k])
            ts("dve", nn.a, nn.a, MAGIC, ALU.add, [nn.k], [nn.k])
            ts("dve", nn.a, nn.a, MAGIC, ALU.subtract, [nn.k], [nn.k])
            rx = sm("rx")
            stt(rx.a, nn.a, -0.693359375, ldk.a, ALU.mult, ALU.add, [nn.k, ldk.k], [rx.k])
            stt(rx.a, nn.a, 2.12194440e-4, rx.a, ALU.mult, ALU.add, [nn.k, rx.k], [rx.k])
            er = sm("er")
            ts("dve", er.a, rx.a, 1.0 / 10.0, ALU.mult, [rx.k], [er.k], s2=1.0, op1=ALU.add)
            for n in range(9, 0, -1):
                stt(er.a, er.a, 1.0 / n, rx.a, ALU.mult, ALU.mult, [er.k, rx.k], [er.k])
                ts("dve", er.a, er.a, 1.0, ALU.add, [er.k], [er.k])
            p2 = sm("p2"); p2t = sm("p2t")
            first = True
            for v in range(-12, 0):
                dst = p2 if first else p2t
                ts("dve", dst.a, nn.a, float(v), ALU.is_equal, [nn.k], [dst.k], s2=float(2.0 ** v), op1=ALU.mult)
                if not first:
                    tt("dve", p2.a, p2.a, p2t.a, ALU.add, [p2.k, p2t.k], [p2.k])
                first = False
            dta = sm("dta")
            tt("dve", dta.a, er.a, p2.a, ALU.mult, [er.k, p2.k], [dta.k])
            tt("dve", lik.a, limk.a, dta.a, ALU.mult, [limk.k, dta.k], [lik.k])
            tt("dve", lrk.a, lrek.a, dta.a, ALU.mult, [lrek.k, dta.k], [lrk.k])
            phi = sm("phi")
            ts("dve", phi.a, lik.a, 8.0, ALU.mult, [lik.k], [phi.k])
            kf = sm("kf")
            ts("dve", kf.a, phi.a, 2.0 / math.pi, ALU.mult, [phi.k], [kf.k])
            ts("dve", kf.a, kf.a, MAGIC, ALU.add, [kf.k], [kf.k])
            ts("dve", kf.a, kf.a, MAGIC, ALU.subtract, [kf.k], [kf.k])
            rr = sm("rr")
            stt(rr.a, kf.a, -1.5703125, phi.a, ALU.mult, ALU.add, [kf.k, phi.k], [rr.k])
            stt(rr.a, kf.a, -4.837512969970703e-4, rr.a, ALU.mult, ALU.add, [kf.k, rr.k], [rr.k])
            stt(rr.a, kf.a, -7.549789954891882e-8, rr.a, ALU.mult, ALU.add, [kf.k, rr.k], [rr.k])
            z = sm("z")
            tt("dve", z.a, rr.a, rr.a, ALU.mult, [rr.k], [z.k])
            ps_ = sm("ps")
            ts("dve", ps_.a, z.a, 1.0 / 362880.0, ALU.mult, [z.k], [ps_.k])
            for coef in (-1.0 / 5040.0, 1.0 / 120.0, -1.0 / 6.0):
                stt(ps_.a, ps_.a, coef, z.a, ALU.add, ALU.mult, [ps_.k, z.k], [ps_.k])
            sr = sm("sr")
            stt(sr.a, ps_.a, 1.0, rr.a, ALU.add, ALU.mult, [ps_.k, rr.k], [sr.k])
            pc_ = sm("pc")
            ts("dve", pc_.a, z.a, -1.0 / 3628800.0, ALU.mult, [z.k], [pc_.k])
            for coef in (1.0 / 40320.0, -1.0 / 720.0, 1.0 / 24.0, -0.5):
                stt(pc_.a, pc_.a, coef, z.a, ALU.add, ALU.mult, [pc_.k, z.k], [pc_.k])
            cr_ = sm("cr")
            ts("dve", cr_.a, pc_.a, 1.0, ALU.add, [pc_.k], [cr_.k])
            mq = sm("mq")
            ts("dve", mq.a, kf.a, 0.25, ALU.mult, [kf.k], [mq.k], s2=-0.375, op1=ALU.add)
            ts("dve", mq.a, mq.a, MAGIC, ALU.add, [mq.k], [mq.k])
            ts("dve", mq.a, mq.a, MAGIC, ALU.subtract, [mq.k], [mq.k])
            qd = sm("qd")
            stt(qd.a, mq.a, -4.0, kf.a, ALU.mult, ALU.add, [mq.k, kf.k], [qd.k])
            mA = sm("mA"); mB = sm("mB"); m2 = sm("m2")
            ts("dve", mA.a, qd.a, 0.0, ALU.is_equal, [qd.k], [mA.k])
            ts("dve", m2.a, qd.a, 2.0, ALU.is_equal, [qd.k], [m2.k])
            tt("dve", mA.a, mA.a, m2.a, ALU.subtract, [mA.k, m2.k], [mA.k])
            ts("dve", mB.a, qd.a, 1.0, ALU.is_equal, [qd.k], [mB.k])
            ts("dve", m2.a, qd.a, 3.0, ALU.is_equal, [qd.k], [m2.k])
            tt("dve", mB.a, mB.a, m2.a, ALU.subtract, [mB.k, m2.k], [mB.k])
            c8 = sm("c8"); s8 = sm("s8")
            tt("dve", a48.a, sr.a, mA.a, ALU.mult, [sr.k, mA.k], [a48.k])
            tt("dve", b48.a, cr_.a, mB.a, ALU.mult, [cr_.k, mB.k], [b48.k])
            tt("dve", s8.a, a48.a, b48.a, ALU.add, [a48.k, b48.k], [s8.k])
            tt("dve", a48.a, cr_.a, mA.a, ALU.mult, [cr_.k, mA.k], [a48.k])
            tt("dve", b48.a, sr.a, mB.a, ALU.mult, [sr.k, mB.k], [b48.k])
            tt("dve", c8.a, a48.a, b48.a, ALU.subtract, [a48.k, b48.k], [c8.k])
            x8 = sm("x8")
            ts("dve", x8.a, lrk.a, 8.0, ALU.mult, [lrk.k], [x8.k])
            pe_ = sm("pe")
            ts("dve", pe_.a, x8.a, 1.0 / 10.0, ALU.mult, [x8.k], [pe_.k], s2=1.0, op1=ALU.add)
            for n in range(9, 0, -1):
                stt(pe_.a, pe_.a, 1.0 / n, x8.a, ALU.mult, ALU.mult, [pe_.k, x8.k], [pe_.k])
                ts("dve", pe_.a, pe_.a, 1.0, ALU.add, [pe_.k], [pe_.k])
            cp("dve", lvs(0), pe_.a, [pe_.k], [lv.k])
            Ec = T(P, "Ec", [128, NC_, 128], F32); Es = T(P, "Es", [128, NC_, 128], F32)
            op("pool", lambda e: e.memset(Ec[:, :, 0:1], 1.0), writes=[Ec.k])
            op("pool", lambda e: e.memset(Es[:, :, 0:1], 0.0), writes=[Es.k])
            wc, ws_ = c8, s8
            e1 = T(P, "e1", [128, NC_, 64], F32); e2 = T(P, "e2", [128, NC_, 64], F32)
            for m in range(7):
                n = 1 << m
                bc = lambda t, n=n: t.a.unsqueeze(2).to_broadcast([128, NC_, n])
                lo = slice(0, n); hi = slice(n, 2 * n)
                tt("dve", e1[:, :, lo], Ec[:, :, lo], bc(wc), ALU.mult, [Ec.k, wc.k], [e1.k])
                tt("dve", e2[:, :, lo], Es[:, :, lo], bc(ws_), ALU.mult, [Es.k, ws_.k], [e2.k])
                tt("dve", Ec[:, :, hi], e1[:, :, lo], e2[:, :, lo], ALU.subtract, [e1.k, e2.k], [Ec.k])
                tt("dve", e1[:, :, lo], Ec[:, :, lo], bc(ws_), ALU.mult, [Ec.k, ws_.k], [e1.k])
                tt("dve", e2[:, :, lo], Es[:, :, lo], bc(wc), ALU.mult, [Es.k, wc.k], [e2.k])
                tt("dve", Es[:, :, hi], e1[:, :, lo], e2[:, :, lo], ALU.add, [e1.k, e2.k], [Es.k])
                if m < 6:
                    wc, ws_ = csq(wc, ws_)
            cp("dve", lvs(1), c8.a, [c8.k], [lv.k])
            cp("dve", lvs(2), s8.a, [s8.k], [lv.k])
            tt("dve", a48.a, Ec[:, :, 127], c8.a, ALU.mult, [Ec.k, c8.k], [a48.k])
            tt("dve", b48.a, Es[:, :, 127], s8.a, ALU.mult, [Es.k, s8.k], [b48.k])
            tt("dve", lvs(3), a48.a, b48.a, ALU.subtract, [a48.k, b48.k], [lv.k])
            tt("dve", a48.a, Ec[:, :, 127], s8.a, ALU.mult, [Ec.k, s8.k], [a48.k])
            tt("dve", b48.a, Es[:, :, 127], c8.a, ALU.mult, [Es.k, c8.k], [b48.k])
            tt("dve", lvs(4), a48.a, b48.a, ALU.add, [a48.k, b48.k], [lv.k])
            act(lvs(5), lrk.a, AF.Exp, [lrk.k], [lv.k], scale=1024.0)
            Es2 = Es.a.rearrange("p (a d) c -> p a d c", d=2)
            ts("pool", Es2[:, :, 1, :], Es2[:, :, 1, :], -1.0, ALU.mult, [Es.k], [Es.k])
            dmk = T(P, "dmk", [128, 2, 2, 128], F32)
            load(dmk.a, dmk.k, D["dmask"])
            r8x = T(P, "r8x", [128, NC_, 128], F32)
            cp("dve", r8x.a, lvs(0).unsqueeze(2).to_broadcast([128, NC_, 128]), [lv.k], [r8x.k])
            for ty in range(2):
                dtt = T(P, "dtt%d" % ty, [128, NC_, 128], F32)
                tt("dve", dtt.a.rearrange("p (a d) c -> p a d c", d=2), r8x.a.rearrange("p (a d) c -> p a d c", d=2),
                   dmk[:, ty, :, :].unsqueeze(1).to_broadcast([128, NC_ // 2, 2, 128]), ALU.mult, [r8x.k, dmk.k], [dtt.k])
                for i in range(NI):
                    for d in range(2):
                        store(D["scr_d"][i0 + i, ty][:, d * 512:(d + 1) * 512].rearrange("p (q c) -> p q c", q=4),
                              dtt[:, 8 * i:8 * i + 8, :].rearrange("p (q d) c -> p q d c", d=2)[:, :, d, :], dtt.k, key=("dt%d" % d, i0 + i, ty))
            for i in range(NI):
                store(D["scr_e"][i0 + i, :, 0, :], Ec[:, 8 * i:8 * i + 8, :].rearrange("p a c -> p (a c)"), Ec.k, key=("ec", i0 + i))
                store(D["scr_e"][i0 + i, :, 1, :], Es[:, 8 * i:8 * i + 8, :].rearrange("p a c -> p (a c)"), Es.k, key=("es", i0 + i))
            if half == 0:
                dbg_out("d_ec0", Ec[:, 0:8, :].rearrange("p a c -> p (a c)"), Ec.k)
                dbg_out("d_es0", Es[:, 0:8, :].rearrange("p a c -> p (a c)"), Es.k)

    for half in range(2):
        with P.scope():
            s5_prep(half)
    dbg_out("d_lv", lv.a.rearrange("p a b -> p (a b)"), lv.k)

    if stage <= 1:
        return

    xres = T(P, "xres", [128, 8, 1024], F32)
    _xk0 = [Tok(), Tok()]
    xk = [_xk0, _xk0]
    stout = [T(P, "stout%d" % r_, [128, 192], F32) for r_ in range(2)]
    REG = [dict(name="p", off=0, cond=0, nseq=4, n=32, xsrc=D["xpT"], pos=None),
           dict(name="s", off=1024, cond=1, nseq=1, n=128, xsrc=D["xsT"][:, 0:1024], pos=D["posT"][:, 0:1024])]

    def wload_bf16(dst_ap, dst_tok, src_ap):
        return dma("pool", lambda e: e.dma_start(out=dst_ap, in_=src_ap), writes=[dst_tok])

    class NormScratch:
        def __init__(self):
            self.sq = [T(P, "nsq%d" % b, [128, 512], BF16) for b in range(2)]
            self.rstd = T(P, "nrstd", [128, 512], F32)
            self.tmp = [T(P, "ntmp%d" % b, [128, 512], F32) for b in range(2)]

    def sumsq_rstd(xa, xks, n, ns):
        pt, pk = P.ps()
        for dt in range(8):
            sq = ns.sq[dt % 2]
            act(sq[:, :n], xa(dt), AF.Square, xks, [sq.k])
            mm(pt[:, :n], ones_bf.a, sq[:, :n], dt == 0, dt == 7, [ones_bf.k, sq.k], pk, signal=(dt == 7))
        act(ns.rstd[:, :n], pt[:, :n], AF.Ln, [pk, epsc.k], [ns.rstd.k], scale=1.0 / 1024.0, bias=epsc.a)
        act(ns.rstd[:, :n], ns.rstd[:, :n], AF.Exp, [ns.rstd.k], [ns.rstd.k], scale=-0.5)

    def modnorm(xa, xks, l, kS, kSH, c, outa, outk, n, ns):
        sumsq_rstd(xa, xks, n, ns)
        for dt in range(8):
            tb = ns.tmp[dt % 2]
            stt(tb[:, :n], xa(dt), modcol(l, kS, dt, c), ns.rstd[:, :n], ALU.mult, ALU.mult, xks + [mod_all.k, ns.rstd.k], [tb.k])
            act(outa(dt), tb[:, :n], AF.Identity, [tb.k, mod_all.k], [outk], bias=modcol(l, kSH, dt, c))

    def post_residual(ya, yks, l, kGG, c, xa, xks, n, ns):
        sumsq_rstd(ya, yks, n, ns)
        for dt in range(8):
            tb = ns.tmp[dt % 2]
            stt(tb[:, :n], ya(dt), modcol(l, kGG, dt, c), ns.rstd[:, :n], ALU.mult, ALU.mult, yks + [mod_all.k, ns.rstd.k], [tb.k])
            tt("pool", xa(dt), xa(dt), tb[:, :n], ALU.add, xks + [tb.k], xks)

    def load_region(rg):
        ri = 0 if rg["name"] == "p" else 1
        src3 = rg["xsrc"].rearrange("(c p) n -> p c n", p=128)
        for blk in range(2):
            dst = xres[:, :, rg["off"] + blk * 512: rg["off"] + (blk + 1) * 512]
            load(dst, xk[ri][blk], src3[:, :, blk * 512:(blk + 1) * 512])
        if rg["pos"] is not None:
            with P.scope():
                pos3 = rg["pos"].rearrange("(c p) n -> p c n", p=128)
                for blk in range(2):
                    pb = T(P, "posb", [128, 8, 512], F32)
                    load(pb.a, pb.k, pos3[:, :, blk * 512:(blk + 1) * 512])
                    dst = xres[:, :, rg["off"] + blk * 512: rg["off"] + (blk + 1) * 512]
                    tt("pool", dst, dst, pb.a, ALU.add, [xk[ri][blk], pb.k], [xk[ri][blk]])

    def s5_section(rname, nseq, n, hT, hk, get_win, gT, mode, init, fin, spans=((0, 512), (512, 512))):
        full = (mode == "full")
        with P.scope():
            NB = 2
            if full:
                XFs = [[T(P, "XF%d" % r_, [128, 4, nseq, n + 1], BF16) for r_ in range(2)] for _ in range(NB)]
                XBs = [[T(P, "XB%d" % r_, [128, 4, nseq, n + 1], BF16) for r_ in range(2)] for _ in range(NB)]
                ycws = [T(P, "ycw", [128, 4096], BF16) for _ in range(NB)]
                ktws = [T(P, "ktw", [128, 1920], BF16) for _ in range(NB)]
                for bb in range(NB):
                    for r_ in range(2):
                        op("pool", lambda e, t=XFs[bb][r_]: e.memset(t.a, 0.0), writes=[XFs[bb][r_].k])
                        op("pool", lambda e, t=XBs[bb][r_]: e.memset(t.a, 0.0), writes=[XBs[bb][r_].k])
            bvws = [T(P, "bvw", [128, 4096], BF16) for _ in range(NB)]
            ews = [T(P, "ew", [128, 2, 1024], F32) for _ in range(NB)]
            dws = [T(P, "dw", [128, 1024], F32) for _ in range(NB)]
            ty = 0 if nseq > 1 else 1
            gsm = [T(P, "gsm%d" % b, [128, 8], F32) for b in range(2)]
            zis = [T(P, "zi", [128, 1024], BF16) for _ in range(NB)]
            tbs = [[T(P, "s5t%d" % b, [128, 1024], F32) for b in range(4)]] * NB
            Gbs = [[T(P, "s5g%d" % b, [128, 1024], F32) for b in range(2)]] * NB
            fsm = [T(P, "fsm%d" % b, [128, 4], F32) for b in range(2)]
            wb_next = get_win(0)
            for i in range(6):
                bb = i % NB
                bvw, ew, zi, tb_, Gb = bvws[bb], ews[bb], zis[bb], tbs[bb], Gbs[bb]
                dw = dws[bb]
                dma("sp", lambda e, dw=dw, i=i: e.dma_start(out=dw.a, in_=D["scr_d"][i, ty]),
                    reads=[scr_tok[("dt0", i, ty)], scr_tok[("dt1", i, ty)]], writes=[dw.k])
                if full:
                    XF, XB, ycw, ktw = XFs[bb], XBs[bb], ycws[bb], ktws[bb]
                loadr(bvw.a, bvw.k, D["scr_bv"][i], scr_tok[("bv", i)])
                if full:
                    loadr(ycw.a, ycw.k, D["scr_yc"][i], scr_tok[("yc", i)])
                    loadr(ktw.a, ktw.k, D["scr_kt"][i], scr_tok[("kt", i)])
                loadr(ew[:, 0, :], ew.k, D["scr_e"][i, :, 0, :], scr_tok[("ec", i)])
                loadr(ew[:, 1, :], ew.k, D["scr_e"][i, :, 1, :], scr_tok[("es", i)])
                wb = wb_next
                for blk in range(2):
                    pt, pk = P.ps()
                    for kt in range(8):
                        mm(pt[:, :], wb[:, kt, :], hT[:, kt, blk * 512:(blk + 1) * 512], kt == 0, kt == 7, [wb.k, hk[blk]], pk, signal=(kt == 7))
                    act(zi[:, blk * 512:(blk + 1) * 512], pt[:, :], AF.Identity, [pk], [zi.k])
                if i < 5:
                    wb_next = get_win(i + 1)
                if i == 0:
                    dbg_out("d_z0_" + rname, zi.a, zi.k)
                zv = zi.a.rearrange("p (c s) -> p c s", s=8)
                bv5 = bvw.a.rearrange("p (d s r m) -> p d s r m", d=2, s=8, r=2)
                Vps = [P.ps() for _ in range(4)]
                for r_ in range(2):
                    for d in range(2):
                        col = (r_ * 2 + d) * 128
                        for s in range(8):
                            for q in range(4):
                                pt, pk = Vps[q]
                                rows = slice(32 * q, 32 * q + 32)
                                mm(pt[:, col:col + 128], bv5[rows, d, s, r_, :], zv[rows, :, s], s == 0, s == 7, [bvw.k, zi.k], pk,
                                   signal=(s == 7 and r_ == 1 and d == 1), tp=(32 * q, 0))
                t1, t2, t3, t4 = tb_
                dq = lambda ap: ap.rearrange("p (d q c) -> p d q c", d=2, q=4)
                for q in range(4):
                    hs = slice(q * 256, (q + 1) * 256)
                    vp, vk = Vps[q]
                    v2 = lambda ap: ap.rearrange("p (d c) -> p d c", d=2)
                    vr, vi = v2(vp[:, 0:256]), v2(vp[:, 256:512])
                    ec_, es_ = v2(ew[:, 0, hs]), v2(ew[:, 1, hs])
                    o1, o2, o3, o4 = dq(t1.a)[:, :, q, :], dq(t2.a)[:, :, q, :], dq(t3.a)[:, :, q, :], dq(t4.a)[:, :, q, :]
                    tt("dve", o1, vr, ec_, ALU.mult, [vk, ew.k], [t1.k])
                    tt("dve", o4, vi, es_, ALU.mult, [vk, ew.k], [t4.k])
                    tt("dve", o3, vr, es_, ALU.mult, [vk, ew.k], [t3.k])
                    tt("dve", o2, vi, ec_, ALU.mult, [vk, ew.k], [t2.k])
                    tt("pool", o1, o1, o4, ALU.add, [t1.k, t4.k], [t1.k])
                    tt("pool", o2, o2, o3, ALU.subtract, [t2.k, t3.k], [t2.k])
                for r_, (src_, dst) in enumerate(((t1, Gb[0]), (t2, Gb[1]))):
                    if init is not None:
                        gi_ = init["G"][r_]
                        tt("dve", gsm[r_].a, gi_[:, 8 * i:8 * i + 8], lv[:, 0, 8 * i:8 * i + 8], ALU.mult, [gi_.k, lv.k], [gsm[r_].k])
                        gq = gsm[r_].a.rearrange("p (q d) -> p q d", d=2)
                        tt("dve", dq(src_.a)[:, 0, :, 0], dq(src_.a)[:, 0, :, 0], gq[:, :, 0], ALU.add, [src_.k, gsm[r_].k], [src_.k])
                        tt("dve", dq(src_.a)[:, 1, :, 127], dq(src_.a)[:, 1, :, 127], gq[:, :, 1], ALU.add, [src_.k, gsm[r_].k], [src_.k])
                    for d in range(2):
                        hs = slice(d * 512, (d + 1) * 512)
                        sa, da, de = src_[:, hs], dst[:, hs], dw[:, hs]
                        if d == 1:
                            sa, da, de = sa[:, ::-1], da[:, ::-1], de[:, ::-1]
                        op("dve", lambda e, da=da, sa=sa, de=de: e.tensor_tensor_scan(out=da, data0=de, data1=sa, initial=0.0, op0=ALU.mult, op1=ALU.add),
                           reads=[src_.k, dw.k], writes=[dst.k])
                g0, g1 = Gb
                if not full:
                    Gv = lambda ap: ap.rearrange("p (cb c) -> p cb c", c=128)
                    Fre, Fim = fin
                    fv = lambda t_: t_[:, 8 * i:8 * i + 8].rearrange("p (q d) -> p q d", d=2)
                    gre_f, gim_f = dq(g0.a)[:, 0, :, n - 1], dq(g1.a)[:, 0, :, n - 1]
                    ec_f, es_f = Gv(ew[:, 0, :])[:, 0::2, n - 1], Gv(ew[:, 1, :])[:, 0::2, n - 1]
                    fa, fb = fsm
                    tt("dve", fa.a, gre_f, ec_f, ALU.mult, [g0.k, ew.k], [fa.k])
                    tt("dve", fb.a, gim_f, es_f, ALU.mult, [g1.k, ew.k], [fb.k])
                    tt("dve", fv(Fre)[:, :, 0], fa.a, fb.a, ALU.subtract, [fa.k, fb.k], [Fre.k])
                    tt("dve", fa.a, gre_f, es_f, ALU.mult, [g0.k, ew.k], [fa.k])
                    tt("dve", fb.a, gim_f, ec_f, ALU.mult, [g1.k, ew.k], [fb.k])
                    tt("dve", fv(Fim)[:, :, 0], fa.a, fb.a, ALU.add, [fa.k, fb.k], [Fim.k])
                    cp("dve", fv(Fre)[:, :, 1], dq(g0.a)[:, 1, :, 0], [g0.k], [Fre.k])
                    cp("dve", fv(Fim)[:, :, 1], dq(g1.a)[:, 1, :, 0], [g1.k], [Fim.k])
                    continue
                qd = lambda ap: ap.rearrange("p (q d c) -> p q d c", q=4, d=2)
                gp = lambda ap: ap.rearrange("p (d q c) -> p q d c", d=2, q=4)
                g2 = tbs[0][0] if False else None
                tt("dve", qd(t1.a), gp(g0.a), qd(ew[:, 0, :]), ALU.mult, [g0.k, ew.k], [t1.k])
                tt("dve", qd(t2.a), gp(g1.a), qd(ew[:, 1, :]), ALU.mult, [g1.k, ew.k], [t2.k])
                tt("dve", qd(t3.a), gp(g0.a), qd(ew[:, 1, :]), ALU.mult, [g0.k, ew.k], [t3.k])
                tt("dve", gp(g0.a), gp(g1.a), qd(ew[:, 0, :]), ALU.mult, [g1.k, ew.k], [g0.k])
                v5 = lambda t: (t.a.rearrange("p (d q b n) -> p q d b n", q=4, d=2, b=nseq) if t is g0
                                else t.a.rearrange("p (q d b n) -> p q d b n", q=4, d=2, b=nseq))
                tt("pool", XF[0][:, :, :, 1:n + 1], v5(t1)[:, :, 0], v5(t2)[:, :, 0], ALU.subtract, [t1.k, t2.k], [XF[0].k])
                tt("pool", XB[0][:, :, :, 0:n], v5(t1)[:, :, 1], v5(t2)[:, :, 1], ALU.subtract, [t1.k, t2.k], [XB[0].k])
                tt("pool", XF[1][:, :, :, 1:n + 1], v5(t3)[:, :, 0], v5(g0)[:, :, 0], ALU.add, [t3.k, g0.k], [XF[1].k])
                tt("pool", XB[1][:, :, :, 0:n], v5(t3)[:, :, 1], v5(g0)[:, :, 1], ALU.add, [t3.k, g0.k], [XB[1].k])
                if init is not None:
                    for r_ in range(2):
                        sv_ = init["S"][r_][:, 8 * i:8 * i + 8].rearrange("p (q d) -> p q d", d=2)
                        cp("dve", XF[r_][:, :, 0, 0], sv_[:, :, 0], [init["S"][r_].k], [XF[r_].k])
                        cp("dve", XB[r_][:, :, 0, n], sv_[:, :, 1], [init["S"][r_].k], [XB[r_].k])
                if rname == "p":
                    so = lambda t_: t_.a.rearrange("p (i q d b) -> p i q d b", i=6, q=4, d=2)
                    tt("dve", so(stout[0])[:, i, :, 0, :], v5(t1)[:, :, 0, :, n - 1], v5(t2)[:, :, 0, :, n - 1], ALU.subtract, [t1.k, t2.k], [stout[0].k])
                    tt("dve", so(stout[0])[:, i, :, 1, :], v5(t1)[:, :, 1, :, 0], v5(t2)[:, :, 1, :, 0], ALU.subtract, [t1.k, t2.k], [stout[0].k])
                    tt("dve", so(stout[1])[:, i, :, 0, :], v5(t3)[:, :, 0, :, n - 1], v5(g0)[:, :, 0, :, n - 1], ALU.add, [t3.k, g0.k], [stout[1].k])
                    tt("dve", so(stout[1])[:, i, :, 1, :], v5(t3)[:, :, 1, :, 0], v5(g0)[:, :, 1, :, 0], ALU.add, [t3.k, g0.k], [stout[1].k])
                kt3 = ktw.a.rearrange("p (s m) -> p s m", s=15)
                yc6 = ycw.a.rearrange("p (q d s r m) -> p q d s r m", q=4, d=2, s=8, r=2)
                for (st, nt) in spans:
                    pt, pk = P.ps()
                    ncs = nt // 8
                    nb = max(1, ncs // n)
                    cpb = ncs // nb
                    zb = zi[:, st:st + nt]
                    zbv = zb.rearrange("p (c s) -> p c s", s=8)
                    pv = pt[:, :nt].rearrange("p (c s) -> p c s", s=8)
                    mm(pt[:, :nt], kt3[:, 0, :], zb, True, False, [ktw.k, zi.k], pk, signal=False)
                    for dl in range(1, 8):
                        mm(pv[:, :, dl:8], kt3[:, dl, :], zbv[:, :, 0:8 - dl], False, False, [ktw.k, zi.k], pk, signal=False)
                        mm(pv[:, :, 0:8 - dl], kt3[:, 7 + dl, :], zbv[:, :, dl:8], False, False, [ktw.k, zi.k], pk, signal=False)
                    pv4 = pt[:, :nt].rearrange("p (b c s) -> p b c s", b=nb, s=8)
                    cnt_ = 0
                    for d in range(2):
                        for s in range(8):
                            for r_ in range(2):
                                for q in range(4):
                                    rows = slice(32 * q, 32 * q + 32)
                                    Xt = (XF if d == 0 else XB)[r_]
                                    if nseq > 1:
                                        b0 = (st // 8) // n
                                        c0 = 0 if d == 0 else 1
                                        rhs = Xt[:, q, b0:b0 + nb, c0:c0 + cpb]
                                    else:
                                        c0 = st // 8 + (0 if d == 0 else 1)
                                        rhs = Xt[:, q, 0:1, c0:c0 + ncs]
                                    cnt_ += 1
                                    mm(pv4[rows, :, :, s], yc6[:, q, d, s, r_, :], rhs, False, cnt_ == 128, [ycw.k, Xt.k], pk,
                                       signal=(cnt_ == 128), tp=(0, 32 * q))
                    act(gT[:, i, st:st + nt], pt[:, :nt], AF.Gelu_apprx_tanh, [pk], [gT.k])

    def glu_wout_post(rg, gT, catT, spans=((0, 512), (512, 512))):
        off, c = rg["off"], rg["cond"]
        with P.scope():
            wg = T(P, "wglu", [128, 6, 768], BF16)
            wload_bf16(wg.a, wg.k, D["w_glu"].rearrange("(kt p) m -> p kt m", p=128))
            wo = T(P, "wout", [128, 8, 1024], BF16)
            wload_bf16(wo.a, wo.k, D["w_out0"].rearrange("(kt p) m -> p kt m", p=128))
            sg = [T(P, "sg%d" % b, [128, 512], F32) for b in range(2)]
            yf = T(P, "yf", [128, 8, 512], F32)
            ns = NormScratch()
            for (st, nt) in spans:
                bs_ = slice(st, st + nt)
                for mt in range(6):
                    pt, pk = P.ps()
                    for kt in range(6):
                        mm(pt[:, :nt], wg[:, kt, mt * 128:(mt + 1) * 128], gT[:, kt, bs_], kt == 0, kt == 5, [wg.k, gT.k], pk, signal=(kt == 5))
                    sb_ = sg[mt % 2]
                    act(sb_[:, :nt], pt[:, :nt], AF.Sigmoid, [pk, cmisc.k], [sb_.k], bias=bglu(mt))
                    tt("dve", catT[:, mt, bs_], gT[:, mt, bs_], sb_[:, :nt], ALU.mult, [gT.k, sb_.k], [catT.k])
            dbg_out("d_cat_" + rg["name"], catT.a.rearrange("p a b -> p (a b)"), catT.k)
            for (st, nt) in spans:
                bs_ = slice(st, st + nt)
                for mt in range(8):
                    pt, pk = P.ps()
                    for kt in range(8):
                        mm(pt[:, :nt], wo[:, kt, mt * 128:(mt + 1) * 128], catT[:, kt, bs_], kt == 0, kt == 7, [wo.k, catT.k], pk, signal=(kt == 7))
                    act(yf[:, mt, :nt], pt[:, :nt], AF.Identity, [pk], [yf.k])
                xs_ = slice(off + st, off + st + nt)
                post_residual(lambda dt: yf[:, dt, :nt], [yf.k], 0, 2, c, lambda dt: xres[:, dt, xs_], [xk[0][st // 512]], nt, ns)


    def l0_mixer(rg):
        ri = 0 if rg["name"] == "p" else 1
        off, c, nseq, n = rg["off"], rg["cond"], rg["nseq"], rg["n"]
        with P.scope():
            hT = T(P, "hT", [128, 8, 1024], BF16)
            hk = [Tok(), Tok()]
            catT = T(P, "catT", [128, 8, 1024], BF16)
            gT = T(P, "gT", [128, 6, 1024], BF16)
            winb = [T(P, "winb%d" % b, [128, 8, 128], BF16) for b in range(2)]
            nwin = [0]
            win_src = D["w_in0"].rearrange("(kt p) m -> p kt m", p=128)

            def get_win(col_tile):
                wb = winb[nwin[0] % 2]
                nwin[0] += 1
                wload_bf16(wb.a, wb.k, win_src[:, :, col_tile * 128:(col_tile + 1) * 128])
                return wb

            with P.scope():
                ns = NormScratch()
                for blk in range(2):
                    sl = slice(off + blk * 512, off + (blk + 1) * 512)
                    modnorm(lambda dt: xres[:, dt, sl], [xk[ri][blk]], 0, 0, 1, c,
                            lambda dt: hT[:, dt, blk * 512:(blk + 1) * 512], hk[blk], 512, ns)
            dbg_out("d_hT_" + rg["name"], hT.a.rearrange("p a b -> p (a b)"), hk[1])

            ckpt(2.2)
            with P.scope():
                fcf = T(P, "fcf", [128, 4, 128], F32)
                load(fcf.a, fcf.k, D["fconst"])
                fcb = T(P, "fcb", [128, 4, 128], BF16)
                cp("dve", fcb.a, fcf.a, [fcf.k], [fcb.k])
                ctf = T(P, "ctf", [128, 2, 2, 256], F32)
                load(ctf.a, ctf.k, D["ctp"])
                ctb = T(P, "ctb", [128, 2, 2, 256], BF16)
                cp("dve", ctb.a, ctf.a, [ctf.k], [ctb.k])
                zfT = T(P, "zfT", [128, 2, 1024], BF16)
                PQ = T(P, "PQ", [128, 8, 512], BF16)
                ZT = T(P, "ZT", [128, 2, 1024], BF16)
                for jt in range(2):
                    wb = get_win(6 + jt)
                    for blk in range(2):
                        pt, pk = P.ps()
                        for kt in range(8):
                            mm(pt[:, :], wb[:, kt, :], hT[:, kt, blk * 512:(blk + 1) * 512], kt == 0, kt == 7, [wb.k, hk[blk]], pk, signal=(kt == 7))
                        act(zfT[:, jt, blk * 512:(blk + 1) * 512], pt[:, :], AF.Identity, [pk], [zfT.k])
                for t8 in range(8):
                    pt, pk = P.ps()
                    tsl = slice(t8 * 128, (t8 + 1) * 128)
                    for cs in range(2):
                        for jt in range(2):
                            col = cs * 256 + jt * 128
                            mm(pt[:, col:col + 128], zfT[:, jt, tsl], fcb[:, cs, :], True, True, [zfT.k, fcb.k], pk, signal=(cs == 1 and jt == 1))
                    cp("dve", PQ[:, t8, :], pt[:, :], [pk], [PQ.k])
                if rg["name"] == "p":
                    for jt in range(2):
                        for blk in range(2):
                            pt, pk = P.ps()
                            for b2 in range(2):
                                kk = 0
                                for ttl in range(2):
                                    t8 = (blk * 2 + b2) * 2 + ttl
                                    for cs in range(2):
                                        mm(pt[:, b2 * 256:(b2 + 1) * 256], PQ[:, t8, cs * 256 + jt * 128: cs * 256 + (jt + 1) * 128], ctb[:, cs, ttl, :],
                                           kk == 0, kk == 3, [PQ.k, ctb.k], pk, signal=(kk == 3))
                                        kk += 1
                            act(ZT[:, jt, blk * 512:(blk + 1) * 512], pt[:, :], AF.Identity, [pk], [ZT.k])
                for jt in range(2):
                    for blk in range(2):
                        pt, pk = P.ps()
                        mm(pt[:, :], fcb[:, 2 + jt, :], ZT[:, jt, blk * 512:(blk + 1) * 512], True, True, [fcb.k, ZT.k], pk, signal=True)
                        act(catT[:, 6 + jt, blk * 512:(blk + 1) * 512], pt[:, :], AF.Identity, [pk, cmisc.k], [catT.k], bias=fnetb(jt))
            dbg_out("d_yb_" + rg["name"], catT[:, 6:8, :].rearrange("p a b -> p (a b)"), catT.k)

            ckpt(2.3)
            s5_section(rg["name"], nseq, n, hT, hk, get_win, gT, "full", None, None)
            dbg_out("d_g_" + rg["name"], gT.a.rearrange("p a b -> p (a b)"), gT.k)

            ckpt(2.6)
            glu_wout_post(rg, gT, catT)

    ckpt(2.01)
    load_region(REG[0])
    ckpt(2.05)
    l0_mixer(REG[0])
    if "d_x1_p" in dbg:
        store(D["d_x1_p"].rearrange("p (a b) -> p a b", a=8), xres[:, :, 0:1024], xk[0][1])
    for nm, t_ in (("st_re", stout[0]), ("st_im", stout[1])):
        store(D[nm].rearrange("p a b -> p (a b)"), t_.a, t_.k)
    ckpt(3.0)

    def ffn(l, rg, spans=((0, 512), (512, 512)), xbuf=None, xtoks=None):
        off, c = rg["off"], rg["cond"]
        xb_ = xres if xbuf is None else xbuf
        w1src = D["w_ff1_%d" % l].rearrange("(kt p) j -> p kt j", p=128)
        w2src = D["w_ff2_%d" % l].rearrange("(jt p) m -> p jt m", p=128)
        with P.scope():
            ns = NormScratch()
            h2 = T(P, "h2", [128, 8, 512], BF16)
            hid = T(P, "hid", [128, 32, 512], BF16)
            w1s = [T(P, "w1s%d" % b, [128, 8, 512], BF16) for b in range(4)]
            w2s = [T(P, "w2s%d" % b, [128, 32, 128], BF16) for b in range(4)]
            rl = [T(P, "rl%d" % b, [128, 512], F32) for b in range(2)]
            yf = T(P, "yff", [128, 8, 512], F32)
            for (st, nt) in spans:
                xs_ = slice(off + st, off + st + nt)
                xkk = [xk[0][st // 512]] if xtoks is None else xtoks
                modnorm(lambda dt: xb_[:, dt, xs_], xkk, l, 3, 4, c, lambda dt: h2[:, dt, :nt], h2.k, nt, ns)
                for jg in range(8):
                    wb = w1s[jg % 4]
                    wload_bf16(wb.a, wb.k, w1src[:, :, jg * 512:(jg + 1) * 512])
                    for j4 in range(4):
                        jt = jg * 4 + j4
                        pt, pk = P.ps()
                        for kt in range(8):
                            mm(pt[:, :nt], wb[:, kt, j4 * 128:(j4 + 1) * 128], h2[:, kt, :nt], kt == 0, kt == 7, [wb.k, h2.k], pk, signal=(kt == 7))
                        rb = rl[jt % 2]
                        act(rb[:, :nt], pt[:, :nt], AF.Relu, [pk], [rb.k])
                        tt("dve", hid[:, jt, :nt], rb[:, :nt], rb[:, :nt], ALU.mult, [rb.k], [hid.k])
                for mt in range(8):
                    wb = w2s[mt % 4]
                    wload_bf16(wb.a, wb.k, w2src[:, :, mt * 128:(mt + 1) * 128])
                    pt, pk = P.ps()
                    for jt in range(32):
                        mm(pt[:, :nt], wb[:, jt, :], hid[:, jt, :nt], jt == 0, jt == 31, [wb.k, hid.k], pk, signal=(jt == 31))
                    act(yf[:, mt, :nt], pt[:, :nt], AF.Identity, [pk], [yf.k])
                post_residual(lambda dt: yf[:, dt, :nt], [yf.k], l, 5, c, lambda dt: xb_[:, dt, xs_], xkk, nt, ns)

    ffn(0, REG[0])
    if "d_x2_p" in dbg:
        store(D["d_x2_p"].rearrange("p (a b) -> p a b", a=8), xres[:, :, 0:1024], xk[0][1])
    ckpt(4.0)

    def l1_mixer(rg, Uall=None, tile_base=0, tps_all=None, own=False):
        ri = 0 if rg["name"] == "p" else 1
        off, c, nseq = rg["off"], rg["cond"], rg["nseq"]
        w1 = D["w_in1"].rearrange("(kt p) m -> p kt m", p=128)
        with P.scope():
            cat1 = T(P, "cat1", [128, 8, 1024], BF16)
            wo = T(P, "wout1", [128, 8, 1024], BF16)
            wload_bf16(wo.a, wo.k, D["w_out1"].rearrange("(kt p) m -> p kt m", p=128))
            with P.scope():
                hT = T(P, "h1T", [128, 8, 1024], BF16)
                hk = [Tok(), Tok()]
                with P.scope():
                    ns = NormScratch()
                    for blk in range(2):
                        sl = slice(off + blk * 512, off + (blk + 1) * 512)
                        modnorm(lambda dt: xres[:, dt, sl], [xk[ri][blk]], 1, 0, 1, c,
                                lambda dt: hT[:, dt, blk * 512:(blk + 1) * 512], hk[blk], 512, ns)
                wg = T(P, "w1g", [128, 8, 512], BF16); wv = T(P, "w1v", [128, 8, 512], BF16)
                if Uall is None:
                    wu = T(P, "w1u", [128, 8, 512], BF16)
                    wload_bf16(wu.a, wu.k, w1[:, :, 0:512])
                wload_bf16(wg.a, wg.k, w1[:, :, 512:1024])
                wload_bf16(wv.a, wv.k, w1[:, :, 1024:1536])
                pmf = T(P, "pmf", [128, 10, 4, 128], F32)
                load(pmf.a, pmf.k, D["pmat"])
                pmb = T(P, "pmb", [128, 10, 4, 128], BF16)
                cp("dve", pmb.a, pmf.a, [pmf.k], [pmb.k])
                wst = T(P, "wst", [128, 4, 128], BF16)
                wload_bf16(wst.a, wst.k, D["wsT"])
                pwb = T(P, "pwb", [128, 4, 128], BF16)
                wload_bf16(pwb.a, pwb.k, D["pool_w"].rearrange("g c d -> c g d"))
                lnb = T(P, "lnb", [128, 3, 512], F32)
                for k3 in range(3):
                    load(lnb[:, k3, :], lnb.k, D["lnv"][k3].partition_broadcast(128))
                Utm = T(P, "Utm", [128, 8, 512], BF16) if Uall is None else None
                vn = T(P, "vn", [128, 8, 512], BF16)
                uT = T(P, "uT", [128, 4, 1024], BF16)
                pT = T(P, "pT", [128, 4, 1024], BF16)
                gv = [T(P, "gv%d" % b, [128, 512], F32) for b in range(2)]
                st6 = T(P, "st6", [128, 4, 6], F32)
                mv = T(P, "mv", [128, 4, 2], F32)
                rs4 = T(P, "rs4", [128, 4], F32)
                for t8 in range(8):
                    tsl = slice(t8 * 128, (t8 + 1) * 128)
                    hkk = hk[t8 // 4]
                    if Uall is None:
                        pt, pk = P.ps()
                        for kt in range(8):
                            mm(pt[:, :], hT[:, kt, tsl], wu[:, kt, :], kt == 0, kt == 7, [hkk, wu.k], pk, signal=(kt == 7))
                        cp("dve", Utm[:, t8, :], pt[:, :], [pk], [Utm.k])
                    pt, pk = P.ps()
                    for kt in range(8):
                        mm(pt[:, :], hT[:, kt, tsl], wv[:, kt, :], kt == 0, kt == 7, [hkk, wv.k], pk, signal=(kt == 7))
                    g_ = gv[t8 % 2]
                    act(g_.a, pt[:, :], AF.Gelu_apprx_tanh, [pk], [g_.k])
                    for h in range(4):
                        op("dve", lambda e, g_=g_, h=h: e.bn_stats(out=st6[:, h, :], in_=g_[:, h * 128:(h + 1) * 128]), reads=[g_.k], writes=[st6.k])
                    for h in range(4):
                        op("dve", lambda e, h=h: e.bn_aggr(out=mv[:, h, :], in_=st6[:, h, :]), reads=[st6.k], writes=[mv.k])
                    act(rs4.a, mv[:, :, 1], AF.Sqrt, [mv.k, epsc.k], [rs4.k], bias=epsc.a)
                    recip(rs4.a, rs4.a, [rs4.k], [rs4.k])
                    for h in range(4):
                        hs = slice(h * 128, (h + 1) * 128)
                        ts("dve", g_[:, hs], g_[:, hs], mv[:, h, 0:1], ALU.subtract, [g_.k, mv.k, rs4.k], [g_.k], s2=rs4[:, h:h + 1], op1=ALU.mult)
                    tt("pool", g_.a, g_.a, lnb[:, 0, :], ALU.mult, [g_.k, lnb.k], [g_.k])
                    tt("pool", vn[:, t8, :], g_.a, lnb[:, 1, :], ALU.add, [g_.k, lnb.k], [vn.k])
                for h in range(4):
                    for blk in range(2):
                        pt, pk = P.ps()
                        for kt in range(8):
                            mm(pt[:, :], wg[:, kt, h * 128:(h + 1) * 128], hT[:, kt, blk * 512:(blk + 1) * 512], kt == 0, kt == 7, [wg.k, hk[blk]], pk, signal=(kt == 7))
                        act(uT[:, h, blk * 512:(blk + 1) * 512], pt[:, :], AF.Gelu_apprx_tanh, [pk], [uT.k])
                tps = (8 // nseq) if tps_all is None else tps_all
                Usrc = Utm if Uall is None else Uall
                for g in range(4):
                    for half in range(2):
                        pt, pk = P.ps()
                        for t4 in range(4):
                            t8 = half * 4 + t4
                            tg = tile_base + t8
                            tl = tg % tps
                            terms = []
                            if own:
                                if tg == 0:
                                    terms = [(31, 7), (0, 5), (1, 1)]
                                elif tg == 7:
                                    terms = [(6, 0), (7, 6), (8, 8)]
                                else:
                                    terms = [(tg - 1, 0), (tg, 2), (tg + 1, 1)]
                            else:
                                if tl > 0:
                                    terms.append((tg - 1, 0))
                                cur = 3 if tl == 0 else (4 if tl == tps - 1 else 2)
                                terms.append((tg, cur))
                                if tl < tps - 1:
                                    terms.append((tg + 1, 1))
                            for kk, (tsrc, slot) in enumerate(terms):
                                mm(pt[:, t4 * 128:(t4 + 1) * 128], Usrc[:, tsrc, g * 128:(g + 1) * 128], pmb[:, slot, g, :], kk == 0, kk == len(terms) - 1,
                                   [Usrc.k, pmb.k], pk, signal=(kk == len(terms) - 1))
                        cp("dve", pT[:, g, half * 512:(half + 1) * 512], pt[:, :], [pk], [pT.k])
                    for half in range(2):
                        pt, pk = P.ps()
                        mm(pt[:, :], pwb[:, g, :], pT[:, g, half * 512:(half + 1) * 512], True, True, [pwb.k, pT.k], pk, signal=True)
                        act(cat1[:, g, half * 512:(half + 1) * 512], pt[:, :], AF.Identity, [pk, cmisc.k], [cat1.k], scale=poolsc(g))
                for t8 in range(8):
                    tsl = slice(t8 * 128, (t8 + 1) * 128)
                    pt, pk = P.ps()
                    for h in range(4):
                        mm(pt[:, h * 128:(h + 1) * 128], vn[:, t8, h * 128:(h + 1) * 128], wst[:, h, :], True, True, [vn.k, wst.k], pk, signal=(h == 3))
                    g_ = gv[t8 % 2]
                    tt("dve", g_.a, pt[:, :], lnb[:, 2, :], ALU.add, [pk, lnb.k], [g_.k])
                    tt("dve", cat1[:, 4:8, tsl], g_.a.rearrange("p (h q) -> p h q", h=4), uT[:, :, tsl], ALU.mult, [g_.k, uT.k], [cat1.k])
            dbg_out("d_cat1_" + rg["name"], cat1.a.rearrange("p a b -> p (a b)"), cat1.k)
            with P.scope():
                yf = T(P, "yf1", [128, 8, 512], F32)
                ns = NormScratch()
                for blk in range(2):
                    bs_ = slice(blk * 512, (blk + 1) * 512)
                    for mt in range(8):
                        pt, pk = P.ps()
                        for kt in range(8):
                            mm(pt[:, :], wo[:, kt, mt * 128:(mt + 1) * 128], cat1[:, kt, bs_], kt == 0, kt == 7, [wo.k, cat1.k], pk, signal=(kt == 7))
                        act(yf[:, mt, :], pt[:, :], AF.Identity, [pk], [yf.k])
                    xs_ = slice(off + blk * 512, off + (blk + 1) * 512)
                    post_residual(lambda dt: yf[:, dt, :], [yf.k], 1, 2, c, lambda dt: xres[:, dt, xs_], [xk[ri][blk]], 512, ns)

    l1_mixer(REG[0])
    if "d_x3_p" in dbg:
        store(D["d_x3_p"].rearrange("p (a b) -> p a b", a=8), xres[:, :, 0:1024], xk[0][1])
    ckpt(5.0)
    ffn(1, REG[0])
    yo = D["ypT"].rearrange("(c p) n -> p c n", p=128)
    for blk in range(2):
        dma("sp", lambda e, blk=blk: e.dma_start(out=yo[:, :, blk * 512:(blk + 1) * 512], in_=xres[:, :, blk * 512:(blk + 1) * 512]),
            reads=[xk[0][blk]], writes=[Tok()])
    ckpt(6.0)

    xsd = D["scr_x"].rearrange("(c p) n -> p c n", p=128)
    xdk = [[Tok(), Tok()] for _ in range(4)]
    SREG = [dict(name="s%d" % q, off=0, cond=1, nseq=1, n=128, q=q) for q in range(4)]

    def xs_load(q):
        for blk in range(2):
            loadr(xres[:, :, blk * 512:(blk + 1) * 512], xk[0][blk], xsd[:, :, q * 1024 + blk * 512: q * 1024 + (blk + 1) * 512], xdk[q][blk])

    def xs_store(q, dst3=None):
        d3 = xsd if dst3 is None else dst3
        for blk in range(2):
            dma("sp", lambda e, blk=blk: e.dma_start(out=d3[:, :, q * 1024 + blk * 512: q * 1024 + (blk + 1) * 512], in_=xres[:, :, blk * 512:(blk + 1) * 512]),
                reads=[xk[0][blk]], writes=[xdk[q][blk]])

    def make_get_win(winb):
        nwin = [0]
        win_src = D["w_in0"].rearrange("(kt p) m -> p kt m", p=128)

        def get_win(col_tile):
            wb = winb[nwin[0] % 2]
            nwin[0] += 1
            wload_bf16(wb.a, wb.k, win_src[:, :, col_tile * 128:(col_tile + 1) * 128])
            return wb
        return get_win

    with P.scope():
        ybT = T(P, "ybTall", [128, 2, 2048], BF16)
        Ffin = [[T(P, "Ff%d_%d" % (q, r_), [128, 48], F32) for r_ in range(2)] for q in range(4)]
        Sin = [[T(P, "Si%d_%d" % (q, r_), [128, 48], F32) for r_ in range(2)] for q in range(4)]
        Gin = [[T(P, "Gi%d_%d" % (q, r_), [128, 48], F32) for r_ in range(2)] for q in range(4)]
        fcb = T(P, "fcb_s", [128, 4, 128], BF16)
        with P.scope():
            fcf = T(P, "fcf_s", [128, 4, 128], F32)
            load(fcf.a, fcf.k, D["fconst"])
            cp("dve", fcb.a, fcf.a, [fcf.k], [fcb.k])
        xs3 = D["xsT"].rearrange("(c p) n -> p c n", p=128)
        pos3 = D["posT"].rearrange("(c p) n -> p c n", p=128)
        pqscope = P.scope()
        pqscope.__enter__()
        PQall = T(P, "PQall", [128, 32, 512], BF16)
        for q in range(4):
            with P.scope():
                with P.scope():
                    for blk in range(2):
                        cs_ = slice(q * 1024 + blk * 512, q * 1024 + (blk + 1) * 512)
                        load(xres[:, :, blk * 512:(blk + 1) * 512], xk[0][blk], xs3[:, :, cs_])
                        pb = T(P, "posb", [128, 8, 512], F32)
                        load(pb.a, pb.k, pos3[:, :, cs_])
                        dst = xres[:, :, blk * 512:(blk + 1) * 512]
                        tt("dve" if blk == 0 else "pool", dst, dst, pb.a, ALU.add, [xk[0][blk], pb.k], [xk[0][blk]])
                xs_store(q)
                hT = T(P, "hTa", [128, 8, 1024], BF16)
                hk = [Tok(), Tok()]
                winb = [T(P, "winba%d" % b, [128, 8, 128], BF16) for b in range(2)]
                get_win = make_get_win(winb)
                with P.scope():
                    ns = NormScratch()
                    for blk in range(2):
                        sl = slice(blk * 512, (blk + 1) * 512)
                        modnorm(lambda dt: xres[:, dt, sl], [xk[0][blk]], 0, 0, 1, 1, lambda dt: hT[:, dt, sl], hk[blk], 512, ns)
                with P.scope():
                    zfT = T(P, "zfTa", [128, 2, 1024], BF16)
                    for jt in range(2):
                        wb = get_win(6 + jt)
                        for blk in range(2):
                            pt, pk = P.ps()
                            for kt in range(8):
                                mm(pt[:, :], wb[:, kt, :], hT[:, kt, blk * 512:(blk + 1) * 512], kt == 0, kt == 7, [wb.k, hk[blk]], pk, signal=(kt == 7))
                            act(zfT[:, jt, blk * 512:(blk + 1) * 512], pt[:, :], AF.Identity, [pk], [zfT.k])
                    for t8 in range(8):
                        pt, pk = P.ps()
                        tsl = slice(t8 * 128, (t8 + 1) * 128)
                        for cs in range(2):
                            for jt in range(2):
                                col = cs * 256 + jt * 128
                                mm(pt[:, col:col + 128], zfT[:, jt, tsl], fcb[:, cs, :], True, True, [zfT.k, fcb.k], pk, signal=(cs == 1 and jt == 1))
                        cp("dve", PQall[:, 8 * q + t8, :], pt[:, :], [pk], [PQall.k])
                s5_section("s", 1, 128, hT, hk, get_win, None, "finals", None, (Ffin[q][0], Ffin[q][1]))
        with P.scope():
            ZT = T(P, "ZTs", [128, 2, 2048], BF16)
            slab = [T(P, "cts%d" % b, [128, 8, 2, 512], BF16) for b in range(3)]
            nsl = 0
            YB = {0: 0, 1: 1, 2: 2, 3: 7}
            for kb in range(4):
                (p0, k0), (p1, k1) = P.ps(), P.ps()
                for qq in range(4):
                    sb_ = slab[nsl % 3]
                    nsl += 1
                    load(sb_.a, sb_.k, D["cts"][:, 8 * qq:8 * qq + 8, :, kb * 512:(kb + 1) * 512])
                    for t8 in range(8):
                        for cs in range(2):
                            first = (qq == 0 and t8 == 0 and cs == 0)
                            last = (qq == 3 and t8 == 7 and cs == 1)
                            for jt, (pp, kk) in enumerate(((p0, k0), (p1, k1))):
                                mm(pp[:, :], PQall[:, 8 * qq + t8, cs * 256 + jt * 128: cs * 256 + (jt + 1) * 128], sb_[:, t8, cs, :], first, last,
                                   [PQall.k, sb_.k], kk, signal=last)
                act(ZT[:, 0, kb * 512:(kb + 1) * 512], p0[:, :], AF.Identity, [k0], [ZT.k])
                act(ZT[:, 1, kb * 512:(kb + 1) * 512], p1[:, :], AF.Identity, [k1], [ZT.k])
            for jt in range(2):
                for kb in range(4):
                    pt, pk = P.ps()
                    mm(pt[:, :], fcb[:, 2 + jt, :], ZT[:, jt, kb * 512:(kb + 1) * 512], True, True, [fcb.k, ZT.k], pk, signal=True)
                    act(ybT[:, jt, kb * 512:(kb + 1) * 512], pt[:, :], AF.Identity, [pk, cmisc.k], [ybT.k], bias=fnetb(jt))
        pqscope.__exit__(None, None, None)
        with P.scope():
            h0 = T(P, "h0s", [128, 2, 48], F32)
            load(h0.a, h0.k, D["h0s"])
            Are = T(P, "Are", [128, 48], F32); Aim = T(P, "Aim", [128, 48], F32)
            tt("dve", Are.a, lv[:, 5, :], lv[:, 3, :], ALU.mult, [lv.k], [Are.k])
            tt("dve", Aim.a, lv[:, 5, :], lv[:, 4, :], ALU.mult, [lv.k], [Aim.k])
            Rc = T(P, "Rc", [128, 48], F32); Rs = T(P, "Rs", [128, 48], F32)
            ev = lambda ap, d: ap.rearrange("p (a d) -> p a d", d=2)[:, :, d]
            cp("dve", ev(Rc.a, 0), ev(lv[:, 1, :], 0), [lv.k], [Rc.k])
            cp("dve", ev(Rs.a, 0), ev(lv[:, 2, :], 0), [lv.k], [Rs.k])
            cp("dve", ev(Rc.a, 1), ev(lv[:, 3, :], 1), [lv.k], [Rc.k])
            cp("dve", ev(Rs.a, 1), ev(lv[:, 4, :], 1), [lv.k], [Rs.k])
            ca = T(P, "ca", [128, 48], F32); cb_ = T(P, "cb", [128, 48], F32)
            Tr = T(P, "Tr", [128, 48], F32); Ti = T(P, "Ti", [128, 48], F32)
            op("pool", lambda e: e.memset(Tr.a, 0.0), writes=[Tr.k])
            op("pool", lambda e: e.memset(Ti.a, 0.0), writes=[Ti.k])
            for d, visits, mbase in ((0, [0, 1, 2, 3, 0, 1, 2], 0), (1, [3, 2, 1, 0, 3, 2, 1], 4)):
                for k in visits:
                    mcol = meta[:, mbase + k:mbase + k + 1]
                    for r_, Tt in ((0, Tr), (1, Ti)):
                        tt("dve", ev(ca.a, d), ev(h0[:, r_, :], d), ev(Tt.a, d), ALU.subtract, [h0.k, Tt.k], [ca.k])
                        stt(ev(Sin[k][r_].a, d), ev(ca.a, d), mcol, ev(Tt.a, d), ALU.mult, ALU.add, [ca.k, meta.k, Tt.k], [Sin[k][r_].k])
                    sr, si = Sin[k]
                    tt("dve", ev(ca.a, d), ev(Are.a, d), ev(sr.a, d), ALU.mult, [Are.k, sr.k], [ca.k])
                    tt("dve", ev(cb_.a, d), ev(Aim.a, d), ev(si.a, d), ALU.mult, [Aim.k, si.k], [cb_.k])
                    tt("dve", ev(ca.a, d), ev(ca.a, d), ev(cb_.a, d), ALU.subtract, [ca.k, cb_.k], [ca.k])
                    tt("dve", ev(Tr.a, d), ev(ca.a, d), ev(Ffin[k][0].a, d), ALU.add, [ca.k, Ffin[k][0].k], [Tr.k])
                    tt("dve", ev(ca.a, d), ev(Are.a, d), ev(si.a, d), ALU.mult, [Are.k, si.k], [ca.k])
                    tt("dve", ev(cb_.a, d), ev(Aim.a, d), ev(sr.a, d), ALU.mult, [Aim.k, sr.k], [cb_.k])
                    tt("dve", ev(ca.a, d), ev(ca.a, d), ev(cb_.a, d), ALU.add, [ca.k, cb_.k], [ca.k])
                    tt("dve", ev(Ti.a, d), ev(ca.a, d), ev(Ffin[k][1].a, d), ALU.add, [ca.k, Ffin[k][1].k], [Ti.k])
            for q in range(4):
                sr, si = Sin[q]
                tt("dve", ca.a, sr.a, Rc.a, ALU.mult, [sr.k, Rc.k], [ca.k])
                tt("dve", cb_.a, si.a, Rs.a, ALU.mult, [si.k, Rs.k], [cb_.k])
                tt("dve", Gin[q][0].a, ca.a, cb_.a, ALU.subtract, [ca.k, cb_.k], [Gin[q][0].k])
                tt("dve", ca.a, sr.a, Rs.a, ALU.mult, [sr.k, Rs.k], [ca.k])
                tt("dve", cb_.a, si.a, Rc.a, ALU.mult, [si.k, Rc.k], [cb_.k])
                tt("dve", Gin[q][1].a, ca.a, cb_.a, ALU.add, [ca.k, cb_.k], [Gin[q][1].k])
        uk = [Tok() for _ in range(32)]
        SPANS = {0: ((0, 512), (512, 512)), 1: ((0, 128),), 3: ((896, 128),)}
        UT8 = {0: list(range(8)), 1: [0], 3: [7]}
        xh = T(P, "xh", [128, 8, 256], F32)
        HCOL = {1: 0, 3: 128}

        def u_tiles(xb_, xtoks_of, spans, tiles):
            with P.scope():
                hT1 = T(P, "hT1u", [128, 8, 1024], BF16)
                hk1 = [Tok(), Tok()]
                wu = T(P, "w1uu", [128, 8, 512], BF16)
                ustg = [T(P, "ustg%d" % b, [128, 512], BF16) for b in range(2)]
                wload_bf16(wu.a, wu.k, D["w_in1"].rearrange("(kt p) m -> p kt m", p=128)[:, :, 0:512])
                with P.scope():
                    ns = NormScratch()
                    for (st, nt) in spans:
                        sl = slice(st, st + nt)
                        modnorm(lambda dt: xb_[:, dt, sl], xtoks_of(st), 1, 0, 1, 1, lambda dt: hT1[:, dt, sl], hk1[st // 512], nt, ns)
                for n_, (col, tg) in enumerate(tiles):
                    pt, pk = P.ps()
                    for kt in range(8):
                        mm(pt[:, :], hT1[:, kt, col:col + 128], wu[:, kt, :], kt == 0, kt == 7, [hk1[col // 512], wu.k], pk, signal=(kt == 7))
                    ust = ustg[n_ % 2]
                    cp("dve", ust.a, pt[:, :], [pk], [ust.k])
                    dma("sp", lambda e, ust=ust, tg=tg: e.dma_start(out=D["scr_u"][:, tg, :], in_=ust.a), reads=[ust.k], writes=[uk[tg]])

        for q in (1, 3, 0):
            rg = SREG[q]
            spans = SPANS[q]
            xs_load(q)
            with P.scope():
                hT = T(P, "hTb", [128, 8, 1024], BF16)
                hk = [Tok(), Tok()]
                catT = T(P, "catTb", [128, 8, 1024], BF16)
                gT = T(P, "gTb", [128, 6, 1024], BF16)
                winb = [T(P, "winbb%d" % b, [128, 8, 128], BF16) for b in range(2)]
                get_win = make_get_win(winb)
                with P.scope():
                    ns = NormScratch()
                    for blk in range(2):
                        sl = slice(blk * 512, (blk + 1) * 512)
                        modnorm(lambda dt: xres[:, dt, sl], [xk[0][blk]], 0, 0, 1, 1, lambda dt: hT[:, dt, sl], hk[blk], 512, ns)
                s5_section("s", 1, 128, hT, hk, get_win, gT, "full", dict(S=Sin[q], G=Gin[q]), None, spans=spans)
                YOFF = {0: 0, 1: 1024, 3: 1536 - 512}
                for (st, nt) in spans:
                    cp("pool", catT[:, 6:8, st:st + nt], ybT[:, :, YOFF[q] + st:YOFF[q] + st + nt], [ybT.k], [catT.k])
                glu_wout_post(rg, gT, catT, spans=spans)
            if q != 0:
                (st, nt), = spans
                cp("pool", xh[:, :, HCOL[q]:HCOL[q] + 128], xres[:, :, st:st + nt], [xk[0][st // 512]], [xh.k])
                continue
            ffn(0, SREG[1], spans=((0, 256),), xbuf=xh, xtoks=[xh.k])
            u_tiles(xh, lambda st: [xh.k], ((0, 256),), [(0, 8), (128, 31)])
            ffn(0, rg, spans=spans)
            u_tiles(xres, lambda st: [xk[0][st // 512]], spans, [(128 * t8, t8) for t8 in range(8)])
            xs_store(q)
    with P.scope():
        yso = D["ysT"].rearrange("(c p) n -> p c n", p=128)
        Uall = T(P, "Uall", [128, 32, 512], BF16)
        for tg in list(range(9)) + [31]:
            loadr(Uall[:, tg, :], Uall.k, D["scr_u"][:, tg, :], uk[tg])
        rg = SREG[0]
        xs_load(0)
        l1_mixer(rg, Uall=Uall, tile_base=0, tps_all=32, own=True)
        ffn(1, rg)
        xs_store(0, dst3=yso)


_POOL_WINDOWS = (2, 4, 8, 16)


def _fm(v):
    v = np.asarray(v, np.float32)
    return np.ascontiguousarray(v.reshape(-1, 128).T)


def _pos_embed_T():
    rows = 4096 // 64
    rr, cc = np.meshgrid(np.arange(rows, dtype=np.float32), np.arange(64, dtype=np.float32), indexing="ij")
    quarter = 256
    omega = (1.0 / (np.float32(10000.0) ** (np.arange(quarter, dtype=np.float32) / np.float32(quarter)))).astype(np.float32)

    def ax(p):
        ang = p.reshape(-1)[:, None].astype(np.float32) * omega[None, :]
        return np.concatenate([np.sin(ang), np.cos(ang)], axis=-1)
    pe = np.concatenate([ax(rr), ax(cc)], axis=-1).astype(np.float32)
    return np.ascontiguousarray(pe.T)


def _s5_layouts(inp):
    lre = np.asarray(inp["l0_s5_lambda_re"], np.float32)
    lim = np.asarray(inp["l0_s5_lambda_im"], np.float32)
    ldt = np.asarray(inp["l0_s5_log_dt"], np.float32)
    bre = np.asarray(inp["l0_s5_b_re"], np.float32)
    bim = np.asarray(inp["l0_s5_b_im"], np.float32)
    cre = np.asarray(inp["l0_s5_c_re"], np.float32)
    cim = np.asarray(inp["l0_s5_c_im"], np.float32)

    def g6(a):
        return a.reshape((2, 6, 4, 2) + a.shape[2:])

    def lc_state(a):
        x = np.transpose(g6(a), (3, 4, 1, 2, 0))
        return np.repeat(x[..., None], 16, axis=-1)
    ldt3 = np.repeat(ldt[:, :, None], 64, axis=2)
    lc = [lc_state(lre), lc_state(lim), lc_state(ldt3)]
    c6 = lambda a: np.transpose(g6(a), (3, 5, 1, 2, 0, 4))
    b6 = lambda a: np.transpose(g6(a), (3, 4, 1, 2, 0, 5))
    lc += [c6(cre), c6(cim), b6(bre), b6(bim)]
    s5C = np.stack([x.reshape(128, 768) for x in lc], axis=1).astype(np.float32)

    def lb_state(a):
        x = np.transpose(g6(a), (2, 3, 1, 0, 4))
        return np.repeat(x[:, :, None], 16, axis=2)
    lb = [lb_state(lre), lb_state(lim), lb_state(ldt3)]
    bb = lambda a: np.transpose(g6(a), (2, 3, 5, 1, 0, 4))
    lb += [bb(bre), bb(bim)]
    s5B = np.stack([x.reshape(128, 768) for x in lb], axis=1).astype(np.float32)
    return np.ascontiguousarray(s5C), np.ascontiguousarray(s5B)


def _lv_layout(a):
    x = np.asarray(a, np.float32).reshape(2, 6, 4, 2, 64)
    return np.ascontiguousarray(np.transpose(x, (3, 4, 1, 2, 0)).reshape(128, 48))


def _band(w, kind):
    h = w // 2
    M = np.zeros((128, 128), np.float64)
    for t in range(128):
        lo, hi = t - h, t + h
        cnt = float(w)
        if kind == "first":
            cnt = float(hi - max(lo, 0))
        if kind == "last":
            cnt = float(min(hi, 128) - lo)
        if kind in ("mid", "first", "last"):
            for tp in range(max(lo, 0), min(hi, 128)):
                M[tp, t] += 1.0 / cnt
            M[t, t] -= 1.0
        elif kind == "prev":
            for tp in range(128):
                if lo <= tp - 128 < hi:
                    M[tp, t] += 1.0 / cnt
        elif kind == "next":
            for tp in range(128):
                if lo <= tp + 128 < hi:
                    M[tp, t] += 1.0 / cnt
    return M.astype(np.float32)


_ALL_INPUTS = (
    "x_prompt", "x_sample", "state_l0_s5_re", "state_l0_s5_im", "c", "c_ctx",
    "l0_w_mod", "l0_b_mod", "l0_g_mix_pre", "l0_g_mix_post", "l0_g_ff_pre", "l0_g_ff_post", "l0_w_ff1", "l0_w_ff2",
    "l0_w_in", "l0_w_out", "l0_s5_lambda_re", "l0_s5_lambda_im", "l0_s5_log_dt", "l0_s5_b_re", "l0_s5_b_im",
    "l0_s5_c_re", "l0_s5_c_im", "l0_s5_d", "l0_s5_w_glu", "l0_s5_b_glu", "l0_fnet_w", "l0_fnet_b",
    "l1_w_mod", "l1_b_mod", "l1_g_mix_pre", "l1_g_mix_post", "l1_g_ff_pre", "l1_g_ff_post", "l1_w_ff1", "l1_w_ff2",
    "l1_w_in", "l1_w_out", "l1_pool_w", "l1_pool_scale", "l1_gmlp_ln_g", "l1_gmlp_ln_b", "l1_gmlp_ws", "l1_gmlp_bs",
)


def _host_inputs(inp):
    for _n in _ALL_INPUTS:
        assert _n in inp, _n
    f32 = lambda a: np.ascontiguousarray(np.asarray(a, np.float32))
    shared = {}
    for l in range(2):
        shared["w_mod%d" % l] = f32(inp["l%d_w_mod" % l])
        shared["w_ff1_%d" % l] = f32(inp["l%d_w_ff1" % l])
        shared["w_ff2_%d" % l] = f32(inp["l%d_w_ff2" % l])
        shared["w_out%d" % l] = f32(inp["l%d_w_out" % l])
    shared["w_in0"] = f32(inp["l0_w_in"])
    shared["w_in1"] = f32(inp["l1_w_in"])
    shared["w_glu"] = f32(inp["l0_s5_w_glu"])
    shared["pool_w"] = f32(inp["l1_pool_w"])
    shared["wsT"] = np.ascontiguousarray(np.transpose(np.asarray(inp["l1_gmlp_ws"], np.float32), (2, 0, 1)))
    shared["lnv"] = np.ascontiguousarray(np.stack([np.asarray(inp["l1_gmlp_ln_g"], np.float32), np.asarray(inp["l1_gmlp_ln_b"], np.float32),
                                                   np.asarray(inp["l1_gmlp_bs"], np.float32).reshape(512)], axis=0))
    shared["bmod"] = np.concatenate([_fm(inp["l%d_b_mod" % l]) for l in range(2)], axis=1)
    gl = []
    for l in range(2):
        for k in ("g_mix_pre", "g_mix_post", "g_ff_pre", "g_ff_post"):
            gl.append(_fm(inp["l%d_%s" % (l, k)]))
    shared["gains"] = np.concatenate(gl, axis=1)
    s5C, s5B = _s5_layouts(inp)
    shared["s5C"], shared["s5B"] = s5C, s5B
    cm = np.zeros((128, 64), np.float32)
    pidx = np.arange(128)
    jj = (pidx // 16) % 2
    jj_c = pidx // 64
    cm[:, 0] = (jj == 0); cm[:, 1] = (jj == 1)
    for q in range(4):
        cm[:, 2 + q] = (pidx // 32 == q)
    cm[:, 6:12] = _fm(inp["l0_s5_d"])
    cm[:, 12:18] = _fm(inp["l0_s5_b_glu"])
    cm[:, 18:20] = _fm(np.asarray(inp["l0_fnet_b"]).reshape(-1))
    cm[:, 20:24] = _fm(inp["l1_pool_scale"])
    cm[:, 24] = (jj_c == 0); cm[:, 25] = (jj_c == 1)
    cm[:, 26] = -1.0 * (jj_c == 0); cm[:, 27] = -1.0 * (jj_c == 1)
    shared["cmisc"] = cm
    shared["ident"] = np.eye(128, dtype=np.float32)
    dm = np.ones((128, 2, 2, 128), np.float32)
    cidx = np.arange(128)
    dm[:, 0, 0, cidx % 32 == 0] = 0.0
    dm[:, 0, 1, cidx % 32 == 31] = 0.0
    dm[:, 1, 0, 0] = 0.0
    dm[:, 1, 1, 127] = 0.0
    shared["dmask"] = dm
    fc = np.zeros((128, 4, 128), np.float32)
    cc = np.arange(64)
    ang = 2 * np.pi * np.outer(cc, cc) / 64.0
    C64 = (np.cos(ang) / 8.0).astype(np.float32); S64 = (np.sin(ang) / 8.0).astype(np.float32)
    fw = np.asarray(inp["l0_fnet_w"], np.float32)
    for g2 in range(2):
        sl = slice(64 * g2, 64 * g2 + 64)
        fc[sl, 0, sl] = C64
        fc[sl, 1, sl] = S64
        fc[sl, 2, sl] = fw[g2]
        fc[sl, 3, sl] = fw[2 + g2]
    shared["fconst"] = fc
    tt_ = np.arange(256)
    angp = 2 * np.pi * np.outer(tt_, tt_) / 256.0
    ctp = np.stack([np.cos(angp) / 16.0, -np.sin(angp) / 16.0], axis=0).astype(np.float32)
    shared["ctp"] = np.ascontiguousarray(np.transpose(ctp.reshape(2, 2, 128, 256), (2, 0, 1, 3)))
    bands = {k: [_band(w, k) for w in _POOL_WINDOWS] for k in ("prev", "next", "mid", "first", "last")}
    posT = _pos_embed_T()
    base = np.arange(4096, dtype=np.float64) * (2 * np.pi / 4096.0)
    ctab = (np.cos(base) / 64.0).astype(np.float32)
    stab = (-np.sin(base) / 64.0).astype(np.float32)
    ctab_bf = ctab.astype(ml_dtypes.bfloat16)
    stab_bf = stab.astype(ml_dtypes.bfloat16)
    xp = np.asarray(inp["x_prompt"], np.float32)
    xs = np.asarray(inp["x_sample"], np.float32)
    cctx = np.asarray(inp["c_ctx"], np.float32)
    cs_ = np.asarray(inp["c"], np.float32)
    sre = np.asarray(inp["state_l0_s5_re"], np.float32)
    sim = np.asarray(inp["state_l0_s5_im"], np.float32)
    maps = []
    for core in range(NCORES):
        s, j = core // 4, core % 4
        m = dict(shared)
        m["xpT"] = np.ascontiguousarray(xp[4 * core:4 * core + 4].reshape(1024, 1024).T)
        order = np.concatenate([np.arange(1024 * ((j + k) % 4), 1024 * ((j + k) % 4) + 1024) for k in range(4)])
        m["xsT"] = np.ascontiguousarray(xs[s][order].T)
        m["posT"] = np.ascontiguousarray(posT[:, order])
        kord = np.concatenate([order[0:1536], order[3584:4096]])
        idx = (order[:, None].astype(np.int64) * kord[None, :].astype(np.int64)) % 4096
        cts = np.empty((128, 32, 2, 2048), ml_dtypes.bfloat16)
        cts[:, :, 0, :] = np.transpose(ctab_bf[idx].reshape(32, 128, 2048), (1, 0, 2))
        cts[:, :, 1, :] = np.transpose(stab_bf[idx].reshape(32, 128, 2048), (1, 0, 2))
        m["cts"] = cts
        cond = np.stack([cctx, cs_[s]], axis=-1)
        m["cond"] = np.ascontiguousarray(np.transpose(cond.reshape(8, 128, 2), (1, 0, 2)).reshape(128, 16))
        m["h0s"] = np.ascontiguousarray(np.stack([_lv_layout(sre[s]), _lv_layout(sim[s])], axis=1))
        meta = np.zeros((128, 16), np.float32)
        for k in range(4):
            meta[:, k] = 1.0 if (j + k) % 4 == 0 else 0.0
            meta[:, 4 + k] = 1.0 if (j + k) % 4 == 3 else 0.0
        m["meta"] = meta
        pm = np.zeros((128, 10, 4, 128), np.float32)
        for gi in range(4):
            pm[:, 0, gi] = bands["prev"][gi]
            pm[:, 1, gi] = bands["next"][gi]
            pm[:, 2, gi] = bands["mid"][gi]
            pm[:, 3, gi] = bands["first"][gi]
            pm[:, 4, gi] = bands["last"][gi]
            pm[:, 5, gi] = bands["first" if j == 0 else "mid"][gi]
            pm[:, 6, gi] = bands["last" if j == 3 else "mid"][gi]
            if j > 0:
                pm[:, 7, gi] = bands["prev"][gi]
            if j < 3:
                pm[:, 8, gi] = bands["next"][gi]
        m["pmat"] = pm
        maps.append(m)
    return maps


_NC_CACHE = {}


def _get_nc(stage=9, dbg=()):
    key = (stage, repr(dbg))
    if key not in _NC_CACHE:
        _NC_CACHE[key] = build(stage, dbg)
    return _NC_CACHE[key]


def _run(inp, stage=9, dbg=()):
    nc = _get_nc(stage, dbg)
    maps = _host_inputs(inp)
    res = run_bass_kernel_spmd(nc, maps, core_ids=list(range(NCORES)))
    return res.results


def kernel(**inp):
    res = _run(inp)
    yp = np.zeros((32, 256, 1024), np.float32)
    ys = np.zeros((2, 4096, 1024), np.float32)
    nre = np.zeros((32, 2, 48, 64), np.float32)
    nim = np.zeros((32, 2, 48, 64), np.float32)
    for core in range(NCORES):
        r = res[core]
        s, j = core // 4, core % 4
        yp[4 * core:4 * core + 4] = np.asarray(r["ypT"]).T.reshape(4, 256, 1024)
        ys[s, 1024 * j:1024 * j + 1024] = np.asarray(r["ysT"]).T
        for name, dst in (("st_re", nre), ("st_im", nim)):
            x = np.asarray(r[name]).reshape(2, 64, 6, 4, 2, 4)
            x = np.transpose(x, (5, 4, 2, 3, 0, 1))
            dst[4 * core:4 * core + 4] = x.reshape(4, 2, 48, 64)
    return (yp, ys, nre, nim)
```

```python
import math
from contextlib import ExitStack, contextmanager
import numpy as np
import ml_dtypes
import concourse.bass as bass
import concourse.mybir as mybir
from concourse.bass_utils import run_bass_kernel_spmd

F32 = mybir.dt.float32
BF16 = mybir.dt.bfloat16
ALU = mybir.AluOpType
AF = mybir.ActivationFunctionType

SAME_ENGINE_SYNC = True
EPS = 1e-6
NCORES = 8


class Tok:
    __slots__ = ("w", "r")

    def __init__(self):
        self.w = None
        self.r = []


class Prog:
    ENGS = ("pe", "act", "dve", "pool", "sp")
    NDMA = {"sp": 24, "pool": 12, "act": 4}

    def __init__(self, nc, es):
        self.nc = nc
        self.scopes = [es]
        self.streams = {e: [] for e in self.ENGS}
        self.cnt = {e: 0 for e in self.ENGS}
        self.sems = {}
        for e in self.ENGS:
            self.sems[e] = es.enter_context(nc.semaphore("s_" + e))
        self.sems["bar"] = es.enter_context(nc.semaphore("s_bar"))
        self.barcnt = 0
        self.dval = {}
        self.dnext = {}
        for q, n in self.NDMA.items():
            self.dnext[q] = 0
            for k in range(n):
                key = "d_%s_%d" % (q, k)
                self.sems[key] = es.enter_context(nc.semaphore(key))
                self.dval[key] = 0
        self.waited = {}
        self.ninstr = 0
        self.uid = 0
        self.psum = []
        self.psum_next = 0
        self.last_unsig = None

    def sb(self, name, shape, dt):
        self.uid += 1
        return self.scopes[-1].enter_context(self.nc.sbuf_tensor("%s_%d" % (name, self.uid), list(shape), dt))

    def init_psum(self):
        for k in range(8):
            t = self.scopes[0].enter_context(self.nc.psum_tensor("psb%d" % k, [128, 512], F32))
            self.psum.append((t, Tok()))

    def ps(self):
        k = self.psum_next
        self.psum_next = (k + 1) % 8
        return self.psum[k]

    @contextmanager
    def scope(self):
        es = ExitStack()
        self.scopes.append(es)
        try:
            yield
        finally:
            self.barrier()
            self.scopes.pop()
            es.close()

    def _need(self, eng, ev, waits):
        if ev is None:
            return
        key, val = ev
        if key == "pe" and eng != "pe" and val > self.cnt["pe"]:
            idx = self.last_unsig
            assert idx is not None and val == self.cnt["pe"] + 1
            ent = self.streams["pe"][idx]
            assert ent[0] == "op" and ent[2] is None
            self.cnt["pe"] += 1
            self.streams["pe"][idx] = ("op", ent[1], self.sems["pe"], 1)
            self.last_unsig = None
        if key == eng:
            if eng in ("pe", "sp"):
                return
            if not SAME_ENGINE_SYNC:
                return
            if val > self.cnt[eng]:
                return
        if val > waits.get(key, 0):
            waits[key] = val

    def _emit_waits(self, eng, waits):
        for key, val in waits.items():
            if self.waited.get((eng, key), 0) >= val:
                continue
            self.waited[(eng, key)] = val
            sem = self.sems[key]
            self.streams[eng].append(("wait", sem, val))

    def _deps(self, eng, reads, writes):
        waits = {}
        for t in reads:
            self._need(eng, t.w, waits)
        for t in writes:
            self._need(eng, t.w, waits)
            for ev in t.r:
                self._need(eng, ev, waits)
        return waits

    def _commit(self, ev, reads, writes):
        for t in writes:
            t.w = ev
            t.r = []
        for t in reads:
            if t not in writes:
                t.r.append(ev)
                if len(t.r) > 48:
                    best = {}
                    for k, v in t.r:
                        if v > best.get(k, 0):
                            best[k] = v
                    t.r = list(best.items())

    def op(self, eng, fn, reads=(), writes=(), signal=True):
        waits = self._deps(eng, reads, writes)
        self._emit_waits(eng, waits)
        self.ninstr += 1
        if signal:
            self.cnt[eng] += 1
            ev = (eng, self.cnt[eng])
            self.streams[eng].append(("op", fn, self.sems[eng], 1))
            if eng == "pe":
                self.last_unsig = None
        else:
            assert eng == "pe"
            ev = (eng, self.cnt[eng] + 1)
            self.streams[eng].append(("op", fn, None, 0))
            self.last_unsig = len(self.streams[eng]) - 1
        self._commit(ev, reads, writes)
        return ev

    def dma(self, q, fn, reads=(), writes=()):
        waits = self._deps(q, reads, writes)
        n = self.NDMA[q]
        k = self.dnext[q]
        self.dnext[q] = (k + 1) % n
        key = "d_%s_%d" % (q, k)
        if self.dval[key] > waits.get(key, 0):
            waits[key] = self.dval[key]
        self._emit_waits(q, waits)
        self.dval[key] += 16
        ev = (key, self.dval[key])
        sem = self.sems[key]
        self.ninstr += 1
        self.streams[q].append(("op", fn, sem, 16))
        self._commit(ev, reads, writes)
        return ev

    def barrier(self):
        assert self.last_unsig is None, "barrier with unsignalled PE work pending"
        for e in ("pe", "act", "dve", "pool"):
            if self.cnt[e] > self.waited.get(("sp", e), 0):
                self.waited[("sp", e)] = self.cnt[e]
                self.streams["sp"].append(("wait", self.sems[e], self.cnt[e]))
        for key, val in self.dval.items():
            if val > self.waited.get(("sp", key), 0):
                self.waited[("sp", key)] = val
                self.streams["sp"].append(("wait", self.sems[key], val))
        self.barcnt += 1
        bs = self.sems["bar"]
        self.streams["sp"].append(("seminc", bs, 1))
        for e in ("pe", "act", "dve", "pool"):
            self.streams[e].append(("wait", bs, self.barcnt))
            for e2 in ("pe", "act", "dve", "pool"):
                self.waited[(e, e2)] = max(self.waited.get((e, e2), 0), self.cnt[e2])
            for key, val in self.dval.items():
                self.waited[(e, key)] = max(self.waited.get((e, key), 0), val)

    def finish(self):
        self.barrier()
        nc = self.nc
        streams = self.streams

        def run(e, lst):
            for ent in lst:
                if ent[0] == "op":
                    ins = ent[1](e)
                    if ent[2] is not None:
                        ins.then_inc(ent[2], ent[3])
                elif ent[0] == "wait":
                    e.wait_ge(ent[1], ent[2])
                else:
                    e.sem_inc(ent[1], ent[2])

        with nc.Block() as block:
            @block.sync
            def _(e):
                run(e, streams["sp"])

            @block.tensor
            def _(e):
                run(e, streams["pe"])

            @block.scalar
            def _(e):
                run(e, streams["act"])

            @block.vector
            def _(e):
                run(e, streams["dve"])

            @block.gpsimd
            def _(e):
                run(e, streams["pool"])


class _Stop(Exception):
    pass


class T:
    def __init__(self, P, name, shape, dt):
        self.t = P.sb(name, shape, dt)
        self.k = Tok()
        self.shape = list(shape)

    def __getitem__(self, idx):
        return self.t[idx]

    @property
    def a(self):
        return self.t[:]


def build(stage=9, dbg=()):
    nc = bass.Bass("TRN2", target_bir_lowering=False)
    D = {}

    def din(name, shape, dt=F32):
        D[name] = nc.dram_tensor(name, list(shape), dt, kind="ExternalInput").ap()
        return D[name]

    def dout(name, shape, dt=F32):
        D[name] = nc.dram_tensor(name, list(shape), dt, kind="ExternalOutput").ap()
        return D[name]

    def dscr(name, shape, dt=F32):
        D[name] = nc.dram_tensor(name, list(shape), dt, kind="Internal").ap()
        return D[name]

    din("xpT", [1024, 1024])
    din("xsT", [1024, 4096])
    din("posT", [1024, 4096])
    din("cond", [128, 16])
    din("bmod", [128, 96])
    din("gains", [128, 64])
    for l in range(2):
        din("w_mod%d" % l, [1024, 6144])
        din("w_ff1_%d" % l, [1024, 4096])
        din("w_ff2_%d" % l, [4096, 1024])
        din("w_out%d" % l, [1024, 1024])
    din("w_in0", [1024, 1024])
    din("w_in1", [1024, 1536])
    din("w_glu", [768, 768])
    din("s5C", [128, 7, 768])
    din("s5B", [128, 5, 768])
    din("cmisc", [128, 64])
    din("ident", [128, 128])
    din("h0s", [128, 2, 48])
    din("fconst", [128, 4, 128])
    din("ctp", [128, 2, 2, 256])
    din("meta", [128, 16])
    din("dmask", [128, 2, 2, 128])
    din("cts", [128, 32, 2, 2048], BF16)
    din("pool_w", [4, 128, 128])
    din("pmat", [128, 10, 4, 128])
    din("wsT", [128, 4, 128])
    din("lnv", [3, 512])
    dout("ypT", [1024, 1024])
    dout("ysT", [1024, 1024])
    dout("st_re", [128, 6, 32])
    dout("st_im", [128, 6, 32])
    for ent in dbg:
        dout(ent[0], ent[1], BF16 if (len(ent) > 2 and ent[2] == 'bf16') else F32)
    dscr("scr_bv", [6, 128, 4096], BF16)
    dscr("scr_yc", [6, 128, 4096], BF16)
    dscr("scr_kt", [6, 128, 1920], BF16)
    dscr("scr_e", [6, 128, 2, 1024], F32)
    dscr("scr_x", [1024, 4096], F32)
    dscr("scr_d", [6, 2, 128, 1024], F32)
    dscr("scr_u", [128, 32, 512], BF16)

    with ExitStack() as es:
        P = Prog(nc, es)
        P.init_psum()
        try:
            _build_body(nc, P, D, stage, set(e[0] for e in dbg))
        except _Stop:
            pass
        P.finish()
    return nc


def _build_body(nc, P, D, stage, dbg):
    op, dma = P.op, P.dma

    def load(dst_ap, dst_tok, src_ap, q="sp"):
        return dma(q, lambda e: e.dma_start(out=dst_ap, in_=src_ap), writes=[dst_tok])

    scr_tok = {}

    def store(dst_ap, src_ap, src_tok, q="sp", key=None):
        tk = Tok()
        if key is not None:
            scr_tok[key] = tk
        return dma(q, lambda e: e.dma_start(out=dst_ap, in_=src_ap), reads=[src_tok], writes=[tk])

    def loadr(dst_ap, dst_tok, src_ap, src_tok, q="sp"):
        return dma(q, lambda e: e.dma_start(out=dst_ap, in_=src_ap), reads=[src_tok], writes=[dst_tok])

    def tt(eng, out, a, b, alu, reads, writes):
        return op(eng, lambda e: e.tensor_tensor(out=out, in0=a, in1=b, op=alu), reads=reads, writes=writes)

    def ts(eng, out, a, s1, op0, reads, writes, s2=None, op1=None):
        if op1 is None:
            return op(eng, lambda e: e.tensor_scalar(out=out, in0=a, scalar1=s1, scalar2=None, op0=op0), reads=reads, writes=writes)
        return op(eng, lambda e: e.tensor_scalar(out=out, in0=a, scalar1=s1, scalar2=s2, op0=op0, op1=op1), reads=reads, writes=writes)

    def stt(out, a, s, b, op0, op1, reads, writes):
        return op("dve", lambda e: e.scalar_tensor_tensor(out=out, in0=a, scalar=s, in1=b, op0=op0, op1=op1), reads=reads, writes=writes)

    def act(out, a, func, reads, writes, scale=None, bias=None):
        kw = {}
        if scale is not None:
            kw["scale"] = scale
        if bias is not None:
            kw["bias"] = bias
        return op("act", lambda e: e.activation(out=out, in_=a, func=func, **kw), reads=reads, writes=writes)

    def mm(out, lhsT, rhs, start, stop, reads, ptok, signal, tp=None):
        kw = {}
        if tp is not None:
            kw["tile_position"] = tp
        return op("pe", lambda e: e.matmul(out, lhsT, rhs, start=start, stop=stop, **kw), reads=reads, writes=[ptok], signal=signal)

    def cp(eng, out, in_, reads, writes):
        return op(eng, lambda e: e.tensor_copy(out=out, in_=in_), reads=reads, writes=writes)

    def recip(out, in_, reads, writes):
        return op("dve", lambda e: e.reciprocal(out=out, in_=in_), reads=reads, writes=writes)

    def ckpt(x):
        if stage < x:
            raise _Stop()

    def dbg_out(name, src_ap, tok):
        if name in dbg:
            store(D[name], src_ap, tok)

    ones_bf = T(P, "ones", [128, 128], BF16)
    op("pool", lambda e: e.memset(ones_bf.a, 1.0), writes=[ones_bf.k])
    epsc = T(P, "epsc", [128, 1], F32)
    op("pool", lambda e: e.memset(epsc.a, EPS), writes=[epsc.k])
    halfpi = T(P, "halfpi", [128, 1], F32)
    op("pool", lambda e: e.memset(halfpi.a, math.pi / 2), writes=[halfpi.k])
    cmisc = T(P, "cmisc", [128, 64], F32)
    load(cmisc.a, cmisc.k, D["cmisc"])
    ident = T(P, "ident", [128, 128], F32)
    load(ident.a, ident.k, D["ident"])
    meta = T(P, "meta", [128, 16], F32)
    load(meta.a, meta.k, D["meta"])
    maskJB = lambda j: cmisc[:, j:j + 1]
    maskJC = lambda j: cmisc[:, 24 + j:25 + j]
    nmaskJC = lambda j: cmisc[:, 26 + j:27 + j]
    maskQ = lambda q: cmisc[:, 2 + q:3 + q]
    dcol = lambda i: cmisc[:, 6 + i:7 + i]
    bglu = lambda i: cmisc[:, 12 + i:13 + i]
    fnetb = lambda jt: cmisc[:, 18 + jt:19 + jt]
    poolsc = lambda g: cmisc[:, 20 + g:21 + g]

    mods = [dict(), dict()]
    mod_all = T(P, "mod_all", [128, 2, 6, 16], F32)
    with P.scope():
        cond = T(P, "cond", [128, 16], F32)
        load(cond.a, cond.k, D["cond"])
        scT = T(P, "scT", [128, 16], BF16)
        act(scT.a, cond.a, AF.Silu, [cond.k], [scT.k])
        bmod = T(P, "bmod", [128, 96], F32)
        load(bmod.a, bmod.k, D["bmod"])
        gains = T(P, "gains", [128, 64], F32)
        load(gains.a, gains.k, D["gains"])
        modv = T(P, "modv", [128, 2, 48, 2], F32)
        wslab = [T(P, "wslab%d" % b, [128, 8, 768], BF16) for b in range(3)]
        nslab = 0
        for l in range(2):
            wsrc = D["w_mod%d" % l].rearrange("(kt p) f -> p kt f", p=128)
            pt, pk = P.ps()
            for sl in range(8):
                wb = wslab[nslab % 3]
                nslab += 1
                dma("pool", lambda e, wb=wb, sl=sl, wsrc=wsrc: e.dma_start(out=wb.a, in_=wsrc[:, :, sl * 768:(sl + 1) * 768]), writes=[wb.k])
                for f6 in range(6):
                    ft = sl * 6 + f6
                    for kt in range(8):
                        mm(pt[:, 2 * ft:2 * ft + 2], wb[:, kt, f6 * 128:(f6 + 1) * 128], scT[:, 2 * kt:2 * kt + 2],
                           kt == 0, kt == 7, [wb.k, scT.k], pk, signal=(kt == 7))
            pv = pt[:, 0:96].rearrange("p (f c) -> p f c", c=2)
            for c in range(2):
                tt("dve", modv[:, l, :, c], pv[:, :, c], bmod[:, l * 48:(l + 1) * 48], ALU.add, [pk, bmod.k], [modv.k])
            gv = lambda kind: gains[:, (l * 4 + kind) * 8:(l * 4 + kind) * 8 + 8]
            mo = lambda m, c: modv[:, l, 8 * m:8 * m + 8, c]
            ma = lambda kind, c: mod_all[:, l, kind, :].rearrange("p (d c) -> p d c", c=2)[:, :, c]
            for c in range(2):
                stt(ma(0, c), mo(1, c), 1.0, gv(0), ALU.add, ALU.mult, [modv.k, gains.k], [mod_all.k])
                cp("dve", ma(1, c), mo(0, c), [modv.k], [mod_all.k])
                tt("dve", ma(2, c), mo(2, c), gv(1), ALU.mult, [modv.k, gains.k], [mod_all.k])
                stt(ma(3, c), mo(4, c), 1.0, gv(2), ALU.add, ALU.mult, [modv.k, gains.k], [mod_all.k])
                cp("dve", ma(4, c), mo(3, c), [modv.k], [mod_all.k])
                tt("dve", ma(5, c), mo(5, c), gv(3), ALU.mult, [modv.k, gains.k], [mod_all.k])
        dbg_out("d_mod", mod_all.a.rearrange("p a b c -> p (a b c)"), mod_all.k)

    def modcol(l, kind, dt, c):
        return mod_all[:, l, kind, 2 * dt + c:2 * dt + c + 1]

    if stage <= 0:
        return

    lv = T(P, "lv", [128, 16, 48], F32)
    NI = 3
    W = NI * 128
    cnt = [0]

    def tmp(name="t", w=None):
        cnt[0] += 1
        return T(P, "%s%d" % (name, cnt[0]), [128, w or W], F32)

    def a_chain(eng, lre, lim, ldt, ktok, npow, bre, bim):
        r = {}
        dt = tmp("dt")
        act(dt.a, ldt, AF.Exp, [ktok], [dt.k])
        lr = tmp("lr"); li = tmp("li")
        tt(eng, lr.a, lre, dt.a, ALU.mult, [ktok, dt.k], [lr.k])
        tt(eng, li.a, lim, dt.a, ALU.mult, [ktok, dt.k], [li.k])
        mag = tmp("mag")
        act(mag.a, lr.a, AF.Exp, [lr.k], [mag.k])
        c = tmp("c"); s = tmp("s")
        act(s.a, li.a, AF.Sin, [li.k], [s.k], scale=0.125)
        act(c.a, li.a, AF.Sin, [li.k], [c.k], scale=-0.125, bias=halfpi.a)
        t1 = tmp("t1"); t2 = tmp("t2")
        for _ in range(3):
            c2 = tmp("c"); s2 = tmp("s")
            tt(eng, t1.a, c.a, c.a, ALU.mult, [c.k], [t1.k])
            tt(eng, t2.a, s.a, s.a, ALU.mult, [s.k], [t2.k])
            tt(eng, s2.a, c.a, s.a, ALU.mult, [c.k, s.k], [s2.k])
            ts(eng, s2.a, s2.a, 2.0, ALU.mult, [s2.k], [s2.k])
            tt(eng, c2.a, t1.a, t2.a, ALU.subtract, [t1.k, t2.k], [c2.k])
            c, s = c2, s2
        r["unit"] = (c, s)
        r["lr"] = lr
        r["li"] = li
        abr = tmp("abr"); abi = tmp("abi")
        tt(eng, abr.a, mag.a, c.a, ALU.mult, [mag.k, c.k], [abr.k])
        tt(eng, abi.a, mag.a, s.a, ALU.mult, [mag.k, s.k], [abi.k])
        r["ab"] = (abr, abi)
        nr = tmp("nr")
        ts(eng, nr.a, abr.a, -1.0, ALU.add, [abr.k], [nr.k])
        den = tmp("den")
        tt(eng, t1.a, lre, lre, ALU.mult, [ktok], [t1.k])
        tt(eng, t2.a, lim, lim, ALU.mult, [ktok], [t2.k])
        tt(eng, den.a, t1.a, t2.a, ALU.add, [t1.k, t2.k], [den.k])
        recip(den.a, den.a, [den.k], [den.k])
        cr = tmp("cr"); ci = tmp("ci")
        tt(eng, t1.a, nr.a, lre, ALU.mult, [nr.k, ktok], [t1.k])
        tt(eng, t2.a, abi.a, lim, ALU.mult, [abi.k, ktok], [t2.k])
        tt(eng, cr.a, t1.a, t2.a, ALU.add, [t1.k, t2.k], [cr.k])
        tt(eng, cr.a, cr.a, den.a, ALU.mult, [cr.k, den.k], [cr.k])
        tt(eng, t1.a, abi.a, lre, ALU.mult, [abi.k, ktok], [t1.k])
        tt(eng, t2.a, nr.a, lim, ALU.mult, [nr.k, ktok], [t2.k])
        tt(eng, ci.a, t1.a, t2.a, ALU.subtract, [t1.k, t2.k], [ci.k])
        tt(eng, ci.a, ci.a, den.a, ALU.mult, [ci.k, den.k], [ci.k])
        Br = tmp("Br"); Bi = tmp("Bi")
        tt(eng, t1.a, cr.a, bre, ALU.mult, [cr.k, ktok], [t1.k])
        tt(eng, t2.a, ci.a, bim, ALU.mult, [ci.k, ktok], [t2.k])
        tt(eng, Br.a, t1.a, t2.a, ALU.subtract, [t1.k, t2.k], [Br.k])
        tt(eng, t1.a, cr.a, bim, ALU.mult, [cr.k, ktok], [t1.k])
        tt(eng, t2.a, ci.a, bre, ALU.mult, [ci.k, ktok], [t2.k])
        tt(eng, Bi.a, t1.a, t2.a, ALU.add, [t1.k, t2.k], [Bi.k])
        r["Bb"] = (Br, Bi)
        pw = {1: (abr, abi)}
        for k in range(2, npow + 1):
            pr = tmp("pr"); pi = tmp("pi")
            p0r, p0i = pw[k - 1]
            tt(eng, t1.a, p0r.a, abr.a, ALU.mult, [p0r.k, abr.k], [t1.k])
            tt(eng, t2.a, p0i.a, abi.a, ALU.mult, [p0i.k, abi.k], [t2.k])
            tt(eng, pr.a, t1.a, t2.a, ALU.subtract, [t1.k, t2.k], [pr.k])
            tt(eng, t1.a, p0r.a, abi.a, ALU.mult, [p0r.k, abi.k], [t1.k])
            tt(eng, t2.a, p0i.a, abr.a, ALU.mult, [p0i.k, abr.k], [t2.k])
            tt(eng, pi.a, t1.a, t2.a, ALU.add, [t1.k, t2.k], [pi.k])
            pw[k] = (pr, pi)
        r["pw"] = pw
        r["t1"], r["t2"] = t1, t2
        return r

    def s5_prep(half):
        i0 = NI * half
        NC_ = NI * 8
        c8k = T(P, "c8k", [128, NC_], F32); s8k = T(P, "s8k", [128, NC_], F32); lrk = T(P, "lrk", [128, NC_], F32); lik = T(P, "lik", [128, NC_], F32)
        ldk = T(P, "ldk", [128, NC_], F32); limk = T(P, "limk", [128, NC_], F32); lrek = T(P, "lrek", [128, NC_], F32)
        with P.scope():
            s5B = T(P, "s5B", [128, 5, W], F32)
            load(s5B.a, s5B.k, D["s5B"][:, :, W * half:W * (half + 1)])
            rb = a_chain("dve", s5B[:, 0, :], s5B[:, 1, :], s5B[:, 2, :], s5B.k, 7, s5B[:, 3, :], s5B[:, 4, :])
            Br, Bi = rb["Bb"]
            bvt = T(P, "bvt", [128, NI * 4096], BF16)
            bv6 = bvt.a.rearrange("p (i d s r j q) -> p i d s r j q", i=NI, d=2, s=8, r=2, j=2)
            wre = tmp("wre"); wim = tmp("wim")
            t1, t2 = rb["t1"], rb["t2"]
            v3 = lambda t: t.a.rearrange("p (i d q) -> p i d q", i=NI, d=2)
            for s in range(8):
                for d in range(2):
                    e = (7 - s) if d == 0 else s
                    sl = lambda t, d=d: v3(t)[:, :, d, :]
                    if e == 0:
                        cp("dve", sl(wre), sl(Br), [Br.k], [wre.k])
                        cp("dve", sl(wim), sl(Bi), [Bi.k], [wim.k])
                    else:
                        pr, pi = rb["pw"][e]
                        tt("dve", sl(t1), sl(pr), sl(Br), ALU.mult, [pr.k, Br.k], [t1.k])
                        tt("dve", sl(t2), sl(pi), sl(Bi), ALU.mult, [pi.k, Bi.k], [t2.k])
                        tt("dve", sl(wre), sl(t1), sl(t2), ALU.subtract, [t1.k, t2.k], [wre.k])
                        tt("dve", sl(t1), sl(pr), sl(Bi), ALU.mult, [pr.k, Bi.k], [t1.k])
                        tt("dve", sl(t2), sl(pi), sl(Br), ALU.mult, [pi.k, Br.k], [t2.k])
                        tt("dve", sl(wim), sl(t1), sl(t2), ALU.add, [t1.k, t2.k], [wim.k])
                for r_, src_ in ((0, wre), (1, wim)):
                    for j in range(2):
                        act(bv6[:, :, :, s, r_, j, :], v3(src_), AF.Copy, [src_.k, cmisc.k], [bvt.k], scale=maskJB(j))
            for i in range(NI):
                store(D["scr_bv"][i0 + i], bvt[:, i * 4096:(i + 1) * 4096], bvt.k, key=("bv", i0 + i))
            if half == 0:
                dbg_out("d_bv0", bvt[:, 0:4096], bvt.k)

        with P.scope():
            s5C = T(P, "s5C", [128, 7, W], F32)
            load(s5C.a, s5C.k, D["s5C"][:, :, W * half:W * (half + 1)])
            rc = a_chain("dve", s5C[:, 0, :], s5C[:, 1, :], s5C[:, 2, :], s5C.k, 8, s5C[:, 5, :], s5C[:, 6, :])
            cre, cim = s5C[:, 3, :], s5C[:, 4, :]
            t1, t2 = rc["t1"], rc["t2"]
            v4 = lambda ap: ap.rearrange("p (i q d h) -> p i q d h", i=NI, q=4, d=2)
            h0v = lambda t: v4(t.a)[:, :, :, :, 0].rearrange("p i q d -> p (i q d)")
            uc, us = rc["unit"]
            cp("dve", c8k.a, h0v(uc), [uc.k], [c8k.k])
            cp("dve", s8k.a, h0v(us), [us.k], [s8k.k])
            cp("dve", ldk.a, v4(s5C[:, 2, :])[:, :, :, :, 0].rearrange("p i q d -> p (i q d)"), [s5C.k], [ldk.k])
            cp("dve", limk.a, v4(s5C[:, 1, :])[:, :, :, :, 0].rearrange("p i q d -> p (i q d)"), [s5C.k], [limk.k])
            cp("dve", lrek.a, v4(s5C[:, 0, :])[:, :, :, :, 0].rearrange("p i q d -> p (i q d)"), [s5C.k], [lrek.k])
            yct = T(P, "yct", [128, NI * 4096], BF16)
            yc7 = yct.a.rearrange("p (i q d s r j h) -> p i q d s r j h", i=NI, q=4, d=2, s=8, r=2, j=2)
            cpr = tmp("cpr"); cpi = tmp("cpi")
            for s in range(8):
                for d in range(2):
                    e = (s + 1) if d == 0 else (8 - s)
                    pr, pi = rc["pw"][e]
                    sl = lambda ap, d=d: v4(ap)[:, :, :, d, :]
                    tt("dve", sl(t1.a), sl(cre), sl(pr.a), ALU.mult, [s5C.k, pr.k], [t1.k])
                    tt("dve", sl(t2.a), sl(cim), sl(pi.a), ALU.mult, [s5C.k, pi.k], [t2.k])
                    tt("dve", sl(cpr.a), sl(t1.a), sl(t2.a), ALU.subtract, [t1.k, t2.k], [cpr.k])
                    tt("dve", sl(t1.a), sl(cre), sl(pi.a), ALU.mult, [s5C.k, pi.k], [t1.k])
                    tt("dve", sl(t2.a), sl(cim), sl(pr.a), ALU.mult, [s5C.k, pr.k], [t2.k])
                    stt(sl(cpi.a), sl(t1.a), -1.0, sl(t2.a), ALU.mult, ALU.subtract, [t1.k, t2.k], [cpi.k])
                for r_, src_ in ((0, cpr), (1, cpi)):
                    for j in range(2):
                        for i in range(NI):
                            act(yc7[:, i, :, :, s, r_, j, :], v4(src_.a)[:, i], AF.Copy, [src_.k, cmisc.k], [yct.k], scale=maskJC(j))
            for i in range(NI):
                store(D["scr_yc"][i0 + i], yct[:, i * 4096:(i + 1) * 4096], yct.k, key=("yc", i0 + i))
            if half == 0:
                dbg_out("d_yc0", yct[:, 0:4096], yct.k)
            Brc, Bic = rc["Bb"]
            Lre = T(P, "Lre", [128, NI * 2048], F32)
            Lim = T(P, "Lim", [128, NI * 2048], F32)
            L6 = lambda t: t.a.rearrange("p (i q d l j h) -> p i q d l j h", i=NI, q=4, d=2, l=8, j=2)
            Rre = T(P, "Rre", [128, NI * 256], F32)
            Rim = T(P, "Rim", [128, NI * 256], F32)
            R5 = lambda t: t.a.rearrange("p (i q d j h) -> p i q d j h", i=NI, q=4, d=2, j=2)
            for j in range(2):
                for i in range(NI):
                    act(R5(Rre)[:, i, :, :, j, :], v4(cre)[:, i], AF.Copy, [s5C.k, cmisc.k], [Rre.k], scale=maskJC(j))
                    act(R5(Rim)[:, i, :, :, j, :], v4(cim)[:, i], AF.Copy, [s5C.k, cmisc.k], [Rim.k], scale=nmaskJC(j))
            lre_t = tmp("lre_t"); lim_t = tmp("lim_t")
            for dlag in range(8):
                if dlag == 0:
                    srcs = (Brc, Bic)
                else:
                    pr, pi = rc["pw"][dlag]
                    tt("dve", t1.a, pr.a, Brc.a, ALU.mult, [pr.k, Brc.k], [t1.k])
                    tt("dve", t2.a, pi.a, Bic.a, ALU.mult, [pi.k, Bic.k], [t2.k])
                    tt("dve", lre_t.a, t1.a, t2.a, ALU.subtract, [t1.k, t2.k], [lre_t.k])
                    tt("dve", t1.a, pr.a, Bic.a, ALU.mult, [pr.k, Bic.k], [t1.k])
                    tt("dve", t2.a, pi.a, Brc.a, ALU.mult, [pi.k, Brc.k], [t2.k])
                    tt("dve", lim_t.a, t1.a, t2.a, ALU.add, [t1.k, t2.k], [lim_t.k])
                    srcs = (lre_t, lim_t)
                for dst, src_ in ((Lre, srcs[0]), (Lim, srcs[1])):
                    for j in range(2):
                        for i in range(NI):
                            act(L6(dst)[:, i, :, :, dlag, j, :], v4(src_.a)[:, i], AF.Copy, [src_.k, cmisc.k], [dst.k], scale=maskJC(j))
            kt_sb = T(P, "kt_sb", [128, NI * 1920], BF16)
            kt4 = kt_sb.a.rearrange("p (i s q c) -> p i s q c", i=NI, s=15, q=4)
            dtmp = T(P, "dtmp", [128, 32], F32)
            for i in range(NI):
                pt, pk = P.ps()
                pv = pt[:, 0:480].rearrange("p (s c) -> p s c", c=32)
                for q in range(4):
                    rows = slice(32 * q, 32 * q + 32)
                    for slot in range(15):
                        if slot == 0:
                            terms = [(0, 0), (1, 0)]
                        elif slot < 8:
                            terms = [(0, slot)]
                        else:
                            terms = [(1, slot - 7)]
                        n = 2 * len(terms)
                        kk = 0
                        for (d, dlag) in terms:
                            for (Lt, Rt) in ((Lre, Rre), (Lim, Rim)):
                                mm(pv[rows, slot, :], L6(Lt)[:, i, q, d, dlag].rearrange("p j h -> p (j h)"),
                                   R5(Rt)[:, i, q, d].rearrange("p j h -> p (j h)"), kk == 0, kk == n - 1, [Lt.k, Rt.k], pk,
                                   signal=(kk == n - 1), tp=(0, 32 * q))
                                kk += 1
                for q2 in range(4):
                    ts("dve", kt4[:, i, 1:15, q2, :], pv[:, 1:15, :], maskQ(q2), ALU.mult, [pk, cmisc.k], [kt_sb.k])
                    ts("dve", dtmp.a, ident[:, 32 * q2:32 * q2 + 32], dcol(i0 + i), ALU.mult, [ident.k, cmisc.k], [dtmp.k])
                    stt(kt4[:, i, 0, q2, :], pv[:, 0, :], maskQ(q2), dtmp.a, ALU.mult, ALU.add, [pk, cmisc.k, dtmp.k], [kt_sb.k])
            for i in range(NI):
                store(D["scr_kt"][i0 + i], kt_sb[:, i * 1920:(i + 1) * 1920], kt_sb.k, key=("kt", i0 + i))
            if half == 0:
                dbg_out("d_kt0", kt_sb[:, 0:1920], kt_sb.k)

        with P.scope():
            a48 = T(P, "a48", [128, NC_], F32); b48 = T(P, "b48", [128, NC_], F32)
            lvs = lambda slot: lv[:, slot, NC_ * half:NC_ * (half + 1)]

            def csq(c, s):
                c2 = T(P, "cq", [128, NC_], F32); s2 = T(P, "sq", [128, NC_], F32)
                tt("dve", a48.a, c.a, c.a, ALU.mult, [c.k], [a48.k])
                tt("dve", b48.a, s.a, s.a, ALU.mult, [s.k], [b48.k])
                tt("dve", s2.a, c.a, s.a, ALU.mult, [c.k, s.k], [s2.k])
                ts("dve", s2.a, s2.a, 2.0, ALU.mult, [s2.k], [s2.k])
                tt("dve", c2.a, a48.a, b48.a, ALU.subtract, [a48.k, b48.k], [c2.k])
                return c2, s2
            MAGIC = 12582912.0

            def sm(name="sm"):
                return T(P, name, [128, NC_], F32)
            nn = sm("nn")
            ts("dve", nn.a, ldk.a, 1.4426950408889634, ALU.mult, [ldk.k], [nn.k])
            ts("dve", nn.a, nn.a, MAGIC, ALU.add, [nn.k], [nn.k])
            ts("dve", nn.a, nn.a, MAGIC, ALU.subtract, [nn.k], [nn.k])
            rx = sm("rx")
            stt(rx.a, nn.a, -0.693359375, ldk.a, ALU.mult, ALU.add, [nn.k, ldk.k], [rx.k])
            stt(rx.a, nn.a, 2.12194440e-4, rx.a, ALU.mult, ALU.add, [nn.k, rx.k], [rx.k])
            er = sm("er")
            ts("dve", er.a, rx.a, 1.0 / 10.0, ALU.mult, [rx.k], [er.k], s2=1.0, op1=ALU.add)
            for n in range(9, 0, -1):
                stt(er.a, er.a, 1.0 / n, rx.a, ALU.mult, ALU.mult, [er.k, rx.k], [er.k])
                ts("dve", er.a, er.a, 1.0, ALU.add, [er.k], [er.k])
            p2 = sm("p2"); p2t = sm("p2t")
            first = True
            for v in range(-12, 0):
                dst = p2 if first else p2t
                ts("dve", dst.a, nn.a, float(v), ALU.is_equal, [nn.k], [dst.k], s2=float(2.0 ** v), op1=ALU.mult)
                if not first:
                    tt("dve", p2.a, p2.a, p2t.a, ALU.add, [p2.k, p2t.k], [p2.k])
                first = False
            dta = sm("dta")
            tt("dve", dta.a, er.a, p2.a, ALU.mult, [er.k, p2.k], [dta.k])
            tt("dve", lik.a, limk.a, dta.a, ALU.mult, [limk.k, dta.k], [lik.k])
            tt("dve", lrk.a, lrek.a, dta.a, ALU.mult, [lrek.k, dta.k], [lrk.k])
            phi = sm("phi")
            ts("dve", phi.a, lik.a, 8.0, ALU.mult, [lik.k], [phi.k])
            kf = sm("kf")
            ts("dve", kf.a, phi.a, 2.0 / math.pi, ALU.mult, [phi.k], [kf.k])
            ts("dve", kf.a, kf.a, MAGIC, ALU.add, [kf.k], [kf.k])
            ts("dve", kf.a, kf.a, MAGIC, ALU.subtract, [kf.k], [kf.k])
            rr = sm("rr")
            stt(rr.a, kf.a, -1.5703125, phi.a, ALU.mult, ALU.add, [kf.k, phi.k], [rr.k])
            stt(rr.a, kf.a, -4.837512969970703e-4, rr.a, ALU.mult, ALU.add, [kf.k, rr.k], [rr.k])
            stt(rr.a, kf.a, -7.549789954891882e-8, rr.a, ALU.mult, ALU.add, [kf.k, rr.k], [rr.k])
            z = sm("z")
            tt("dve", z.a, rr.a, rr.a, ALU.mult, [rr.k], [z.k])
            ps_ = sm("ps")
            ts("dve", ps_.a, z.a, 1.0 / 362880.0, ALU.mult, [z.k], [ps_.k])
            for coef in (-1.0 / 5040.0, 1.0 / 120.0, -1.0 / 6.0):
                stt(ps_.a, ps_.a, coef, z.a, ALU.add, ALU.mult, [ps_.k, z.k], [ps_.k])
            sr = sm("sr")
            stt(sr.a, ps_.a, 1.0, rr.a, ALU.add, ALU.mult, [ps_.k, rr.k], [sr.k])
            pc_ = sm("pc")
            ts("dve", pc_.a, z.a, -1.0 / 3628800.0, ALU.mult, [z.k], [pc_.k])
            for coef in (1.0 / 40320.0, -1.0 / 720.0, 1.0 / 24.0, -0.5):
                stt(pc_.a, pc_.a, coef, z.a, ALU.add, ALU.mult, [pc_.k, z.k], [pc_.k])
            cr_ = sm("cr")
            ts("dve", cr_.a, pc_.a, 1.0, ALU.add, [pc_.k], [cr_.k])
            mq = sm("mq")
            ts("dve", mq.a, kf.a, 0.25, ALU.mult, [kf.k], [mq.k], s2=-0.375, op1=ALU.add)
            ts("dve", mq.a, mq.a, MAGIC, ALU.add, [mq.k], [mq.k])
            ts("dve", mq.a, mq.a, MAGIC, ALU.subtract, [mq.k], [mq.k])
            qd = sm("qd")
            stt(qd.a, mq.a, -4.0, kf.a, ALU.mult, ALU.add, [mq.k, kf.k], [qd.k])
            mA = sm("mA"); mB = sm("mB"); m2 = sm("m2")
            ts("dve", mA.a, qd.a, 0.0, ALU.is_equal, [qd.k], [mA.k])
            ts("dve", m2.a, qd.a, 2.0, ALU.is_equal, [qd.k], [m2.k])
            tt("dve", mA.a, mA.a, m2.a, ALU.subtract, [mA.k, m2.k], [mA.k])
            ts("dve", mB.a, qd.a, 1.0, ALU.is_equal, [qd.k], [mB.k])
            ts("dve", m2.a, qd.a, 3.0, ALU.is_equal, [qd.k], [m2.k])
            tt("dve", mB.a, mB.a, m2.a, ALU.subtract, [mB.k, m2.k], [mB.k])
            c8 = sm("c8"); s8 = sm("s8")
            tt("dve", a48.a, sr.a, mA.a, ALU.mult, [sr.k, mA.k], [a48.k])
            tt("dve", b48.a, cr_.a, mB.a, ALU.mult, [cr_.k, mB.k], [b48.k])
            tt("dve", s8.a, a48.a, b48.a, ALU.add, [a48.k, b48.k], [s8.k])
            tt("dve", a48.a, cr_.a, mA.a, ALU.mult, [cr_.k, mA.k], [a48.k])
            tt("dve", b48.a, sr.a, mB.a, ALU.mult, [sr.k, mB.k], [b48.k])
            tt("dve", c8.a, a48.a, b48.a, ALU.subtract, [a48.k, b48.k], [c8.k])
            x8 = sm("x8")
            ts("dve", x8.a, lrk.a, 8.0, ALU.mult, [lrk.k], [x8.k])
            pe_ = sm("pe")
            ts("dve", pe_.a, x8.a, 1.0 / 10.0, ALU.mult, [x8.k], [pe_.k], s2=1.0, op1=ALU.add)
            for n in range(9, 0, -1):
                stt(pe_.a, pe_.a, 1.0 / n, x8.a, ALU.mult, ALU.mult, [pe_.k, x8.k], [pe_.k])
                ts("dve", pe_.a, pe_.a, 1.0, ALU.add, [pe_.k], [pe_.k])
            cp("dve", lvs(0), pe_.a, [pe_.k], [lv.k])
            Ec = T(P, "Ec", [128, NC_, 128], F32); Es = T(P, "Es", [128, NC_, 128], F32)
            op("pool", lambda e: e.memset(Ec[:, :, 0:1], 1.0), writes=[Ec.k])
            op("pool", lambda e: e.memset(Es[:, :, 0:1], 0.0), writes=[Es.k])
            wc, ws_ = c8, s8
            e1 = T(P, "e1", [128, NC_, 64], F32); e2 = T(P, "e2", [128, NC_, 64], F32)
            for m in range(7):
                n = 1 << m
                bc = lambda t, n=n: t.a.unsqueeze(2).to_broadcast([128, NC_, n])
                lo = slice(0, n); hi = slice(n, 2 * n)
                tt("dve", e1[:, :, lo], Ec[:, :, lo], bc(wc), ALU.mult, [Ec.k, wc.k], [e1.k])
                tt("dve", e2[:, :, lo], Es[:, :, lo], bc(ws_), ALU.mult, [Es.k, ws_.k], [e2.k])
                tt("dve", Ec[:, :, hi], e1[:, :, lo], e2[:, :, lo], ALU.subtract, [e1.k, e2.k], [Ec.k])
                tt("dve", e1[:, :, lo], Ec[:, :, lo], bc(ws_), ALU.mult, [Ec.k, ws_.k], [e1.k])
                tt("dve", e2[:, :, lo], Es[:, :, lo], bc(wc), ALU.mult, [Es.k, wc.k], [e2.k])
                tt("dve", Es[:, :, hi], e1[:, :, lo], e2[:, :, lo], ALU.add, [e1.k, e2.k], [Es.k])
                if m < 6:
                    wc, ws_ = csq(wc, ws_)
            cp("dve", lvs(1), c8.a, [c8.k], [lv.k])
            cp("dve", lvs(2), s8.a, [s8.k], [lv.k])
            tt("dve", a48.a, Ec[:, :, 127], c8.a, ALU.mult, [Ec.k, c8.k], [a48.k])
            tt("dve", b48.a, Es[:, :, 127], s8.a, ALU.mult, [Es.k, s8.k], [b48.k])
            tt("dve", lvs(3), a48.a, b48.a, ALU.subtract, [a48.k, b48.k], [lv.k])
            tt("dve", a48.a, Ec[:, :, 127], s8.a, ALU.mult, [Ec.k, s8.k], [a48.k])
            tt("dve", b48.a, Es[:, :, 127], c8.a, ALU.mult, [Es.k, c8.k], [b48.k])
            tt("dve", lvs(4), a48.a, b48.a, ALU.add, [a48.k, b48.k], [lv.k])
            act(lvs(5), lrk.a, AF.Exp, [lrk.k], [lv.k], scale=1024.0)
            Es2 = Es.a.rearrange("p (a d) c -> p a d c", d=2)
            ts("pool", Es2[:, :, 1, :], Es2[:, :, 1, :], -1.0, ALU.mult, [Es.k], [Es.k])
            dmk = T(P, "dmk", [128, 2, 2, 128], F32)
            load(dmk.a, dmk.k, D["dmask"])
            r8x = T(P, "r8x", [128, NC_, 128], F32)
            cp("dve", r8x.a, lvs(0).unsqueeze(2).to_broadcast([128, NC_, 128]), [lv.k], [r8x.k])
            for ty in range(2):
                dtt = T(P, "dtt%d" % ty, [128, NC_, 128], F32)
                tt("dve", dtt.a.rearrange("p (a d) c -> p a d c", d=2), r8x.a.rearrange("p (a d) c -> p a d c", d=2),
                   dmk[:, ty, :, :].unsqueeze(1).to_broadcast([128, NC_ // 2, 2, 128]), ALU.mult, [r8x.k, dmk.k], [dtt.k])
                for i in range(NI):
                    for d in range(2):
                        store(D["scr_d"][i0 + i, ty][:, d * 512:(d + 1) * 512].rearrange("p (q c) -> p q c", q=4),
                              dtt[:, 8 * i:8 * i + 8, :].rearrange("p (q d) c -> p q d c", d=2)[:, :, d, :], dtt.k, key=("dt%d" % d, i0 + i, ty))
            for i in range(NI):
                store(D["scr_e"][i0 + i, :, 0, :], Ec[:, 8 * i:8 * i + 8, :].rearrange("p a c -> p (a c)"), Ec.k, key=("ec", i0 + i))
                store(D["scr_e"][i0 + i, :, 1, :], Es[:, 8 * i:8 * i + 8, :].rearrange("p a c -> p (a c)"), Es.k, key=("es", i0 + i))
            if half == 0:
                dbg_out("d_ec0", Ec[:, 0:8, :].rearrange("p a c -> p (a c)"), Ec.k)
                dbg_out("d_es0", Es[:, 0:8, :].rearrange("p a c -> p (a c)"), Es.k)

    for half in range(2):
        with P.scope():
            s5_prep(half)
    dbg_out("d_lv", lv.a.rearrange("p a b -> p (a b)"), lv.k)

    if stage <= 1:
        return

    xres = T(P, "xres", [128, 8, 1024], F32)
    _xk0 = [Tok(), Tok()]
    xk = [_xk0, _xk0]
    stout = [T(P, "stout%d" % r_, [128, 192], F32) for r_ in range(2)]
    REG = [dict(name="p", off=0, cond=0, nseq=4, n=32, xsrc=D["xpT"], pos=None),
           dict(name="s", off=1024, cond=1, nseq=1, n=128, xsrc=D["xsT"][:, 0:1024], pos=D["posT"][:, 0:1024])]

    def wload_bf16(dst_ap, dst_tok, src_ap):
        return dma("pool", lambda e: e.dma_start(out=dst_ap, in_=src_ap), writes=[dst_tok])

    class NormScratch:
        def __init__(self):
            self.sq = [T(P, "nsq%d" % b, [128, 512], BF16) for b in range(2)]
            self.rstd = T(P, "nrstd", [128, 512], F32)
            self.tmp = [T(P, "ntmp%d" % b, [128, 512], F32) for b in range(2)]

    def sumsq_rstd(xa, xks, n, ns):
        pt, pk = P.ps()
        for dt in range(8):
            sq = ns.sq[dt % 2]
            act(sq[:, :n], xa(dt), AF.Square, xks, [sq.k])
            mm(pt[:, :n], ones_bf.a, sq[:, :n], dt == 0, dt == 7, [ones_bf.k, sq.k], pk, signal=(dt == 7))
        act(ns.rstd[:, :n], pt[:, :n], AF.Ln, [pk, epsc.k], [ns.rstd.k], scale=1.0 / 1024.0, bias=epsc.a)
        act(ns.rstd[:, :n], ns.rstd[:, :n], AF.Exp, [ns.rstd.k], [ns.rstd.k], scale=-0.5)

    def modnorm(xa, xks, l, kS, kSH, c, outa, outk, n, ns):
        sumsq_rstd(xa, xks, n, ns)
        for dt in range(8):
            tb = ns.tmp[dt % 2]
            stt(tb[:, :n], xa(dt), modcol(l, kS, dt, c), ns.rstd[:, :n], ALU.mult, ALU.mult, xks + [mod_all.k, ns.rstd.k], [tb.k])
            act(outa(dt), tb[:, :n], AF.Identity, [tb.k, mod_all.k], [outk], bias=modcol(l, kSH, dt, c))

    def post_residual(ya, yks, l, kGG, c, xa, xks, n, ns):
        sumsq_rstd(ya, yks, n, ns)
        for dt in range(8):
            tb = ns.tmp[dt % 2]
            stt(tb[:, :n], ya(dt), modcol(l, kGG, dt, c), ns.rstd[:, :n], ALU.mult, ALU.mult, yks + [mod_all.k, ns.rstd.k], [tb.k])
            tt("pool" if dt % 2 else "dve", xa(dt), xa(dt), tb[:, :n], ALU.add, xks + [tb.k], xks)

    def load_region(rg):
        ri = 0 if rg["name"] == "p" else 1
        src3 = rg["xsrc"].rearrange("(c p) n -> p c n", p=128)
        for blk in range(2):
            dst = xres[:, :, rg["off"] + blk * 512: rg["off"] + (blk + 1) * 512]
            load(dst, xk[ri][blk], src3[:, :, blk * 512:(blk + 1) * 512])
        if rg["pos"] is not None:
            with P.scope():
                pos3 = rg["pos"].rearrange("(c p) n -> p c n", p=128)
                for blk in range(2):
                    pb = T(P, "posb", [128, 8, 512], F32)
                    load(pb.a, pb.k, pos3[:, :, blk * 512:(blk + 1) * 512])
                    dst = xres[:, :, rg["off"] + blk * 512: rg["off"] + (blk + 1) * 512]
                    tt("pool", dst, dst, pb.a, ALU.add, [xk[ri][blk], pb.k], [xk[ri][blk]])

    def s5_section(rname, nseq, n, hT, hk, get_win, gT, mode, init, fin, spans=((0, 512), (512, 512))):
        full = (mode == "full")
        with P.scope():
            NB = 2
            if full:
                XFs = [[T(P, "XF%d" % r_, [128, 4, nseq, n + 1], BF16) for r_ in range(2)] for _ in range(NB)]
                XBs = [[T(P, "XB%d" % r_, [128, 4, nseq, n + 1], BF16) for r_ in range(2)] for _ in range(NB)]
                ycws = [T(P, "ycw", [128, 4096], BF16) for _ in range(NB)]
                ktws = [T(P, "ktw", [128, 1920], BF16) for _ in range(NB)]
                for bb in range(NB):
                    for r_ in range(2):
                        op("pool", lambda e, t=XFs[bb][r_]: e.memset(t.a, 0.0), writes=[XFs[bb][r_].k])
                        op("pool", lambda e, t=XBs[bb][r_]: e.memset(t.a, 0.0), writes=[XBs[bb][r_].k])
            bvws = [T(P, "bvw", [128, 4096], BF16) for _ in range(NB)]
            ews = [T(P, "ew", [128, 2, 1024], F32) for _ in range(NB)]
            dws = [T(P, "dw", [128, 1024], F32) for _ in range(NB)]
            ty = 0 if nseq > 1 else 1
            gsm = [T(P, "gsm%d" % b, [128, 8], F32) for b in range(2)]
            zis = [T(P, "zi", [128, 1024], BF16) for _ in range(NB)]
            tbs = [[T(P, "s5t%d" % b, [128, 1024], F32) for b in range(4)]] * NB
            Gbs = [[T(P, "s5g%d" % b, [128, 1024], F32) for b in range(2)]] * NB
            fsm = [T(P, "fsm%d" % b, [128, 4], F32) for b in range(2)]
            wb_next = get_win(0)
            for i in range(6):
                bb = i % NB
                bvw, ew, zi, tb_, Gb = bvws[bb], ews[bb], zis[bb], tbs[bb], Gbs[bb]
                dw = dws[bb]
                dma("sp", lambda e, dw=dw, i=i: e.dma_start(out=dw.a, in_=D["scr_d"][i, ty]),
                    reads=[scr_tok[("dt0", i, ty)], scr_tok[("dt1", i, ty)]], writes=[dw.k])
                if full:
                    XF, XB, ycw, ktw = XFs[bb], XBs[bb], ycws[bb], ktws[bb]
                loadr(bvw.a, bvw.k, D["scr_bv"][i], scr_tok[("bv", i)])
                if full:
                    loadr(ycw.a, ycw.k, D["scr_yc"][i], scr_tok[("yc", i)])
                    loadr(ktw.a, ktw.k, D["scr_kt"][i], scr_tok[("kt", i)])
                loadr(ew[:, 0, :], ew.k, D["scr_e"][i, :, 0, :], scr_tok[("ec", i)])
                loadr(ew[:, 1, :], ew.k, D["scr_e"][i, :, 1, :], scr_tok[("es", i)])
                wb = wb_next
                for blk in range(2):
                    pt, pk = P.ps()
                    for kt in range(8):
                        mm(pt[:, :], wb[:, kt, :], hT[:, kt, blk * 512:(blk + 1) * 512], kt == 0, kt == 7, [wb.k, hk[blk]], pk, signal=(kt == 7))
                    act(zi[:, blk * 512:(blk + 1) * 512], pt[:, :], AF.Identity, [pk], [zi.k])
                if i < 5:
                    wb_next = get_win(i + 1)
                if i == 0:
                    dbg_out("d_z0_" + rname, zi.a, zi.k)
                zv = zi.a.rearrange("p (c s) -> p c s", s=8)
                bv5 = bvw.a.rearrange("p (d s r m) -> p d s r m", d=2, s=8, r=2)
                Vps = [P.ps() for _ in range(4)]
                for r_ in range(2):
                    for d in range(2):
                        col = (r_ * 2 + d) * 128
                        for s in range(8):
                            for q in range(4):
                                pt, pk = Vps[q]
                                rows = slice(32 * q, 32 * q + 32)
                                mm(pt[:, col:col + 128], bv5[rows, d, s, r_, :], zv[rows, :, s], s == 0, s == 7, [bvw.k, zi.k], pk,
                                   signal=(s == 7 and r_ == 1 and d == 1), tp=(32 * q, 0))
                t1, t2, t3, t4 = tb_
                dq = lambda ap: ap.rearrange("p (d q c) -> p d q c", d=2, q=4)
                for q in range(4):
                    hs = slice(q * 256, (q + 1) * 256)
                    vp, vk = Vps[q]
                    v2 = lambda ap: ap.rearrange("p (d c) -> p d c", d=2)
                    vr, vi = v2(vp[:, 0:256]), v2(vp[:, 256:512])
                    ec_, es_ = v2(ew[:, 0, hs]), v2(ew[:, 1, hs])
                    o1, o2, o3, o4 = dq(t1.a)[:, :, q, :], dq(t2.a)[:, :, q, :], dq(t3.a)[:, :, q, :], dq(t4.a)[:, :, q, :]
                    tt("dve", o1, vr, ec_, ALU.mult, [vk, ew.k], [t1.k])
                    tt("dve", o4, vi, es_, ALU.mult, [vk, ew.k], [t4.k])
                    tt("dve", o3, vr, es_, ALU.mult, [vk, ew.k], [t3.k])
                    tt("dve", o2, vi, ec_, ALU.mult, [vk, ew.k], [t2.k])
                    tt("pool", o1, o1, o4, ALU.add, [t1.k, t4.k], [t1.k])
                    tt("pool", o2, o2, o3, ALU.subtract, [t2.k, t3.k], [t2.k])
                for r_, (src_, dst) in enumerate(((t1, Gb[0]), (t2, Gb[1]))):
                    if init is not None:
                        gi_ = init["G"][r_]
                        tt("dve", gsm[r_].a, gi_[:, 8 * i:8 * i + 8], lv[:, 0, 8 * i:8 * i + 8], ALU.mult, [gi_.k, lv.k], [gsm[r_].k])
                        gq = gsm[r_].a.rearrange("p (q d) -> p q d", d=2)
                        tt("dve", dq(src_.a)[:, 0, :, 0], dq(src_.a)[:, 0, :, 0], gq[:, :, 0], ALU.add, [src_.k, gsm[r_].k], [src_.k])
                        tt("dve", dq(src_.a)[:, 1, :, 127], dq(src_.a)[:, 1, :, 127], gq[:, :, 1], ALU.add, [src_.k, gsm[r_].k], [src_.k])
                    for d in range(2):
                        hs = slice(d * 512, (d + 1) * 512)
                        sa, da, de = src_[:, hs], dst[:, hs], dw[:, hs]
                        if d == 1:
                            sa, da, de = sa[:, ::-1], da[:, ::-1], de[:, ::-1]
                        op("dve", lambda e, da=da, sa=sa, de=de: e.tensor_tensor_scan(out=da, data0=de, data1=sa, initial=0.0, op0=ALU.mult, op1=ALU.add),
                           reads=[src_.k, dw.k], writes=[dst.k])
                g0, g1 = Gb
                if not full:
                    Gv = lambda ap: ap.rearrange("p (cb c) -> p cb c", c=128)
                    Fre, Fim = fin
                    fv = lambda t_: t_[:, 8 * i:8 * i + 8].rearrange("p (q d) -> p q d", d=2)
                    gre_f, gim_f = dq(g0.a)[:, 0, :, n - 1], dq(g1.a)[:, 0, :, n - 1]
                    ec_f, es_f = Gv(ew[:, 0, :])[:, 0::2, n - 1], Gv(ew[:, 1, :])[:, 0::2, n - 1]
                    fa, fb = fsm
                    tt("dve", fa.a, gre_f, ec_f, ALU.mult, [g0.k, ew.k], [fa.k])
                    tt("dve", fb.a, gim_f, es_f, ALU.mult, [g1.k, ew.k], [fb.k])
                    tt("dve", fv(Fre)[:, :, 0], fa.a, fb.a, ALU.subtract, [fa.k, fb.k], [Fre.k])
                    tt("dve", fa.a, gre_f, es_f, ALU.mult, [g0.k, ew.k], [fa.k])
                    tt("dve", fb.a, gim_f, ec_f, ALU.mult, [g1.k, ew.k], [fb.k])
                    tt("dve", fv(Fim)[:, :, 0], fa.a, fb.a, ALU.add, [fa.k, fb.k], [Fim.k])
                    cp("dve", fv(Fre)[:, :, 1], dq(g0.a)[:, 1, :, 0], [g0.k], [Fre.k])
                    cp("dve", fv(Fim)[:, :, 1], dq(g1.a)[:, 1, :, 0], [g1.k], [Fim.k])
                    continue
                qd = lambda ap: ap.rearrange("p (q d c) -> p q d c", q=4, d=2)
                gp = lambda ap: ap.rearrange("p (d q c) -> p q d c", d=2, q=4)
                g2 = tbs[0][0] if False else None
                tt("dve", qd(t1.a), gp(g0.a), qd(ew[:, 0, :]), ALU.mult, [g0.k, ew.k], [t1.k])
                tt("dve", qd(t2.a), gp(g1.a), qd(ew[:, 1, :]), ALU.mult, [g1.k, ew.k], [t2.k])
                tt("dve", qd(t3.a), gp(g0.a), qd(ew[:, 1, :]), ALU.mult, [g0.k, ew.k], [t3.k])
                tt("dve", gp(g0.a), gp(g1.a), qd(ew[:, 0, :]), ALU.mult, [g1.k, ew.k], [g0.k])
                v5 = lambda t: (t.a.rearrange("p (d q b n) -> p q d b n", q=4, d=2, b=nseq) if t is g0
                                else t.a.rearrange("p (q d b n) -> p q d b n", q=4, d=2, b=nseq))
                tt("pool", XF[0][:, :, :, 1:n + 1], v5(t1)[:, :, 0], v5(t2)[:, :, 0], ALU.subtract, [t1.k, t2.k], [XF[0].k])
                tt("pool", XB[0][:, :, :, 0:n], v5(t1)[:, :, 1], v5(t2)[:, :, 1], ALU.subtract, [t1.k, t2.k], [XB[0].k])
                tt("pool", XF[1][:, :, :, 1:n + 1], v5(t3)[:, :, 0], v5(g0)[:, :, 0], ALU.add, [t3.k, g0.k], [XF[1].k])
                tt("pool", XB[1][:, :, :, 0:n], v5(t3)[:, :, 1], v5(g0)[:, :, 1], ALU.add, [t3.k, g0.k], [XB[1].k])
                if init is not None:
                    for r_ in range(2):
                        sv_ = init["S"][r_][:, 8 * i:8 * i + 8].rearrange("p (q d) -> p q d", d=2)
                        cp("dve", XF[r_][:, :, 0, 0], sv_[:, :, 0], [init["S"][r_].k], [XF[r_].k])
                        cp("dve", XB[r_][:, :, 0, n], sv_[:, :, 1], [init["S"][r_].k], [XB[r_].k])
                if rname == "p":
                    so = lambda t_: t_.a.rearrange("p (i q d b) -> p i q d b", i=6, q=4, d=2)
                    tt("dve", so(stout[0])[:, i, :, 0, :], v5(t1)[:, :, 0, :, n - 1], v5(t2)[:, :, 0, :, n - 1], ALU.subtract, [t1.k, t2.k], [stout[0].k])
                    tt("dve", so(stout[0])[:, i, :, 1, :], v5(t1)[:, :, 1, :, 0], v5(t2)[:, :, 1, :, 0], ALU.subtract, [t1.k, t2.k], [stout[0].k])
                    tt("dve", so(stout[1])[:, i, :, 0, :], v5(t3)[:, :, 0, :, n - 1], v5(g0)[:, :, 0, :, n - 1], ALU.add, [t3.k, g0.k], [stout[1].k])
                    tt("dve", so(stout[1])[:, i, :, 1, :], v5(t3)[:, :, 1, :, 0], v5(g0)[:, :, 1, :, 0], ALU.add, [t3.k, g0.k], [stout[1].k])
                kt3 = ktw.a.rearrange("p (s m) -> p s m", s=15)
                yc6 = ycw.a.rearrange("p (q d s r m) -> p q d s r m", q=4, d=2, s=8, r=2)
                for (st, nt) in spans:
                    pt, pk = P.ps()
                    ncs = nt // 8
                    nb = max(1, ncs // n)
                    cpb = ncs // nb
                    zb = zi[:, st:st + nt]
                    zbv = zb.rearrange("p (c s) -> p c s", s=8)
                    pv = pt[:, :nt].rearrange("p (c s) -> p c s", s=8)
                    mm(pt[:, :nt], kt3[:, 0, :], zb, True, False, [ktw.k, zi.k], pk, signal=False)
                    for dl in range(1, 8):
                        mm(pv[:, :, dl:8], kt3[:, dl, :], zbv[:, :, 0:8 - dl], False, False, [ktw.k, zi.k], pk, signal=False)
                        mm(pv[:, :, 0:8 - dl], kt3[:, 7 + dl, :], zbv[:, :, dl:8], False, False, [ktw.k, zi.k], pk, signal=False)
                    pv4 = pt[:, :nt].rearrange("p (b c s) -> p b c s", b=nb, s=8)
                    cnt_ = 0
                    for d in range(2):
                        for s in range(8):
                            for r_ in range(2):
                                for q in range(4):
                                    rows = slice(32 * q, 32 * q + 32)
                                    Xt = (XF if d == 0 else XB)[r_]
                                    if nseq > 1:
                                        b0 = (st // 8) // n
                                        c0 = 0 if d == 0 else 1
                                        rhs = Xt[:, q, b0:b0 + nb, c0:c0 + cpb]
                                    else:
                                        c0 = st // 8 + (0 if d == 0 else 1)
                                        rhs = Xt[:, q, 0:1, c0:c0 + ncs]
                                    cnt_ += 1
                                    mm(pv4[rows, :, :, s], yc6[:, q, d, s, r_, :], rhs, False, cnt_ == 128, [ycw.k, Xt.k], pk,
                                       signal=(cnt_ == 128), tp=(0, 32 * q))
                    act(gT[:, i, st:st + nt], pt[:, :nt], AF.Gelu_apprx_tanh, [pk], [gT.k])

    def glu_wout_post(rg, gT, catT, spans=((0, 512), (512, 512))):
        off, c = rg["off"], rg["cond"]
        with P.scope():
            wg = T(P, "wglu", [128, 6, 768], BF16)
            wload_bf16(wg.a, wg.k, D["w_glu"].rearrange("(kt p) m -> p kt m", p=128))
            wo = T(P, "wout", [128, 8, 1024], BF16)
            wload_bf16(wo.a, wo.k, D["w_out0"].rearrange("(kt p) m -> p kt m", p=128))
            sg = [T(P, "sg%d" % b, [128, 512], F32) for b in range(2)]
            yf = T(P, "yf", [128, 8, 512], F32)
            ns = NormScratch()
            for (st, nt) in spans:
                bs_ = slice(st, st + nt)
                for mt in range(6):
                    pt, pk = P.ps()
                    for kt in range(6):
                        mm(pt[:, :nt], wg[:, kt, mt * 128:(mt + 1) * 128], gT[:, kt, bs_], kt == 0, kt == 5, [wg.k, gT.k], pk, signal=(kt == 5))
                    sb_ = sg[mt % 2]
                    act(sb_[:, :nt], pt[:, :nt], AF.Sigmoid, [pk, cmisc.k], [sb_.k], bias=bglu(mt))
                    tt("dve", catT[:, mt, bs_], gT[:, mt, bs_], sb_[:, :nt], ALU.mult, [gT.k, sb_.k], [catT.k])
            dbg_out("d_cat_" + rg["name"], catT.a.rearrange("p a b -> p (a b)"), catT.k)
            for (st, nt) in spans:
                bs_ = slice(st, st + nt)
                for mt in range(8):
                    pt, pk = P.ps()
                    for kt in range(8):
                        mm(pt[:, :nt], wo[:, kt, mt * 128:(mt + 1) * 128], catT[:, kt, bs_], kt == 0, kt == 7, [wo.k, catT.k], pk, signal=(kt == 7))
                    act(yf[:, mt, :nt], pt[:, :nt], AF.Identity, [pk], [yf.k])
                xs_ = slice(off + st, off + st + nt)
                post_residual(lambda dt: yf[:, dt, :nt], [yf.k], 0, 2, c, lambda dt: xres[:, dt, xs_], [xk[0][st // 512]], nt, ns)


    def l0_mixer(rg):
        ri = 0 if rg["name"] == "p" else 1
        off, c, nseq, n = rg["off"], rg["cond"], rg["nseq"], rg["n"]
        with P.scope():
            hT = T(P, "hT", [128, 8, 1024], BF16)
            hk = [Tok(), Tok()]
            catT = T(P, "catT", [128, 8, 1024], BF16)
            gT = T(P, "gT", [128, 6, 1024], BF16)
            winb = [T(P, "winb%d" % b, [128, 8, 128], BF16) for b in range(2)]
            nwin = [0]
            win_src = D["w_in0"].rearrange("(kt p) m -> p kt m", p=128)

            def get_win(col_tile):
                wb = winb[nwin[0] % 2]
                nwin[0] += 1
                wload_bf16(wb.a, wb.k, win_src[:, :, col_tile * 128:(col_tile + 1) * 128])
                return wb

            with P.scope():
                ns = NormScratch()
                for blk in range(2):
                    sl = slice(off + blk * 512, off + (blk + 1) * 512)
                    modnorm(lambda dt: xres[:, dt, sl], [xk[ri][blk]], 0, 0, 1, c,
                            lambda dt: hT[:, dt, blk * 512:(blk + 1) * 512], hk[blk], 512, ns)
            dbg_out("d_hT_" + rg["name"], hT.a.rearrange("p a b -> p (a b)"), hk[1])

            ckpt(2.2)
            with P.scope():
                fcf = T(P, "fcf", [128, 4, 128], F32)
                load(fcf.a, fcf.k, D["fconst"])
                fcb = T(P, "fcb", [128, 4, 128], BF16)
                cp("dve", fcb.a, fcf.a, [fcf.k], [fcb.k])
                ctf = T(P, "ctf", [128, 2, 2, 256], F32)
                load(ctf.a, ctf.k, D["ctp"])
                ctb = T(P, "ctb", [128, 2, 2, 256], BF16)
                cp("dve", ctb.a, ctf.a, [ctf.k], [ctb.k])
                zfT = T(P, "zfT", [128, 2, 1024], BF16)
                PQ = T(P, "PQ", [128, 8, 512], BF16)
                ZT = T(P, "ZT", [128, 2, 1024], BF16)
                for jt in range(2):
                    wb = get_win(6 + jt)
                    for blk in range(2):
                        pt, pk = P.ps()
                        for kt in range(8):
                            mm(pt[:, :], wb[:, kt, :], hT[:, kt, blk * 512:(blk + 1) * 512], kt == 0, kt == 7, [wb.k, hk[blk]], pk, signal=(kt == 7))
                        act(zfT[:, jt, blk * 512:(blk + 1) * 512], pt[:, :], AF.Identity, [pk], [zfT.k])
                for t8 in range(8):
                    pt, pk = P.ps()
                    tsl = slice(t8 * 128, (t8 + 1) * 128)
                    for cs in range(2):
                        for jt in range(2):
                            col = cs * 256 + jt * 128
                            mm(pt[:, col:col + 128], zfT[:, jt, tsl], fcb[:, cs, :], True, True, [zfT.k, fcb.k], pk, signal=(cs == 1 and jt == 1))
                    cp("dve", PQ[:, t8, :], pt[:, :], [pk], [PQ.k])
                if rg["name"] == "p":
                    for jt in range(2):
                        for blk in range(2):
                            pt, pk = P.ps()
                            for b2 in range(2):
                                kk = 0
                                for ttl in range(2):
                                    t8 = (blk * 2 + b2) * 2 + ttl
                                    for cs in range(2):
                                        mm(pt[:, b2 * 256:(b2 + 1) * 256], PQ[:, t8, cs * 256 + jt * 128: cs * 256 + (jt + 1) * 128], ctb[:, cs, ttl, :],
                                           kk == 0, kk == 3, [PQ.k, ctb.k], pk, signal=(kk == 3))
                                        kk += 1
                            act(ZT[:, jt, blk * 512:(blk + 1) * 512], pt[:, :], AF.Identity, [pk], [ZT.k])
                for jt in range(2):
                    for blk in range(2):
                        pt, pk = P.ps()
                        mm(pt[:, :], fcb[:, 2 + jt, :], ZT[:, jt, blk * 512:(blk + 1) * 512], True, True, [fcb.k, ZT.k], pk, signal=True)
                        act(catT[:, 6 + jt, blk * 512:(blk + 1) * 512], pt[:, :], AF.Identity, [pk, cmisc.k], [catT.k], bias=fnetb(jt))
            dbg_out("d_yb_" + rg["name"], catT[:, 6:8, :].rearrange("p a b -> p (a b)"), catT.k)

            ckpt(2.3)
            s5_section(rg["name"], nseq, n, hT, hk, get_win, gT, "full", None, None)
            dbg_out("d_g_" + rg["name"], gT.a.rearrange("p a b -> p (a b)"), gT.k)

            ckpt(2.6)
            glu_wout_post(rg, gT, catT)

    ckpt(2.01)
    load_region(REG[0])
    ckpt(2.05)
    l0_mixer(REG[0])
    if "d_x1_p" in dbg:
        store(D["d_x1_p"].rearrange("p (a b) -> p a b", a=8), xres[:, :, 0:1024], xk[0][1])
    for nm, t_ in (("st_re", stout[0]), ("st_im", stout[1])):
        store(D[nm].rearrange("p a b -> p (a b)"), t_.a, t_.k)
    ckpt(3.0)

    def ffn(l, rg, spans=((0, 512), (512, 512)), xbuf=None, xtoks=None):
        off, c = rg["off"], rg["cond"]
        xb_ = xres if xbuf is None else xbuf
        w1src = D["w_ff1_%d" % l].rearrange("(kt p) j -> p kt j", p=128)
        w2src = D["w_ff2_%d" % l].rearrange("(jt p) m -> p jt m", p=128)
        with P.scope():
            ns = NormScratch()
            h2 = T(P, "h2", [128, 8, 512], BF16)
            hid = T(P, "hid", [128, 32, 512], BF16)
            w1s = [T(P, "w1s%d" % b, [128, 8, 512], BF16) for b in range(4)]
            w2s = [T(P, "w2s%d" % b, [128, 32, 128], BF16) for b in range(4)]
            rl = [T(P, "rl%d" % b, [128, 512], F32) for b in range(2)]
            yf = T(P, "yff", [128, 8, 512], F32)
            for (st, nt) in spans:
                xs_ = slice(off + st, off + st + nt)
                xkk = [xk[0][st // 512]] if xtoks is None else xtoks
                modnorm(lambda dt: xb_[:, dt, xs_], xkk, l, 3, 4, c, lambda dt: h2[:, dt, :nt], h2.k, nt, ns)
                for jg in range(8):
                    wb = w1s[jg % 4]
                    wload_bf16(wb.a, wb.k, w1src[:, :, jg * 512:(jg + 1) * 512])
                    for j4 in range(4):
                        jt = jg * 4 + j4
                        pt, pk = P.ps()
                        for kt in range(8):
                            mm(pt[:, :nt], wb[:, kt, j4 * 128:(j4 + 1) * 128], h2[:, kt, :nt], kt == 0, kt == 7, [wb.k, h2.k], pk, signal=(kt == 7))
                        rb = rl[jt % 2]
                        act(rb[:, :nt], pt[:, :nt], AF.Relu, [pk], [rb.k])
                        tt("dve", hid[:, jt, :nt], rb[:, :nt], rb[:, :nt], ALU.mult, [rb.k], [hid.k])
                for mt in range(8):
                    wb = w2s[mt % 4]
                    wload_bf16(wb.a, wb.k, w2src[:, :, mt * 128:(mt + 1) * 128])
                    pt, pk = P.ps()
                    for jt in range(32):
                        mm(pt[:, :nt], wb[:, jt, :], hid[:, jt, :nt], jt == 0, jt == 31, [wb.k, hid.k], pk, signal=(jt == 31))
                    act(yf[:, mt, :nt], pt[:, :nt], AF.Identity, [pk], [yf.k])
                post_residual(lambda dt: yf[:, dt, :nt], [yf.k], l, 5, c, lambda dt: xb_[:, dt, xs_], xkk, nt, ns)

    ffn(0, REG[0])
    if "d_x2_p" in dbg:
        store(D["d_x2_p"].rearrange("p (a b) -> p a b", a=8), xres[:, :, 0:1024], xk[0][1])
    ckpt(4.0)

    def l1_mixer(rg, Uall=None, tile_base=0, tps_all=None, own=False):
        ri = 0 if rg["name"] == "p" else 1
        off, c, nseq = rg["off"], rg["cond"], rg["nseq"]
        w1 = D["w_in1"].rearrange("(kt p) m -> p kt m", p=128)
        with P.scope():
            cat1 = T(P, "cat1", [128, 8, 1024], BF16)
            wo = T(P, "wout1", [128, 8, 1024], BF16)
            wload_bf16(wo.a, wo.k, D["w_out1"].rearrange("(kt p) m -> p kt m", p=128))
            with P.scope():
                hT = T(P, "h1T", [128, 8, 1024], BF16)
                hk = [Tok(), Tok()]
                with P.scope():
                    ns = NormScratch()
                    for blk in range(2):
                        sl = slice(off + blk * 512, off + (blk + 1) * 512)
                        modnorm(lambda dt: xres[:, dt, sl], [xk[ri][blk]], 1, 0, 1, c,
                                lambda dt: hT[:, dt, blk * 512:(blk + 1) * 512], hk[blk], 512, ns)
                wg = T(P, "w1g", [128, 8, 512], BF16); wv = T(P, "w1v", [128, 8, 512], BF16)
                if Uall is None:
                    wu = T(P, "w1u", [128, 8, 512], BF16)
                    wload_bf16(wu.a, wu.k, w1[:, :, 0:512])
                wload_bf16(wg.a, wg.k, w1[:, :, 512:1024])
                wload_bf16(wv.a, wv.k, w1[:, :, 1024:1536])
                pmf = T(P, "pmf", [128, 10, 4, 128], F32)
                load(pmf.a, pmf.k, D["pmat"])
                pmb = T(P, "pmb", [128, 10, 4, 128], BF16)
                cp("dve", pmb.a, pmf.a, [pmf.k], [pmb.k])
                wst = T(P, "wst", [128, 4, 128], BF16)
                wload_bf16(wst.a, wst.k, D["wsT"])
                pwb = T(P, "pwb", [128, 4, 128], BF16)
                wload_bf16(pwb.a, pwb.k, D["pool_w"].rearrange("g c d -> c g d"))
                lnb = T(P, "lnb", [128, 3, 512], F32)
                for k3 in range(3):
                    load(lnb[:, k3, :], lnb.k, D["lnv"][k3].partition_broadcast(128))
                Utm = T(P, "Utm", [128, 8, 512], BF16) if Uall is None else None
                vn = T(P, "vn", [128, 8, 512], BF16)
                uT = T(P, "uT", [128, 4, 1024], BF16)
                pT = T(P, "pT", [128, 4, 1024], BF16)
                gv = [T(P, "gv%d" % b, [128, 512], F32) for b in range(2)]
                st6 = T(P, "st6", [128, 4, 6], F32)
                mv = T(P, "mv", [128, 4, 2], F32)
                rs4 = T(P, "rs4", [128, 4], F32)
                for t8 in range(8):
                    tsl = slice(t8 * 128, (t8 + 1) * 128)
                    hkk = hk[t8 // 4]
                    if Uall is None:
                        pt, pk = P.ps()
                        for kt in range(8):
                            mm(pt[:, :], hT[:, kt, tsl], wu[:, kt, :], kt == 0, kt == 7, [hkk, wu.k], pk, signal=(kt == 7))
                        cp("dve", Utm[:, t8, :], pt[:, :], [pk], [Utm.k])
                    pt, pk = P.ps()
                    for kt in range(8):
                        mm(pt[:, :], hT[:, kt, tsl], wv[:, kt, :], kt == 0, kt == 7, [hkk, wv.k], pk, signal=(kt == 7))
                    g_ = gv[t8 % 2]
                    act(g_.a, pt[:, :], AF.Gelu_apprx_tanh, [pk], [g_.k])
                    for h in range(4):
                        op("dve", lambda e, g_=g_, h=h: e.bn_stats(out=st6[:, h, :], in_=g_[:, h * 128:(h + 1) * 128]), reads=[g_.k], writes=[st6.k])
                    for h in range(4):
                        op("dve", lambda e, h=h: e.bn_aggr(out=mv[:, h, :], in_=st6[:, h, :]), reads=[st6.k], writes=[mv.k])
                    act(rs4.a, mv[:, :, 1], AF.Sqrt, [mv.k, epsc.k], [rs4.k], bias=epsc.a)
                    recip(rs4.a, rs4.a, [rs4.k], [rs4.k])
                    for h in range(4):
                        hs = slice(h * 128, (h + 1) * 128)
                        ts("dve", g_[:, hs], g_[:, hs], mv[:, h, 0:1], ALU.subtract, [g_.k, mv.k, rs4.k], [g_.k], s2=rs4[:, h:h + 1], op1=ALU.mult)
                    tt("pool", g_.a, g_.a, lnb[:, 0, :], ALU.mult, [g_.k, lnb.k], [g_.k])
                    tt("pool", vn[:, t8, :], g_.a, lnb[:, 1, :], ALU.add, [g_.k, lnb.k], [vn.k])
                for h in range(4):
                    for blk in range(2):
                        pt, pk = P.ps()
                        for kt in range(8):
                            mm(pt[:, :], wg[:, kt, h * 128:(h + 1) * 128], hT[:, kt, blk * 512:(blk + 1) * 512], kt == 0, kt == 7, [wg.k, hk[blk]], pk, signal=(kt == 7))
                        act(uT[:, h, blk * 512:(blk + 1) * 512], pt[:, :], AF.Gelu_apprx_tanh, [pk], [uT.k])
                tps = (8 // nseq) if tps_all is None else tps_all
                Usrc = Utm if Uall is None else Uall
                for g in range(4):
                    for half in range(2):
                        pt, pk = P.ps()
                        for t4 in range(4):
                            t8 = half * 4 + t4
                            tg = tile_base + t8
                            tl = tg % tps
                            terms = []
                            if own:
                                if tg == 0:
                                    terms = [(31, 7), (0, 5), (1, 1)]
                                elif tg == 7:
                                    terms = [(6, 0), (7, 6), (8, 8)]
                                else:
                                    terms = [(tg - 1, 0), (tg, 2), (tg + 1, 1)]
                            else:
                                if tl > 0:
                                    terms.append((tg - 1, 0))
                                cur = 3 if tl == 0 else (4 if tl == tps - 1 else 2)
                                terms.append((tg, cur))
                                if tl < tps - 1:
                                    terms.append((tg + 1, 1))
                            for kk, (tsrc, slot) in enumerate(terms):
                                mm(pt[:, t4 * 128:(t4 + 1) * 128], Usrc[:, tsrc, g * 128:(g + 1) * 128], pmb[:, slot, g, :], kk == 0, kk == len(terms) - 1,
                                   [Usrc.k, pmb.k], pk, signal=(kk == len(terms) - 1))
                        cp("dve", pT[:, g, half * 512:(half + 1) * 512], pt[:, :], [pk], [pT.k])
                    for half in range(2):
                        pt, pk = P.ps()
                        mm(pt[:, :], pwb[:, g, :], pT[:, g, half * 512:(half + 1) * 512], True, True, [pwb.k, pT.k], pk, signal=True)
                        act(cat1[:, g, half * 512:(half + 1) * 512], pt[:, :], AF.Identity, [pk, cmisc.k], [cat1.k], scale=poolsc(g))
                for t8 in range(8):
                    tsl = slice(t8 * 128, (t8 + 1) * 128)
                    pt, pk = P.ps()
                    for h in range(4):
                        mm(pt[:, h * 128:(h + 1) * 128], vn[:, t8, h * 128:(h + 1) * 128], wst[:, h, :], True, True, [vn.k, wst.k], pk, signal=(h == 3))
                    g_ = gv[t8 % 2]
                    tt("dve", g_.a, pt[:, :], lnb[:, 2, :], ALU.add, [pk, lnb.k], [g_.k])
                    tt("dve", cat1[:, 4:8, tsl], g_.a.rearrange("p (h q) -> p h q", h=4), uT[:, :, tsl], ALU.mult, [g_.k, uT.k], [cat1.k])
            dbg_out("d_cat1_" + rg["name"], cat1.a.rearrange("p a b -> p (a b)"), cat1.k)
            with P.scope():
                yf = T(P, "yf1", [128, 8, 512], F32)
                ns = NormScratch()
                for blk in range(2):
                    bs_ = slice(blk * 512, (blk + 1) * 512)
                    for mt in range(8):
                        pt, pk = P.ps()
                        for kt in range(8):
                            mm(pt[:, :], wo[:, kt, mt * 128:(mt + 1) * 128], cat1[:, kt, bs_], kt == 0, kt == 7, [wo.k, cat1.k], pk, signal=(kt == 7))
                        act(yf[:, mt, :], pt[:, :], AF.Identity, [pk], [yf.k])
                    xs_ = slice(off + blk * 512, off + (blk + 1) * 512)
                    post_residual(lambda dt: yf[:, dt, :], [yf.k], 1, 2, c, lambda dt: xres[:, dt, xs_], [xk[ri][blk]], 512, ns)

    l1_mixer(REG[0])
    if "d_x3_p" in dbg:
        store(D["d_x3_p"].rearrange("p (a b) -> p a b", a=8), xres[:, :, 0:1024], xk[0][1])
    ckpt(5.0)
    ffn(1, REG[0])
    yo = D["ypT"].rearrange("(c p) n -> p c n", p=128)
    for blk in range(2):
        dma("sp", lambda e, blk=blk: e.dma_start(out=yo[:, :, blk * 512:(blk + 1) * 512], in_=xres[:, :, blk * 512:(blk + 1) * 512]),
            reads=[xk[0][blk]], writes=[Tok()])
    ckpt(6.0)

    xsd = D["scr_x"].rearrange("(c p) n -> p c n", p=128)
    xdk = [[Tok(), Tok()] for _ in range(4)]
    SREG = [dict(name="s%d" % q, off=0, cond=1, nseq=1, n=128, q=q) for q in range(4)]

    def xs_load(q):
        for blk in range(2):
            loadr(xres[:, :, blk * 512:(blk + 1) * 512], xk[0][blk], xsd[:, :, q * 1024 + blk * 512: q * 1024 + (blk + 1) * 512], xdk[q][blk])

    def xs_store(q, dst3=None):
        d3 = xsd if dst3 is None else dst3
        for blk in range(2):
            dma("sp", lambda e, blk=blk: e.dma_start(out=d3[:, :, q * 1024 + blk * 512: q * 1024 + (blk + 1) * 512], in_=xres[:, :, blk * 512:(blk + 1) * 512]),
                reads=[xk[0][blk]], writes=[xdk[q][blk]])

    def make_get_win(winb):
        nwin = [0]
        win_src = D["w_in0"].rearrange("(kt p) m -> p kt m", p=128)

        def get_win(col_tile):
            wb = winb[nwin[0] % 2]
            nwin[0] += 1
            wload_bf16(wb.a, wb.k, win_src[:, :, col_tile * 128:(col_tile + 1) * 128])
            return wb
        return get_win

    with P.scope():
        ybT = T(P, "ybTall", [128, 2, 2048], BF16)
        Ffin = [[T(P, "Ff%d_%d" % (q, r_), [128, 48], F32) for r_ in range(2)] for q in range(4)]
        Sin = [[T(P, "Si%d_%d" % (q, r_), [128, 48], F32) for r_ in range(2)] for q in range(4)]
        Gin = [[T(P, "Gi%d_%d" % (q, r_), [128, 48], F32) for r_ in range(2)] for q in range(4)]
        fcb = T(P, "fcb_s", [128, 4, 128], BF16)
        with P.scope():
            fcf = T(P, "fcf_s", [128, 4, 128], F32)
            load(fcf.a, fcf.k, D["fconst"])
            cp("dve", fcb.a, fcf.a, [fcf.k], [fcb.k])
        xs3 = D["xsT"].rearrange("(c p) n -> p c n", p=128)
        pos3 = D["posT"].rearrange("(c p) n -> p c n", p=128)
        pqscope = P.scope()
        pqscope.__enter__()
        PQall = T(P, "PQall", [128, 32, 512], BF16)
        for q in range(4):
            with P.scope():
                with P.scope():
                    for blk in range(2):
                        cs_ = slice(q * 1024 + blk * 512, q * 1024 + (blk + 1) * 512)
                        load(xres[:, :, blk * 512:(blk + 1) * 512], xk[0][blk], xs3[:, :, cs_])
                        pb = T(P, "posb", [128, 8, 512], F32)
                        load(pb.a, pb.k, pos3[:, :, cs_])
                        dst = xres[:, :, blk * 512:(blk + 1) * 512]
                        tt("dve" if blk == 0 else "pool", dst, dst, pb.a, ALU.add, [xk[0][blk], pb.k], [xk[0][blk]])
                xs_store(q)
                hT = T(P, "hTa", [128, 8, 1024], BF16)
                hk = [Tok(), Tok()]
                winb = [T(P, "winba%d" % b, [128, 8, 128], BF16) for b in range(2)]
                get_win = make_get_win(winb)
                with P.scope():
                    ns = NormScratch()
                    for blk in range(2):
                        sl = slice(blk * 512, (blk + 1) * 512)
                        modnorm(lambda dt: xres[:, dt, sl], [xk[0][blk]], 0, 0, 1, 1, lambda dt: hT[:, dt, sl], hk[blk], 512, ns)
                with P.scope():
                    zfT = T(P, "zfTa", [128, 2, 1024], BF16)
                    for jt in range(2):
                        wb = get_win(6 + jt)
                        for blk in range(2):
                            pt, pk = P.ps()
                            for kt in range(8):
                                mm(pt[:, :], wb[:, kt, :], hT[:, kt, blk * 512:(blk + 1) * 512], kt == 0, kt == 7, [wb.k, hk[blk]], pk, signal=(kt == 7))
                            act(zfT[:, jt, blk * 512:(blk + 1) * 512], pt[:, :], AF.Identity, [pk], [zfT.k])
                    for t8 in range(8):
                        pt, pk = P.ps()
                        tsl = slice(t8 * 128, (t8 + 1) * 128)
                        for cs in range(2):
                            for jt in range(2):
                                col = cs * 256 + jt * 128
                                mm(pt[:, col:col + 128], zfT[:, jt, tsl], fcb[:, cs, :], True, True, [zfT.k, fcb.k], pk, signal=(cs == 1 and jt == 1))
                        cp("dve", PQall[:, 8 * q + t8, :], pt[:, :], [pk], [PQall.k])
                s5_section("s", 1, 128, hT, hk, get_win, None, "finals", None, (Ffin[q][0], Ffin[q][1]))
        with P.scope():
            ZT = T(P, "ZTs", [128, 2, 2048], BF16)
            slab = [T(P, "cts%d" % b, [128, 8, 2, 512], BF16) for b in range(3)]
            nsl = 0
            YB = {0: 0, 1: 1, 2: 2, 3: 7}
            for kb in range(4):
                (p0, k0), (p1, k1) = P.ps(), P.ps()
                for qq in range(4):
                    sb_ = slab[nsl % 3]
                    nsl += 1
                    load(sb_.a, sb_.k, D["cts"][:, 8 * qq:8 * qq + 8, :, kb * 512:(kb + 1) * 512])
                    for t8 in range(8):
                        for cs in range(2):
                            first = (qq == 0 and t8 == 0 and cs == 0)
                            last = (qq == 3 and t8 == 7 and cs == 1)
                            for jt, (pp, kk) in enumerate(((p0, k0), (p1, k1))):
                                mm(pp[:, :], PQall[:, 8 * qq + t8, cs * 256 + jt * 128: cs * 256 + (jt + 1) * 128], sb_[:, t8, cs, :], first, last,
                                   [PQall.k, sb_.k], kk, signal=last)
                act(ZT[:, 0, kb * 512:(kb + 1) * 512], p0[:, :], AF.Identity, [k0], [ZT.k])
                act(ZT[:, 1, kb * 512:(kb + 1) * 512], p1[:, :], AF.Identity, [k1], [ZT.k])
            for jt in range(2):
                for kb in range(4):
                    pt, pk = P.ps()
                    mm(pt[:, :], fcb[:, 2 + jt, :], ZT[:, jt, kb * 512:(kb + 1) * 512], True, True, [fcb.k, ZT.k], pk, signal=True)
                    act(ybT[:, jt, kb * 512:(kb + 1) * 512], pt[:, :], AF.Identity, [pk, cmisc.k], [ybT.k], bias=fnetb(jt))
        pqscope.__exit__(None, None, None)
        with P.scope():
            h0 = T(P, "h0s", [128, 2, 48], F32)
            load(h0.a, h0.k, D["h0s"])
            Are = T(P, "Are", [128, 48], F32); Aim = T(P, "Aim", [128, 48], F32)
            tt("dve", Are.a, lv[:, 5, :], lv[:, 3, :], ALU.mult, [lv.k], [Are.k])
            tt("dve", Aim.a, lv[:, 5, :], lv[:, 4, :], ALU.mult, [lv.k], [Aim.k])
            Rc = T(P, "Rc", [128, 48], F32); Rs = T(P, "Rs", [128, 48], F32)
            ev = lambda ap, d: ap.rearrange("p (a d) -> p a d", d=2)[:, :, d]
            cp("dve", ev(Rc.a, 0), ev(lv[:, 1, :], 0), [lv.k], [Rc.k])
            cp("dve", ev(Rs.a, 0), ev(lv[:, 2, :], 0), [lv.k], [Rs.k])
            cp("dve", ev(Rc.a, 1), ev(lv[:, 3, :], 1), [lv.k], [Rc.k])
            cp("dve", ev(Rs.a, 1), ev(lv[:, 4, :], 1), [lv.k], [Rs.k])
            ca = T(P, "ca", [128, 48], F32); cb_ = T(P, "cb", [128, 48], F32)
            Tr = T(P, "Tr", [128, 48], F32); Ti = T(P, "Ti", [128, 48], F32)
            op("pool", lambda e: e.memset(Tr.a, 0.0), writes=[Tr.k])
            op("pool", lambda e: e.memset(Ti.a, 0.0), writes=[Ti.k])
            for d, visits, mbase in ((0, [0, 1, 2, 3, 0, 1, 2], 0), (1, [3, 2, 1, 0, 3, 2, 1], 4)):
                for k in visits:
                    mcol = meta[:, mbase + k:mbase + k + 1]
                    for r_, Tt in ((0, Tr), (1, Ti)):
                        tt("dve", ev(ca.a, d), ev(h0[:, r_, :], d), ev(Tt.a, d), ALU.subtract, [h0.k, Tt.k], [ca.k])
                        stt(ev(Sin[k][r_].a, d), ev(ca.a, d), mcol, ev(Tt.a, d), ALU.mult, ALU.add, [ca.k, meta.k, Tt.k], [Sin[k][r_].k])
                    sr, si = Sin[k]
                    tt("dve", ev(ca.a, d), ev(Are.a, d), ev(sr.a, d), ALU.mult, [Are.k, sr.k], [ca.k])
                    tt("dve", ev(cb_.a, d), ev(Aim.a, d), ev(si.a, d), ALU.mult, [Aim.k, si.k], [cb_.k])
                    tt("dve", ev(ca.a, d), ev(ca.a, d), ev(cb_.a, d), ALU.subtract, [ca.k, cb_.k], [ca.k])
                    tt("dve", ev(Tr.a, d), ev(ca.a, d), ev(Ffin[k][0].a, d), ALU.add, [ca.k, Ffin[k][0].k], [Tr.k])
                    tt("dve", ev(ca.a, d), ev(Are.a, d), ev(si.a, d), ALU.mult, [Are.k, si.k], [ca.k])
                    tt("dve", ev(cb_.a, d), ev(Aim.a, d), ev(sr.a, d), ALU.mult, [Aim.k, sr.k], [cb_.k])
                    tt("dve", ev(ca.a, d), ev(ca.a, d), ev(cb_.a, d), ALU.add, [ca.k, cb_.k], [ca.k])
                    tt("dve", ev(Ti.a, d), ev(ca.a, d), ev(Ffin[k][1].a, d), ALU.add, [ca.k, Ffin[k][1].k], [Ti.k])
            for q in range(4):
                sr, si = Sin[q]
                tt("dve", ca.a, sr.a, Rc.a, ALU.mult, [sr.k, Rc.k], [ca.k])
                tt("dve", cb_.a, si.a, Rs.a, ALU.mult, [si.k, Rs.k], [cb_.k])
                tt("dve", Gin[q][0].a, ca.a, cb_.a, ALU.subtract, [ca.k, cb_.k], [Gin[q][0].k])
                tt("dve", ca.a, sr.a, Rs.a, ALU.mult, [sr.k, Rs.k], [ca.k])
                tt("dve", cb_.a, si.a, Rc.a, ALU.mult, [si.k, Rc.k], [cb_.k])
                tt("dve", Gin[q][1].a, ca.a, cb_.a, ALU.add, [ca.k, cb_.k], [Gin[q][1].k])
        uk = [Tok() for _ in range(32)]
        SPANS = {0: ((0, 512), (512, 512)), 1: ((0, 128),), 3: ((896, 128),)}
        UT8 = {0: list(range(8)), 1: [0], 3: [7]}
        xh = T(P, "xh", [128, 8, 256], F32)
        HCOL = {1: 0, 3: 128}

        def u_tiles(xb_, xtoks_of, spans, tiles):
            with P.scope():
                hT1 = T(P, "hT1u", [128, 8, 1024], BF16)
                hk1 = [Tok(), Tok()]
                wu = T(P, "w1uu", [128, 8, 512], BF16)
                ustg = [T(P, "ustg%d" % b, [128, 512], BF16) for b in range(2)]
                wload_bf16(wu.a, wu.k, D["w_in1"].rearrange("(kt p) m -> p kt m", p=128)[:, :, 0:512])
                with P.scope():
                    ns = NormScratch()
                    for (st, nt) in spans:
                        sl = slice(st, st + nt)
                        modnorm(lambda dt: xb_[:, dt, sl], xtoks_of(st), 1, 0, 1, 1, lambda dt: hT1[:, dt, sl], hk1[st // 512], nt, ns)
                for n_, (col, tg) in enumerate(tiles):
                    pt, pk = P.ps()
                    for kt in range(8):
                        mm(pt[:, :], hT1[:, kt, col:col + 128], wu[:, kt, :], kt == 0, kt == 7, [hk1[col // 512], wu.k], pk, signal=(kt == 7))
                    ust = ustg[n_ % 2]
                    cp("dve", ust.a, pt[:, :], [pk], [ust.k])
                    dma("sp", lambda e, ust=ust, tg=tg: e.dma_start(out=D["scr_u"][:, tg, :], in_=ust.a), reads=[ust.k], writes=[uk[tg]])

        for q in (1, 3, 0):
            rg = SREG[q]
            spans = SPANS[q]
            xs_load(q)
            with P.scope():
                hT = T(P, "hTb", [128, 8, 1024], BF16)
                hk = [Tok(), Tok()]
                catT = T(P, "catTb", [128, 8, 1024], BF16)
                gT = T(P, "gTb", [128, 6, 1024], BF16)
                winb = [T(P, "winbb%d" % b, [128, 8, 128], BF16) for b in range(2)]
                get_win = make_get_win(winb)
                with P.scope():
                    ns = NormScratch()
                    for blk in range(2):
                        sl = slice(blk * 512, (blk + 1) * 512)
                        modnorm(lambda dt: xres[:, dt, sl], [xk[0][blk]], 0, 0, 1, 1, lambda dt: hT[:, dt, sl], hk[blk], 512, ns)
                s5_section("s", 1, 128, hT, hk, get_win, gT, "full", dict(S=Sin[q], G=Gin[q]), None, spans=spans)
                YOFF = {0: 0, 1: 1024, 3: 1536 - 512}
                for (st, nt) in spans:
                    cp("pool", catT[:, 6:8, st:st + nt], ybT[:, :, YOFF[q] + st:YOFF[q] + st + nt], [ybT.k], [catT.k])
                glu_wout_post(rg, gT, catT, spans=spans)
            if q != 0:
                (st, nt), = spans
                cp("pool", xh[:, :, HCOL[q]:HCOL[q] + 128], xres[:, :, st:st + nt], [xk[0][st // 512]], [xh.k])
                continue
            ffn(0, SREG[1], spans=((0, 256),), xbuf=xh, xtoks=[xh.k])
            u_tiles(xh, lambda st: [xh.k], ((0, 256),), [(0, 8), (128, 31)])
            ffn(0, rg, spans=spans)
            u_tiles(xres, lambda st: [xk[0][st // 512]], spans, [(128 * t8, t8) for t8 in range(8)])
            xs_store(q)
    with P.scope():
        yso = D["ysT"].rearrange("(c p) n -> p c n", p=128)
        Uall = T(P, "Uall", [128, 32, 512], BF16)
        for tg in list(range(9)) + [31]:
            loadr(Uall[:, tg, :], Uall.k, D["scr_u"][:, tg, :], uk[tg])
        rg = SREG[0]
        xs_load(0)
        l1_mixer(rg, Uall=Uall, tile_base=0, tps_all=32, own=True)
        ffn(1, rg)
        xs_store(0, dst3=yso)


_POOL_WINDOWS = (2, 4, 8, 16)


def _fm(v):
    v = np.asarray(v, np.float32)
    return np.ascontiguousarray(v.reshape(-1, 128).T)


def _pos_embed_T():
    rows = 4096 // 64
    rr, cc = np.meshgrid(np.arange(rows, dtype=np.float32), np.arange(64, dtype=np.float32), indexing="ij")
    quarter = 256
    omega = (1.0 / (np.float32(10000.0) ** (np.arange(quarter, dtype=np.float32) / np.float32(quarter)))).astype(np.float32)

    def ax(p):
        ang = p.reshape(-1)[:, None].astype(np.float32) * omega[None, :]
        return np.concatenate([np.sin(ang), np.cos(ang)], axis=-1)
    pe = np.concatenate([ax(rr), ax(cc)], axis=-1).astype(np.float32)
    return np.ascontiguousarray(pe.T)


def _s5_layouts(inp):
    lre = np.asarray(inp["l0_s5_lambda_re"], np.float32)
    lim = np.asarray(inp["l0_s5_lambda_im"], np.float32)
    ldt = np.asarray(inp["l0_s5_log_dt"], np.float32)
    bre = np.asarray(inp["l0_s5_b_re"], np.float32)
    bim = np.asarray(inp["l0_s5_b_im"], np.float32)
    cre = np.asarray(inp["l0_s5_c_re"], np.float32)
    cim = np.asarray(inp["l0_s5_c_im"], np.float32)

    def g6(a):
        return a.reshape((2, 6, 4, 2) + a.shape[2:])

    def lc_state(a):
        x = np.transpose(g6(a), (3, 4, 1, 2, 0))
        return np.repeat(x[..., None], 16, axis=-1)
    ldt3 = np.repeat(ldt[:, :, None], 64, axis=2)
    lc = [lc_state(lre), lc_state(lim), lc_state(ldt3)]
    c6 = lambda a: np.transpose(g6(a), (3, 5, 1, 2, 0, 4))
    b6 = lambda a: np.transpose(g6(a), (3, 4, 1, 2, 0, 5))
    lc += [c6(cre), c6(cim), b6(bre), b6(bim)]
    s5C = np.stack([x.reshape(128, 768) for x in lc], axis=1).astype(np.float32)

    def lb_state(a):
        x = np.transpose(g6(a), (2, 3, 1, 0, 4))
        return np.repeat(x[:, :, None], 16, axis=2)
    lb = [lb_state(lre), lb_state(lim), lb_state(ldt3)]
    bb = lambda a: np.transpose(g6(a), (2, 3, 5, 1, 0, 4))
    lb += [bb(bre), bb(bim)]
    s5B = np.stack([x.reshape(128, 768) for x in lb], axis=1).astype(np.float32)
    return np.ascontiguousarray(s5C), np.ascontiguousarray(s5B)


def _lv_layout(a):
    x = np.asarray(a, np.float32).reshape(2, 6, 4, 2, 64)
    return np.ascontiguousarray(np.transpose(x, (3, 4, 1, 2, 0)).reshape(128, 48))


def _band(w, kind):
    h = w // 2
    M = np.zeros((128, 128), np.float64)
    for t in range(128):
        lo, hi = t - h, t + h
        cnt = float(w)
        if kind == "first":
            cnt = float(hi - max(lo, 0))
        if kind == "last":
            cnt = float(min(hi, 128) - lo)
        if kind in ("mid", "first", "last"):
            for tp in range(max(lo, 0), min(hi, 128)):
                M[tp, t] += 1.0 / cnt
            M[t, t] -= 1.0
        elif kind == "prev":
            for tp in range(128):
                if lo <= tp - 128 < hi:
                    M[tp, t] += 1.0 / cnt
        elif kind == "next":
            for tp in range(128):
                if lo <= tp + 128 < hi:
                    M[tp, t] += 1.0 / cnt
    return M.astype(np.float32)


_ALL_INPUTS = (
    "x_prompt", "x_sample", "state_l0_s5_re", "state_l0_s5_im", "c", "c_ctx",
    "l0_w_mod", "l0_b_mod", "l0_g_mix_pre", "l0_g_mix_post", "l0_g_ff_pre", "l0_g_ff_post", "l0_w_ff1", "l0_w_ff2",
    "l0_w_in", "l0_w_out", "l0_s5_lambda_re", "l0_s5_lambda_im", "l0_s5_log_dt", "l0_s5_b_re", "l0_s5_b_im",
    "l0_s5_c_re", "l0_s5_c_im", "l0_s5_d", "l0_s5_w_glu", "l0_s5_b_glu", "l0_fnet_w", "l0_fnet_b",
    "l1_w_mod", "l1_b_mod", "l1_g_mix_pre", "l1_g_mix_post", "l1_g_ff_pre", "l1_g_ff_post", "l1_w_ff1", "l1_w_ff2",
    "l1_w_in", "l1_w_out", "l1_pool_w", "l1_pool_scale", "l1_gmlp_ln_g", "l1_gmlp_ln_b", "l1_gmlp_ws", "l1_gmlp_bs",
)


def _host_inputs(inp):
    for _n in _ALL_INPUTS:
        assert _n in inp, _n
    f32 = lambda a: np.ascontiguousarray(np.asarray(a, np.float32))
    shared = {}
    for l in range(2):
        shared["w_mod%d" % l] = f32(inp["l%d_w_mod" % l])
        shared["w_ff1_%d" % l] = f32(inp["l%d_w_ff1" % l])
        shared["w_ff2_%d" % l] = f32(inp["l%d_w_ff2" % l])
        shared["w_out%d" % l] = f32(inp["l%d_w_out" % l])
    shared["w_in0"] = f32(inp["l0_w_in"])
    shared["w_in1"] = f32(inp["l1_w_in"])
    shared["w_glu"] = f32(inp["l0_s5_w_glu"])
    shared["pool_w"] = f32(inp["l1_pool_w"])
    shared["wsT"] = np.ascontiguousarray(np.transpose(np.asarray(inp["l1_gmlp_ws"], np.float32), (2, 0, 1)))
    shared["lnv"] = np.ascontiguousarray(np.stack([np.asarray(inp["l1_gmlp_ln_g"], np.float32), np.asarray(inp["l1_gmlp_ln_b"], np.float32),
                                                   np.asarray(inp["l1_gmlp_bs"], np.float32).reshape(512)], axis=0))
    shared["bmod"] = np.concatenate([_fm(inp["l%d_b_mod" % l]) for l in range(2)], axis=1)
    gl = []
    for l in range(2):
        for k in ("g_mix_pre", "g_mix_post", "g_ff_pre", "g_ff_post"):
            gl.append(_fm(inp["l%d_%s" % (l, k)]))
    shared["gains"] = np.concatenate(gl, axis=1)
    s5C, s5B = _s5_layouts(inp)
    shared["s5C"], shared["s5B"] = s5C, s5B
    cm = np.zeros((128, 64), np.float32)
    pidx = np.arange(128)
    jj = (pidx // 16) % 2
    jj_c = pidx // 64
    cm[:, 0] = (jj == 0); cm[:, 1] = (jj == 1)
    for q in range(4):
        cm[:, 2 + q] = (pidx // 32 == q)
    cm[:, 6:12] = _fm(inp["l0_s5_d"])
    cm[:, 12:18] = _fm(inp["l0_s5_b_glu"])
    cm[:, 18:20] = _fm(np.asarray(inp["l0_fnet_b"]).reshape(-1))
    cm[:, 20:24] = _fm(inp["l1_pool_scale"])
    cm[:, 24] = (jj_c == 0); cm[:, 25] = (jj_c == 1)
    cm[:, 26] = -1.0 * (jj_c == 0); cm[:, 27] = -1.0 * (jj_c == 1)
    shared["cmisc"] = cm
    shared["ident"] = np.eye(128, dtype=np.float32)
    dm = np.ones((128, 2, 2, 128), np.float32)
    cidx = np.arange(128)
    dm[:, 0, 0, cidx % 32 == 0] = 0.0
    dm[:, 0, 1, cidx % 32 == 31] = 0.0
    dm[:, 1, 0, 0] = 0.0
    dm[:, 1, 1, 127] = 0.0
    shared["dmask"] = dm
    fc = np.zeros((128, 4, 128), np.float32)
    cc = np.arange(64)
    ang = 2 * np.pi * np.outer(cc, cc) / 64.0
    C64 = (np.cos(ang) / 8.0).astype(np.float32); S64 = (np.sin(ang) / 8.0).astype(np.float32)
    fw = np.asarray(inp["l0_fnet_w"], np.float32)
    for g2 in range(2):
        sl = slice(64 * g2, 64 * g2 + 64)
        fc[sl, 0, sl] = C64
        fc[sl, 1, sl] = S64
        fc[sl, 2, sl] = fw[g2]
        fc[sl, 3, sl] = fw[2 + g2]
    shared["fconst"] = fc
    tt_ = np.arange(256)
    angp = 2 * np.pi * np.outer(tt_, tt_) / 256.0
    ctp = np.stack([np.cos(angp) / 16.0, -np.sin(angp) / 16.0], axis=0).astype(np.float32)
    shared["ctp"] = np.ascontiguousarray(np.transpose(ctp.reshape(2, 2, 128, 256), (2, 0, 1, 3)))
    bands = {k: [_band(w, k) for w in _POOL_WINDOWS] for k in ("prev", "next", "mid", "first", "last")}
    posT = _pos_embed_T()
    base = np.arange(4096, dtype=np.float64) * (2 * np.pi / 4096.0)
    ctab = (np.cos(base) / 64.0).astype(np.float32)
    stab = (-np.sin(base) / 64.0).astype(np.float32)
    ctab_bf = ctab.astype(ml_dtypes.bfloat16)
    stab_bf = stab.astype(ml_dtypes.bfloat16)
    xp = np.asarray(inp["x_prompt"], np.float32)
    xs = np.asarray(inp["x_sample"], np.float32)
    cctx = np.asarray(inp["c_ctx"], np.float32)
    cs_ = np.asarray(inp["c"], np.float32)
    sre = np.asarray(inp["state_l0_s5_re"], np.float32)
    sim = np.asarray(inp["state_l0_s5_im"], np.float32)
    maps = []
    for core in range(NCORES):
        s, j = core // 4, core % 4
        m = dict(shared)
        m["xpT"] = np.ascontiguousarray(xp[4 * core:4 * core + 4].reshape(1024, 1024).T)
        order = np.concatenate([np.arange(1024 * ((j + k) % 4), 1024 * ((j + k) % 4) + 1024) for k in range(4)])
        m["xsT"] = np.ascontiguousarray(xs[s][order].T)
        m["posT"] = np.ascontiguousarray(posT[:, order])
        kord = np.concatenate([order[0:1536], order[3584:4096]])
        idx = (order[:, None].astype(np.int64) * kord[None, :].astype(np.int64)) % 4096
        cts = np.empty((128, 32, 2, 2048), ml_dtypes.bfloat16)
        cts[:, :, 0, :] = np.transpose(ctab_bf[idx].reshape(32, 128, 2048), (1, 0, 2))
        cts[:, :, 1, :] = np.transpose(stab_bf[idx].reshape(32, 128, 2048), (1, 0, 2))
        m["cts"] = cts
        cond = np.stack([cctx, cs_[s]], axis=-1)
        m["cond"] = np.ascontiguousarray(np.transpose(cond.reshape(8, 128, 2), (1, 0, 2)).reshape(128, 16))
        m["h0s"] = np.ascontiguousarray(np.stack([_lv_layout(sre[s]), _lv_layout(sim[s])], axis=1))
        meta = np.zeros((128, 16), np.float32)
        for k in range(4):
            meta[:, k] = 1.0 if (j + k) % 4 == 0 else 0.0
            meta[:, 4 + k] = 1.0 if (j + k) % 4 == 3 else 0.0
        m["meta"] = meta
        pm = np.zeros((128, 10, 4, 128), np.float32)
        for gi in range(4):
            pm[:, 0, gi] = bands["prev"][gi]
            pm[:, 1, gi] = bands["next"][gi]
            pm[:, 2, gi] = bands["mid"][gi]
            pm[:, 3, gi] = bands["first"][gi]
            pm[:, 4, gi] = bands["last"][gi]
            pm[:, 5, gi] = bands["first" if j == 0 else "mid"][gi]
            pm[:, 6, gi] = bands["last" if j == 3 else "mid"][gi]
            if j > 0:
                pm[:, 7, gi] = bands["prev"][gi]
            if j < 3:
                pm[:, 8, gi] = bands["next"][gi]
        m["pmat"] = pm
        maps.append(m)
    return maps


_NC_CACHE = {}


def _get_nc(stage=9, dbg=()):
    key = (stage, repr(dbg))
    if key not in _NC_CACHE:
        _NC_CACHE[key] = build(stage, dbg)
    return _NC_CACHE[key]


def _run(inp, stage=9, dbg=()):
    nc = _get_nc(stage, dbg)
    maps = _host_inputs(inp)
    res = run_bass_kernel_spmd(nc, maps, core_ids=list(range(NCORES)))
    return res.results


def kernel(**inp):
    res = _run(inp)
    yp = np.zeros((32, 256, 1024), np.float32)
    ys = np.zeros((2, 4096, 1024), np.float32)
    nre = np.zeros((32, 2, 48, 64), np.float32)
    nim = np.zeros((32, 2, 48, 64), np.float32)
    for core in range(NCORES):
        r = res[core]
        s, j = core // 4, core % 4
        yp[4 * core:4 * core + 4] = np.asarray(r["ypT"]).T.reshape(4, 256, 1024)
        ys[s, 1024 * j:1024 * j + 1024] = np.asarray(r["ysT"]).T
        for name, dst in (("st_re", nre), ("st_im", nim)):
            x = np.asarray(r[name]).reshape(2, 64, 6, 4, 2, 4)
            x = np.transpose(x, (5, 4, 2, 3, 0, 1))
            dst[4 * core:4 * core + 4] = x.reshape(4, 2, 48, 64)
    return (yp, ys, nre, nim)
```
